# Optimizing a Trainium2 kernel written in Bass

```python
import math
import jax, jax.numpy as jnp
from jax import lax
import numpy as np

D_MODEL = 1024
BATCH = 8
SEQ = 4096
DEPTH = 4

CTX_LEN = 256
GRID_W = 64
D_MIX = D_MODEL
D_CONV = D_MIX // 4
CONV_K = 31
D_SSM = D_MIX // 2
SSM_HEAD_DIM = 64
SSM_HEADS = D_SSM // SSM_HEAD_DIM
SSM_GROUPS = 2
SSM_STATE = 128
SSM_CONV_K = 4
SSM_CHUNK = 64
D_XBC = D_SSM + 2 * SSM_GROUPS * SSM_STATE
D_POOL = D_MIX - D_CONV - D_SSM
POOL_WINDOWS = (2, 4, 8, 16)
POOL_GROUP = D_POOL // len(POOL_WINDOWS)
D_FF = 2816
FFN_K = 3
IN_SPLITS = (D_CONV, 2 * D_CONV, 2 * D_CONV + D_SSM, 2 * D_CONV + D_SSM + D_XBC, 2 * D_CONV + D_SSM + D_XBC + 2 * SSM_HEADS)
IN_COLS = IN_SPLITS[-1] + D_POOL
N_MOD = 6
DEEPNORM_ALPHA = (2.0 * DEPTH) ** 0.25
DEEPNORM_BETA = (8.0 * DEPTH) ** -0.25
LN_EPS = 1e-5
RMS_EPS = 1e-5
DT_MIN = 1e-3
DT_MAX = 1e-1

kernel_name = 'hybrid_conv_ssd_pool_dit_block'

F32 = jnp.float32


def _layer_norm(x, gain, bias):
    xf = x.astype(F32)
    mu = jnp.mean(xf, axis=-1, keepdims=True)
    var = jnp.mean(jnp.square(xf - mu), axis=-1, keepdims=True)
    y = (xf - mu) * lax.rsqrt(var + LN_EPS)
    return (y * gain.astype(F32) + bias.astype(F32)).astype(x.dtype)


def _gated_rms_norm(y, z, gain):
    v = y.astype(F32) * jax.nn.silu(z.astype(F32))
    v = v * lax.rsqrt(jnp.mean(v * v, axis=-1, keepdims=True) + RMS_EPS)
    return (v * gain.astype(F32)).astype(z.dtype)


def _modulate(x, shift, scale):
    return x * (1.0 + scale) + shift


def _dwconv1d(x, w, b, pad):
    y = lax.conv_general_dilated(x, w[:, None, :].astype(x.dtype), window_strides=(1,), padding=[pad],
                                 dimension_numbers=('NWC', 'WIO', 'NWC'), feature_group_count=x.shape[-1])
    return y + b.astype(x.dtype)


def _dwconv2d(x, w, b):
    p = FFN_K // 2
    y = lax.conv_general_dilated(x, w[:, :, None, :].astype(x.dtype), window_strides=(1, 1), padding=[(p, p), (p, p)],
                                 dimension_numbers=('NHWC', 'HWIO', 'NHWC'), feature_group_count=x.shape[-1])
    return y + b.astype(x.dtype)


def _segsum(a):
    T = a.shape[-1]
    a_rep = jnp.broadcast_to(a[..., :, None], a.shape + (T,))
    a_rep = jnp.where(jnp.tril(jnp.ones((T, T), bool), -1), a_rep, 0.0)
    s = jnp.cumsum(a_rep, axis=-2)
    return jnp.where(jnp.tril(jnp.ones((T, T), bool)), s, -jnp.inf)


def _ssd_scan(xh, dt, A, Bm, Cm, h0):
    b, L, H, P = xh.shape
    N = Bm.shape[-1]
    nc, Q = L // SSM_CHUNK, SSM_CHUNK
    xdt = (xh.astype(F32) * dt[..., None]).reshape(b, nc, Q, H, P)
    Bc = Bm.astype(F32).reshape(b, nc, Q, H, N)
    Cc = Cm.astype(F32).reshape(b, nc, Q, H, N)
    a = (dt * A).reshape(b, nc, Q, H).transpose(0, 3, 1, 2)
    a_cum = jnp.cumsum(a, axis=-1)
    scores = jnp.einsum('bclhn,bcshn->bhcls', Cc, Bc) * jnp.exp(_segsum(a))
    y_diag = jnp.einsum('bhcls,bcshp->bclhp', scores, xdt)
    decay_states = jnp.exp(a_cum[..., -1:] - a_cum).transpose(0, 2, 3, 1)
    states = jnp.einsum('bclhn,bclhp->bchpn', Bc * decay_states[..., None], xdt)
    states = jnp.concatenate([h0[:, None], states], axis=1)
    chunk_a = jnp.pad(a_cum[..., -1], ((0, 0), (0, 0), (1, 0)))
    new_states = jnp.einsum('bhzc,bchpn->bzhpn', jnp.exp(_segsum(chunk_a)), states)
    states, final = new_states[:, :-1], new_states[:, -1]
    state_decay = jnp.exp(a_cum).transpose(0, 2, 3, 1)
    y_off = jnp.einsum('bclhn,bchpn->bclhp', Cc, states) * state_decay[..., None]
    return (y_diag + y_off).reshape(b, L, H, P), final


def _ssd_direction(xbc, dt_raw, conv_w, conv_b, dt_bias, A_log, D_skip, h0, reverse):
    if reverse:
        xbc, dt_raw = xbc[:, ::-1], dt_raw[:, ::-1]
    b, L, _ = xbc.shape
    u = jax.nn.silu(_dwconv1d(xbc, conv_w, conv_b, (SSM_CONV_K - 1, 0)))
    xs, Bm, Cm = jnp.split(u, (D_SSM, D_SSM + SSM_GROUPS * SSM_STATE), axis=-1)
    xh = xs.reshape(b, L, SSM_HEADS, SSM_HEAD_DIM)
    rep = SSM_HEADS // SSM_GROUPS
    Bm = jnp.repeat(Bm.reshape(b, L, SSM_GROUPS, SSM_STATE), rep, axis=2)
    Cm = jnp.repeat(Cm.reshape(b, L, SSM_GROUPS, SSM_STATE), rep, axis=2)
    dt = jax.nn.softplus(dt_raw.astype(F32) + dt_bias.astype(F32))
    A = -jnp.exp(A_log.astype(F32))
    y, h_final = _ssd_scan(xh, dt, A, Bm, Cm, h0)
    y = (y + D_skip.astype(F32)[:, None] * xh.astype(F32)).reshape(b, L, D_SSM)
    if reverse:
        y = y[:, ::-1]
    return y, h_final


def _ssd_branch(proj, lp, h0_f, h0_b):
    _, _, z, xbc, dt_raw, _ = jnp.split(proj, IN_SPLITS + (IN_SPLITS[-1],), axis=-1)[:5] + [None]
    y_f, st_f = _ssd_direction(xbc, dt_raw[..., :SSM_HEADS], lp['ssm_conv_w'][0], lp['ssm_conv_b'][0],
                               lp['ssm_dt_bias'][0], lp['ssm_A_log'][0], lp['ssm_D'][0], h0_f, False)
    y_b, st_b = _ssd_direction(xbc, dt_raw[..., SSM_HEADS:], lp['ssm_conv_w'][1], lp['ssm_conv_b'][1],
                               lp['ssm_dt_bias'][1], lp['ssm_A_log'][1], lp['ssm_D'][1], h0_b, True)
    return _gated_rms_norm(y_f + y_b, z, lp['ssm_norm_g']), st_f, st_b


def _conformer_conv(a_val, a_gate, lp):
    u = a_val * jax.nn.sigmoid(a_gate)
    u = _dwconv1d(u, lp['conv_dw_w'], lp['conv_dw_b'], (CONV_K // 2, CONV_K // 2))
    u = jax.nn.silu(_layer_norm(u, lp['conv_ln_g'], lp['conv_ln_b']))
    return u @ lp['conv_pw_w'] + lp['conv_pw_b']


def _pool_mixer(u, lp):
    b, L, _ = u.shape
    uf = u.astype(F32)
    csum = jnp.concatenate([jnp.zeros((b, 1, D_POOL), F32), jnp.cumsum(uf, axis=1)], axis=1)
    t = jnp.arange(L)
    means = []
    for g, w in enumerate(POOL_WINDOWS):
        lo = jnp.clip(t - w // 2, 0, L)
        hi = jnp.clip(t + (w - w // 2), 0, L)
        cs = csum[:, :, g * POOL_GROUP:(g + 1) * POOL_GROUP]
        means.append((cs[:, hi] - cs[:, lo]) / (hi - lo).astype(F32)[None, :, None])
    r = (jnp.concatenate(means, axis=-1) - uf).astype(u.dtype)
    r = jnp.einsum('blgc,gcd->blgd', r.reshape(b, L, len(POOL_WINDOWS), POOL_GROUP), lp['pool_w'])
    return r.reshape(b, L, D_POOL) * lp['pool_scale']


def _mixer(h, lp, h0_f, h0_b):
    proj = h @ lp['w_in']
    a_val, a_gate, _, _, _, u_pool = jnp.split(proj, IN_SPLITS, axis=-1)
    ya = _conformer_conv(a_val, a_gate, lp)
    yb, st_f, st_b = _ssd_branch(proj, lp, h0_f, h0_b)
    yc = _pool_mixer(u_pool, lp)
    return jnp.concatenate([ya, yb, yc], axis=-1) @ lp['w_out'], st_f, st_b


def _conv_ffn(h, lp, rows, width):
    b, L, _ = h.shape
    val, gate = jnp.split(h @ lp['w_up'], 2, axis=-1)
    gate = _dwconv2d(gate.reshape(b, rows, width, D_FF), lp['ffn_dw_w'], lp['ffn_dw_b']).reshape(b, L, D_FF)
    return (val * jax.nn.gelu(gate)) @ lp['w_down']


def setup_inputs(seed: int = 0) -> dict:
    key = jax.random.key(seed)
    ks = iter(jax.random.split(key, 40))

    def nrm(shape, scale):
        return scale * jax.random.normal(next(ks), shape, jnp.float32)

    def near_one(shape):
        return 1.0 + nrm(shape, 0.05)

    L = DEPTH
    dt0 = jnp.exp(jax.random.uniform(next(ks), (L, 2, SSM_HEADS), jnp.float32, math.log(DT_MIN), math.log(DT_MAX)))
    return {
        'x': nrm((BATCH, SEQ, D_MODEL), 1.0),
        'c': nrm((BATCH, D_MODEL), 1.0),
        'ctx': nrm((BATCH, CTX_LEN, D_MODEL), 1.0),
        'c_ctx': nrm((D_MODEL,), 1.0),
        'w_mod': nrm((L, D_MODEL, N_MOD * D_MODEL), 0.5 * D_MODEL ** -0.5),
        'b_mod': nrm((L, N_MOD * D_MODEL), 0.02),
        'w_in': nrm((L, D_MODEL, IN_COLS), D_MODEL ** -0.5),
        'conv_dw_w': nrm((L, CONV_K, D_CONV), CONV_K ** -0.5),
        'conv_dw_b': nrm((L, D_CONV), 0.02),
        'conv_ln_g': near_one((L, D_CONV)),
        'conv_ln_b': nrm((L, D_CONV), 0.02),
        'conv_pw_w': nrm((L, D_CONV, D_CONV), D_CONV ** -0.5),
        'conv_pw_b': nrm((L, D_CONV), 0.02),
        'ssm_conv_w': nrm((L, 2, SSM_CONV_K, D_XBC), SSM_CONV_K ** -0.5),
        'ssm_conv_b': nrm((L, 2, D_XBC), 0.02),
        'ssm_dt_bias': dt0 + jnp.log(-jnp.expm1(-dt0)),
        'ssm_A_log': jnp.log(jax.random.uniform(next(ks), (L, 2, SSM_HEADS), jnp.float32, 1.0, 16.0)),
        'ssm_D': near_one((L, 2, SSM_HEADS)),
        'ssm_norm_g': near_one((L, D_SSM)),
        'pool_w': nrm((L, len(POOL_WINDOWS), POOL_GROUP, POOL_GROUP), POOL_GROUP ** -0.5),
        'pool_scale': near_one((L, D_POOL)),
        'w_out': nrm((L, D_MIX, D_MODEL), DEEPNORM_BETA * D_MIX ** -0.5),
        'ln1_g': near_one((L, D_MODEL)),
        'ln1_b': nrm((L, D_MODEL), 0.02),
        'w_up': nrm((L, D_MODEL, 2 * D_FF), D_MODEL ** -0.5),
        'ffn_dw_w': nrm((L, FFN_K, FFN_K, D_FF), 1.0 / FFN_K),
        'ffn_dw_b': nrm((L, D_FF), 0.02),
        'w_down': nrm((L, D_FF, D_MODEL), DEEPNORM_BETA * D_FF ** -0.5),
        'ln2_g': near_one((L, D_MODEL)),
        'ln2_b': nrm((L, D_MODEL), 0.02),
    }


def reference(x, c, ctx, c_ctx, w_mod, b_mod, w_in, conv_dw_w, conv_dw_b, conv_ln_g, conv_ln_b, conv_pw_w,
              conv_pw_b, ssm_conv_w, ssm_conv_b, ssm_dt_bias, ssm_A_log, ssm_D, ssm_norm_g, pool_w, pool_scale,
              w_out, ln1_g, ln1_b, w_up, ffn_dw_w, ffn_dw_b, w_down, ln2_g, ln2_b):
    rows = x.shape[1] // GRID_W
    ctx_len = ctx.shape[1]
    s_lat = jax.nn.silu(c)
    s_ctx = jax.nn.silu(c_ctx)
    for l in range(DEPTH):
        last = l == DEPTH - 1
        lp = {'w_in': w_in[l], 'conv_dw_w': conv_dw_w[l], 'conv_dw_b': conv_dw_b[l], 'conv_ln_g': conv_ln_g[l],
              'conv_ln_b': conv_ln_b[l], 'conv_pw_w': conv_pw_w[l], 'conv_pw_b': conv_pw_b[l],
              'ssm_conv_w': ssm_conv_w[l], 'ssm_conv_b': ssm_conv_b[l], 'ssm_dt_bias': ssm_dt_bias[l],
              'ssm_A_log': ssm_A_log[l], 'ssm_D': ssm_D[l], 'ssm_norm_g': ssm_norm_g[l], 'pool_w': pool_w[l],
              'pool_scale': pool_scale[l], 'w_out': w_out[l], 'w_up': w_up[l], 'ffn_dw_w': ffn_dw_w[l],
              'ffn_dw_b': ffn_dw_b[l], 'w_down': w_down[l]}
        shx1, scx1, gx1, shx2, scx2, gx2 = jnp.split((s_lat @ w_mod[l] + b_mod[l])[:, None, :], N_MOD, axis=-1)
        shc1, scc1, gc1, shc2, scc2, gc2 = jnp.split(s_ctx @ w_mod[l] + b_mod[l], N_MOD, axis=-1)
        h0 = jnp.zeros((ctx.shape[0], SSM_HEADS, SSM_HEAD_DIM, SSM_STATE), F32)
        h_c = _modulate(ctx, shc1, scc1)
        if last:
            _, st_f, st_b = _ssd_branch(h_c @ lp['w_in'], lp, h0, h0)
        else:
            mix_c, st_f, st_b = _mixer(h_c, lp, h0, h0)
        mix_x, _, _ = _mixer(_modulate(x, shx1, scx1), lp, st_f, st_b)
        x = _layer_norm(DEEPNORM_ALPHA * x + gx1 * mix_x, ln1_g[l], ln1_b[l])
        f_x = _conv_ffn(_modulate(x, shx2, scx2), lp, rows, GRID_W)
        x = _layer_norm(DEEPNORM_ALPHA * x + gx2 * f_x, ln2_g[l], ln2_b[l])
        if not last:
            ctx = _layer_norm(DEEPNORM_ALPHA * ctx + gc1 * mix_c, ln1_g[l], ln1_b[l])
            f_c = _conv_ffn(_modulate(ctx, shc2, scc2), lp, 1, ctx_len)
            ctx = _layer_norm(DEEPNORM_ALPHA * ctx + gc2 * f_c, ln2_g[l], ln2_b[l])
    return x
```

```python
import numpy as np
import concourse.bass as bass
import concourse.mybir as mybir
from concourse.bass_utils import run_bass_kernel_spmd
from contextlib import ExitStack

F32 = mybir.dt.float32
BF16 = mybir.dt.bfloat16
AF = mybir.ActivationFunctionType
ALU = mybir.AluOpType

D = 1024
TC = 256
TL = 4096
T = TC + TL
NT = T // 128
DEPTH = 4
DFF = 2816
NJ = 22
INC = 2320
ALPHA = float((2.0 * DEPTH) ** 0.25)
LN_EPS = 1e-5
RMS_EPS = 1e-5
NEG = -30000.0
POOL_W = (2, 4, 8, 16)

C_ID, C_UF, C_UB, C_MF, C_MB, C_PT = 0, 128, 256, 384, 512, 640
C_IW = C_PT + 2 * 16 * 128
C_EL = C_IW + 2
C_ER = C_EL + 16
NCONST = C_ER + 16

R_FW, R_FB, R_CW, R_CB, R_CG, R_CLB, R_PWB, R_PS, R_SW, R_SB, R_NG, R_SD = 0, 9, 10, 41, 42, 43, 44, 45, 46, 54, 56, 57
NROW = 64


def make_consts():
    c = np.zeros((128, NCONST), np.float32)
    k = np.arange(128)
    c[:, C_ID:C_ID + 128] = np.eye(128, dtype=np.float32)
    U = (k[:, None] <= k[None, :]).astype(np.float32)
    c[:, C_UF:C_UF + 128] = U
    c[:, C_UB:C_UB + 128] = U.T
    c[:, C_MF:C_MF + 128] = np.where(k[None, :] < k[:, None], NEG, 0.0)
    c[:, C_MB:C_MB + 128] = np.where(k[None, :] > k[:, None], NEG, 0.0)
    pt = np.zeros((128, 2, 16, 128), np.float32)
    iw = np.zeros((128, 2), np.float32)
    el = np.zeros((128, 2, 8), np.float32)
    er = np.zeros((128, 2, 8), np.float32)
    for ch in range(2):
        for p in range(128):
            w = POOL_W[2 * ch + p // 64]
            iw[p, ch] = 1.0 / w
            for d in range(-8, 8):
                if -(w // 2) <= d < w - w // 2:
                    pt[p, ch, d + 8, p] = 1.0
            for t in range(8):
                el[p, ch, t] = 1.0 / (min(t + (w - w // 2), 100000) - max(t - w // 2, 0))
                i = t
                er[p, ch, i] = 1.0 / (min(w - w // 2, 8 - i) + w // 2)
    c[:, C_PT:C_IW] = pt.reshape(128, -1)
    c[:, C_IW:C_EL] = iw
    c[:, C_EL:C_ER] = el.reshape(128, -1)
    c[:, C_ER:NCONST] = er.reshape(128, -1)
    return c


class Buf:
    __slots__ = ("name", "w", "r")

    def __init__(self, name=""):
        self.name = name
        self.w = None
        self.r = {}


class Sched:
    def __init__(self, nc, es):
        self.nc = nc
        self.eng = {"pe": nc.tensor, "act": nc.scalar, "dve": nc.vector, "pool": nc.gpsimd, "sp": nc.sync}
        self.sem = {}
        self.cnt = {}
        for k in self.eng:
            self.sem[k] = es.enter_context(nc.semaphore("s_" + k))
            self.cnt[k] = 0
        self.known = {k: {} for k in self.eng}
        self.dsem = {"sp": [], "pool": [], "act": []}
        self.drr = {"sp": 0, "pool": 0, "act": 0}
        for q, n in (("sp", 16), ("pool", 8)):
            for i in range(n):
                key = "d_%s%d" % (q, i)
                self.sem[key] = es.enter_context(nc.semaphore("s_" + key))
                self.cnt[key] = 0
                self.dsem[q].append(key)

    def _wait(self, e, key, val):
        if self.known[e].get(key, 0) >= val:
            return
        self.eng[e].wait_ge(self.sem[key], val)
        self.known[e][key] = val

    def _deps(self, e, reads, writes, skip_self=False):
        deps = {}
        for b in reads:
            if b.w is not None and deps.get(b.w[0], 0) < b.w[1]:
                deps[b.w[0]] = b.w[1]
        for b in writes:
            if b.w is not None and deps.get(b.w[0], 0) < b.w[1]:
                deps[b.w[0]] = b.w[1]
            for k, v in b.r.items():
                if deps.get(k, 0) < v:
                    deps[k] = v
        for k, v in deps.items():
            if skip_self and k == e:
                continue
            self._wait(e, k, v)

    def _commit(self, key, val, reads, writes):
        for b in writes:
            b.w = (key, val)
            b.r = {}
        for b in reads:
            if b.r.get(key, 0) < val:
                b.r[key] = val

    def op(self, e, fn, reads=(), writes=()):
        self._deps(e, reads, writes, skip_self=(e == "pe"))
        ins = fn()
        self.cnt[e] += 1
        ins.then_inc(self.sem[e], 1)
        self._commit(e, self.cnt[e], reads, writes)

    def group(self, e, fns, reads=(), writes=()):
        self._deps(e, reads, writes, skip_self=(e == "pe"))
        ins = None
        for fn in fns:
            ins = fn()
        self.cnt[e] += 1
        ins.then_inc(self.sem[e], 1)
        self._commit(e, self.cnt[e], reads, writes)

    def dma(self, q, out, in_, reads=(), writes=(), **kw):
        lst = self.dsem[q]
        key = lst[self.drr[q]]
        self.drr[q] = (self.drr[q] + 1) % len(lst)
        if self.cnt[key] > 0:
            self._wait(q, key, self.cnt[key])
        self._deps(q, reads, writes)
        ins = self.eng[q].dma_start(out=out, in_=in_, **kw)
        self.cnt[key] += 16
        ins.then_inc(self.sem[key], 16)
        self._commit(key, self.cnt[key], reads, writes)

    def barrier(self):
        for e in self.eng:
            for k, v in self.cnt.items():
                if k != e and v > 0:
                    self._wait(e, k, v)


def build_program(n_layers=DEPTH, stop_after=None, dbg=False):
    nc = bass.Bass("TRN2", target_bir_lowering=False)

    def din(name, shape, dt=F32):
        return nc.dram_tensor(name, list(shape), dt, kind="ExternalInput").ap()

    def dscr(name, shape, dt, kind="Internal"):
        return nc.dram_tensor(name, list(shape), dt, kind=kind).ap()

    xs_in = din("xs", [T, D])
    cvec = din("cvec", [2, D])
    consts_d = din("consts", [128, NCONST])
    w_mod = din("w_mod", [DEPTH, D, 6 * D])
    b_mod = din("b_mod", [DEPTH, 6 * D])
    w_in = din("w_in", [DEPTH, D, INC])
    conv_dw_w = din("conv_dw_w", [DEPTH, 31, 256])
    conv_dw_b = din("conv_dw_b", [DEPTH, 256])
    conv_ln_g = din("conv_ln_g", [DEPTH, 256])
    conv_ln_b = din("conv_ln_b", [DEPTH, 256])
    conv_pw_w = din("conv_pw_w", [DEPTH, 256, 256])
    conv_pw_b = din("conv_pw_b", [DEPTH, 256])
    ssm_conv_w = din("ssm_conv_w", [DEPTH, 2, 4, 1024])
    ssm_conv_b = din("ssm_conv_b", [DEPTH, 2, 1024])
    ssm_dt_bias = din("ssm_dt_bias", [DEPTH, 16])
    ssm_A_log = din("ssm_A_log", [DEPTH, 16])
    ssm_D = din("ssm_D", [DEPTH, 16])
    ssm_norm_g = din("ssm_norm_g", [DEPTH, 512])
    pool_w = din("pool_w", [DEPTH, 4, 64, 64])
    pool_scale = din("pool_scale", [DEPTH, 256])
    w_out = din("w_out", [DEPTH, D, D])
    ln1_g = din("ln1_g", [DEPTH, D])
    ln1_b = din("ln1_b", [DEPTH, D])
    w_up = din("w_up", [DEPTH, D, 2 * DFF])
    ffn_dw_w = din("ffn_dw_w", [DEPTH, 9, DFF])
    ffn_dw_b = din("ffn_dw_b", [DEPTH, DFF])
    w_down = din("w_down", [DEPTH, DFF, D])
    ln2_g = din("ln2_g", [DEPTH, D])
    ln2_b = din("ln2_b", [DEPTH, D])

    out_d = nc.dram_tensor("out", [TL, D], F32, kind="ExternalOutput").ap()
    kd = "ExternalOutput" if dbg else "Internal"
    xs_d = dscr("xs_d", [T, D], F32, kd)
    projT_d = dscr("projT_d", [18, 128, T], BF16, kd)
    catT_d = dscr("catT_d", [8, 128, T], BF16, kd)
    actT_d = dscr("actT_d", [NJ, 128, T], BF16, kd)
    gate_d = dscr("gate_d", [4, D], F32, kd)
    dta_d = dscr("dta_d", [128, NT * 32], F32, kd) if dbg else None

    es = ExitStack()
    S = Sched(nc, es)
    uid = [0]

    def sb(shape, dt, scope=None, name=None):
        uid[0] += 1
        t = (scope or es).enter_context(nc.sbuf_tensor("%s_%d" % (name or "t", uid[0]), list(shape), dt))
        return t, Buf(name or "t")

    def MM(out, lhsT, rhs, start, stop):
        return lambda: nc.tensor.matmul(out, lhsT=lhsT, rhs=rhs, start=start, stop=stop)

    def TR(out, in_, ident):
        return lambda: nc.tensor.transpose(out, in_, ident)

    PS = []
    for i in range(4):
        uid[0] += 1
        t = es.enter_context(nc.psum_tensor("ps2_%d" % i, [128, 1024], F32))
        PS.append((t, Buf("ps2_%d" % i)))
    PH = []
    for i in range(4):
        for h in range(2):
            PH.append((PS[i][0], h * 512, Buf("ph%d_%d" % (i, h))))

    def ph(i):
        t, o, b = PH[i]
        return (lambda a, c, t=t, o=o: t[:, o + a:o + c]), b

    ident_f, b_identf = sb([128, 128], F32, name="identf")
    ident_bf, b_identb = sb([128, 128], BF16, name="identb")
    ones_bf, b_ones = sb([128, 128], BF16, name="ones")
    onesm256, b_o256 = sb([128, 128], BF16, name="o256")
    onesm512, b_o512 = sb([128, 128], BF16, name="o512")
    U_bf, b_U = sb([128, 2, 128], BF16, name="U")
    nU_bf, b_nU = sb([128, 2, 128], BF16, name="nU")
    mask_bf, b_mask = sb([128, 2, 128], BF16, name="mask")
    ptap_bf, b_ptap = sb([128, 2, 16, 128], BF16, name="ptap")
    cst_s, b_cst = sb([128, 34], F32, name="csts")
    epsc, b_eps = sb([128, 4], F32, name="eps")
    dta, b_dta = sb([128, NT, 32], F32, name="dta")
    ahl, b_ahl = sb([128, NT, 2, 16], BF16, name="ahl")
    wst = [sb([128, 2816], F32, name="wst%d" % i) for i in range(2)]
    wsti = [0]
    modT, b_modT = sb([128, 4, 8, 2], F32, name="modT")
    colT, b_colT = sb([128, NJ, NROW], F32, name="colT")

    with ExitStack() as sc:
        cf, b_cf = sb([128, NCONST], F32, sc, "constf")
        S.dma("sp", cf[:], consts_d[:, :], writes=[b_cf])
        S.op("dve", lambda: nc.vector.tensor_copy(out=ident_f[:], in_=cf[:, C_ID:C_ID + 128]), reads=[b_cf], writes=[b_identf])
        S.op("dve", lambda: nc.vector.tensor_copy(out=ident_bf[:], in_=cf[:, C_ID:C_ID + 128]), reads=[b_cf], writes=[b_identb])
        S.op("dve", lambda: nc.vector.memset(ones_bf[:], 1.0), writes=[b_ones])
        S.op("dve", lambda: nc.vector.memset(onesm256[:], 1.0 / 256), writes=[b_o256])
        S.op("dve", lambda: nc.vector.memset(onesm512[:], 1.0 / 512), writes=[b_o512])
        S.op("dve", lambda: nc.vector.tensor_copy(out=U_bf[:].rearrange("p a b -> p (a b)"), in_=cf[:, C_UF:C_UF + 256]), reads=[b_cf], writes=[b_U])
        S.op("dve", lambda: nc.vector.tensor_scalar(out=nU_bf[:].rearrange("p a b -> p (a b)"), in0=cf[:, C_UF:C_UF + 256], scalar1=-1.0, scalar2=None, op0=ALU.mult), reads=[b_cf], writes=[b_nU])
        S.op("dve", lambda: nc.vector.tensor_copy(out=mask_bf[:].rearrange("p a b -> p (a b)"), in_=cf[:, C_MF:C_MF + 256]), reads=[b_cf], writes=[b_mask])
        S.op("dve", lambda: nc.vector.tensor_copy(out=ptap_bf[:].rearrange("p a b c -> p (a b c)"), in_=cf[:, C_PT:C_IW]), reads=[b_cf], writes=[b_ptap])
        S.op("dve", lambda: nc.vector.tensor_copy(out=cst_s[:], in_=cf[:, C_IW:NCONST]), reads=[b_cf], writes=[b_cst])
        S.op("dve", lambda: nc.vector.memset(epsc[:, 0:1], LN_EPS), writes=[b_eps])
        S.op("dve", lambda: nc.vector.memset(epsc[:, 1:2], 1.0), writes=[b_eps])
        S.op("dve", lambda: nc.vector.memset(epsc[:, 2:3], RMS_EPS), writes=[b_eps])
        S.op("dve", lambda: nc.vector.memset(epsc[:, 3:4], LN_EPS / (ALPHA * ALPHA)), writes=[b_eps])
        S.barrier()

    def load_cvt(dst_ap, dst_buf, src_ap, shape):
        st, bst = wst[wsti[0] % 2]
        wsti[0] += 1
        n = int(np.prod(shape))
        sv = st[:, 0:n]
        if len(shape) == 2:
            sv = sv.rearrange("p (a b) -> p a b", a=shape[0])
        S.dma("sp", sv, src_ap, writes=[bst])
        S.op("pool", lambda: nc.gpsimd.tensor_copy(out=dst_ap, in_=sv), reads=[bst], writes=[dst_buf])

    blocks = [(0, TC, 0)] + [(TC + 512 * i, 512, 1) for i in range(8)]
    seg_rng = [(0, TC), (TC, T)]

    for l in range(n_layers):
        last = (l == DEPTH - 1)
        xsrc = xs_in if l == 0 else xs_d

        with ExitStack() as sc:
            rows, b_rows = sb([NROW, 2816], F32, sc, "rows")
            S.op("dve", lambda: nc.vector.memset(rows[:], 0.0), writes=[b_rows])
            S.dma("sp", rows[R_FW:R_FW + 9, :], ffn_dw_w[l], writes=[b_rows])
            S.dma("sp", rows[R_FB:R_FB + 1, :], ffn_dw_b[l:l + 1, :], writes=[b_rows])
            S.dma("sp", rows[R_CW:R_CW + 31, 0:256], conv_dw_w[l], writes=[b_rows])
            S.dma("sp", rows[R_CB:R_CB + 1, 0:256], conv_dw_b[l:l + 1, :], writes=[b_rows])
            S.dma("sp", rows[R_CG:R_CG + 1, 0:256], conv_ln_g[l:l + 1, :], writes=[b_rows])
            S.dma("sp", rows[R_CLB:R_CLB + 1, 0:256], conv_ln_b[l:l + 1, :], writes=[b_rows])
            S.dma("sp", rows[R_PWB:R_PWB + 1, 0:256], conv_pw_b[l:l + 1, :], writes=[b_rows])
            S.dma("sp", rows[R_PS:R_PS + 1, 0:256], pool_scale[l:l + 1, :], writes=[b_rows])
            S.dma("sp", rows[R_SW:R_SW + 8, 0:1024], ssm_conv_w[l].rearrange("a k c -> (a k) c"), writes=[b_rows])
            S.dma("sp", rows[R_SB:R_SB + 2, 0:1024], ssm_conv_b[l], writes=[b_rows])
            S.dma("sp", rows[R_NG:R_NG + 1, 0:512], ssm_norm_g[l:l + 1, :], writes=[b_rows])
            pf, pb_ = ph(0)
            for j0 in range(0, NJ, 8):
                nj = min(8, NJ - j0)
                S.group("pe", [TR(pf(jj * 64, jj * 64 + 64), rows[:, (j0 + jj) * 128:(j0 + jj + 1) * 128], ident_f[0:NROW, 0:NROW])
                               for jj in range(nj)], reads=[b_rows, b_identf], writes=[pb_])
                S.op("dve", lambda: nc.vector.tensor_copy(out=colT[:, j0:j0 + nj, :].rearrange("p a b -> p (a b)"), in_=pf(0, nj * 64)),
                     reads=[pb_], writes=[b_colT])
            S.op("dve", lambda: nc.vector.tensor_scalar(out=colT[:, 0:8, R_SW:R_SW + 8], in0=colT[:, 0:8, R_SW:R_SW + 8], scalar1=0.5, scalar2=None, op0=ALU.mult),
                 reads=[b_colT], writes=[b_colT])

            sT, b_sT = sb([128, 2, 8], F32, sc, "sT")
            bmT, b_bmT = sb([128, 48], F32, sc, "bmT")
            brow, b_brow = sb([1, 6 * D], F32, sc, "brow")
            wm = [sb([128, 8, 512], F32, sc, "wm%d" % i) for i in range(2)]
            grow, b_grow = sb([128, 512], F32, sc, "grow")
            onesf, b_onesf = sb([1, 128], F32, sc, "onesf")
            S.op("dve", lambda: nc.vector.memset(onesf[:], 1.0), writes=[b_onesf])
            for s_ in range(2):
                S.dma("sp", sT[:, s_, :], cvec[s_].rearrange("(k p) -> p k", p=128), writes=[b_sT], allow_slow_non_contiguous=True)
            S.dma("sp", bmT[:], b_mod[l].rearrange("(c p) -> p c", p=128), writes=[b_bmT], allow_slow_non_contiguous=True)
            S.dma("sp", brow[:], b_mod[l:l + 1, :], writes=[b_brow])
            S.op("act", lambda: nc.scalar.activation(out=sT[:], in_=sT[:], func=AF.Silu), reads=[b_sT], writes=[b_sT])
            for blk in range(12):
                wmt, b_wm = wm[blk % 2]
                S.dma("sp", wmt[:], w_mod[l, :, blk * 512:(blk + 1) * 512].rearrange("(k p) n -> p k n", p=128), writes=[b_wm])
                which = blk // 2
                if which in (2, 5):
                    gi = 0 if which == 2 else 1
                    for s in range(2):
                        pf, pb_ = ph(1 + s)
                        fns = [MM(pf(0, 512), sT[:, s, k:k + 1].to_broadcast([128, 128]), wmt[:, k, :], k == 0, False) for k in range(8)]
                        fns.append(MM(pf(0, 512), onesf[0:1, :], brow[0:1, blk * 512:(blk + 1) * 512], False, True))
                        S.group("pe", fns, reads=[b_sT, b_wm, b_onesf, b_brow], writes=[pb_])
                        S.op("dve", lambda: nc.vector.tensor_copy(out=grow[:], in_=pf(0, 512)), reads=[pb_], writes=[b_grow])
                        half = blk % 2
                        S.dma("sp", gate_d[gi * 2 + s:gi * 2 + s + 1, half * 512:(half + 1) * 512], grow[0:1, :], reads=[b_grow], writes=[])
                else:
                    mi = {0: 0, 1: 1, 3: 2, 4: 3}[which]
                    pf, pb_ = ph(3)
                    fns = []
                    for oc in range(4):
                        for k in range(8):
                            fns.append(MM(pf(oc * 2, oc * 2 + 2), wmt[:, k, oc * 128:(oc + 1) * 128], sT[:, :, k], k == 0, k == 7))
                    S.group("pe", fns, reads=[b_sT, b_wm], writes=[pb_])
                    c0 = (blk % 2) * 4
                    for oc in range(4):
                        gc = blk * 4 + oc
                        S.op("dve", lambda: nc.vector.tensor_scalar(out=modT[:, mi, c0 + oc, :], in0=pf(oc * 2, oc * 2 + 2), scalar1=bmT[:, gc:gc + 1],
                                                                     scalar2=None, op0=ALU.add), reads=[pb_, b_bmT], writes=[b_modT])
                        if mi in (1, 3):
                            S.op("dve", lambda: nc.vector.tensor_scalar(out=modT[:, mi, c0 + oc, :], in0=modT[:, mi, c0 + oc, :], scalar1=1.0,
                                                                         scalar2=None, op0=ALU.add), reads=[b_modT], writes=[b_modT])
            S.barrier()
        if stop_after == "mod":
            break

        with ExitStack() as sc:
            win, b_win = sb([128, 8, INC], BF16, sc, "win")
            xt = [sb([128, 4, D], F32, sc, "xtA%d" % i) for i in range(2)]
            hT = [sb([128, 8, 512], BF16, sc, "hT%d" % i) for i in range(2)]
            stg = [sb([128, 18, 512], BF16, sc, "stgA%d" % i) for i in range(2)]
            dtb, b_dtb = sb([128, 16], F32, sc, "dtb")
            expA, b_expA = sb([128, 16], F32, sc, "expA")
            tmp16 = [sb([128, 16], F32, sc, "tmp16_%d" % i) for i in range(2)]
            S.dma("sp", dtb[:], ssm_dt_bias[l:l + 1, :].partition_broadcast(128), writes=[b_dtb])
            S.dma("sp", expA[:], ssm_A_log[l:l + 1, :].partition_broadcast(128), writes=[b_expA])
            S.op("act", lambda: nc.scalar.activation(out=expA[:], in_=expA[:], func=AF.Exp), reads=[b_expA], writes=[b_expA])
            for k in range(8):
                load_cvt(win[:, k, :], b_win, w_in[l, k * 128:(k + 1) * 128, :], [INC])
            for bi, (t0, nt, seg) in enumerate(blocks):
                ntile = nt // 128
                xtt, b_xt = xt[bi % 2]
                hTt, b_hT = hT[bi % 2]
                stt, b_st = stg[bi % 2]
                S.dma("sp", xtt[:, 0:ntile, :], xsrc[t0:t0 + nt, :].rearrange("(i p) d -> p i d", p=128), writes=[b_xt])
                for c in range(8):
                    pf, pb_ = ph(c % 2)
                    S.group("pe", [TR(pf(i * 128, (i + 1) * 128), xtt[:, i, c * 128:(c + 1) * 128], ident_f[:]) for i in range(ntile)],
                            reads=[b_xt, b_identf], writes=[pb_])
                    S.op("act", lambda: nc.scalar.activation(out=hTt[:, c, 0:nt], in_=pf(0, nt), func=AF.Identity,
                                                             scale=modT[:, 1, c, seg:seg + 1], bias=modT[:, 0, c, seg:seg + 1]),
                         reads=[pb_, b_modT], writes=[b_hT])
                for oc in range(18):
                    col0 = oc * 128 if oc < 16 else 2064 + (oc - 16) * 128
                    pf, pb_ = ph(2 + oc % 4)
                    S.group("pe", [MM(pf(0, nt), win[:, k, col0:col0 + 128], hTt[:, k, 0:nt], k == 0, k == 7) for k in range(8)],
                            reads=[b_win, b_hT], writes=[pb_])
                    if oc % 2 == 0:
                        S.op("act", lambda: nc.scalar.copy(out=stt[:, oc, 0:nt], in_=pf(0, nt)), reads=[pb_], writes=[b_st])
                    else:
                        S.op("dve", lambda: nc.vector.tensor_copy(out=stt[:, oc, 0:nt], in_=pf(0, nt)), reads=[pb_], writes=[b_st])
                S.dma("pool", projT_d[:, :, t0:t0 + nt].rearrange("c p t -> p c t"), stt[:, :, 0:nt], reads=[b_st], writes=[])
                for i in range(ntile):
                    ti = t0 // 128 + i
                    pf, pb_ = ph(6 + i % 2)
                    S.group("pe", [MM(pf(0, 16), hTt[:, k, i * 128:(i + 1) * 128], win[:, k, 2048:2064], k == 0, k == 7) for k in range(8)],
                            reads=[b_win, b_hT], writes=[pb_])
                    t16, b_t16 = tmp16[i % 2]
                    S.op("dve", lambda: nc.vector.tensor_tensor(out=t16[:], in0=pf(0, 16), in1=dtb[:], op=ALU.add), reads=[pb_, b_dtb], writes=[b_t16])
                    S.op("act", lambda: nc.scalar.activation(out=dta[:, ti, 0:16], in_=t16[:], func=AF.Exp), reads=[b_t16], writes=[b_dta])
            S.op("act", lambda: nc.scalar.activation(out=dta[:, :, 0:16], in_=dta[:, :, 0:16], func=AF.Ln, bias=epsc[:, 1:2], scale=1.0),
                 reads=[b_dta, b_eps], writes=[b_dta])
            S.op("dve", lambda: nc.vector.scalar_tensor_tensor(out=dta[:, :, 16:32], in0=dta[:, :, 0:16], scalar=-1.0,
                                                               in1=expA[:].unsqueeze(1).to_broadcast([128, NT, 16]), op0=ALU.mult, op1=ALU.mult),
                 reads=[b_dta, b_expA], writes=[b_dta])
            alo, b_alo = sb([128, NT, 16], F32, sc, "alo")
            S.op("dve", lambda: nc.vector.tensor_copy(out=ahl[:, :, 0, :], in_=dta[:, :, 16:32]), reads=[b_dta], writes=[b_ahl])
            S.op("dve", lambda: nc.vector.tensor_tensor(out=alo[:], in0=dta[:, :, 16:32], in1=ahl[:, :, 0, :], op=ALU.subtract),
                 reads=[b_dta, b_ahl], writes=[b_alo])
            S.op("dve", lambda: nc.vector.tensor_copy(out=ahl[:, :, 1, :], in_=alo[:]), reads=[b_alo], writes=[b_ahl])
            if dbg:
                S.dma("sp", dta_d[:, :], dta[:].rearrange("p a b -> p (a b)"), reads=[b_dta], writes=[])
            S.barrier()
        if stop_after == "A":
            break

        with ExitStack() as sc:
            cdiag, b_cdiag = sb([128, 2, 31, 128], BF16, sc, "cdiag")
            pw, b_pw = sb([128, 2, 256], BF16, sc, "pw")
            pbd, b_pbd = sb([128, 2, 128], BF16, sc, "pbd")
            pbdf, b_pbdf = sb([128, 2, 128], F32, sc, "pbdf")
            cin = [sb([128, 6, 542], BF16, sc, "cin%d" % i) for i in range(2)]
            sig, b_sig = sb([128, 2, 542], BF16, sc, "sig")
            ug, b_ug = sb([128, 2, 542], BF16, sc, "ug")
            uc, b_uc = sb([128, 2, 512], F32, sc, "uc")
            ucb, b_ucb = sb([128, 2, 512], BF16, sc, "ucb")
            usq, b_usq = sb([128, 2, 512], BF16, sc, "usq")
            m2, b_m2 = sb([128, 512], F32, sc, "m2")
            rstd, b_rstd = sb([128, 512], F32, sc, "rstd")
            tn, b_tn = sb([128, 2, 512], F32, sc, "tn")
            un, b_un = sb([128, 2, 512], BF16, sc, "un")
            rp, b_rp = sb([128, 2, 512], BF16, sc, "rp")
            ed, b_ed = sb([128, 8], F32, sc, "ed")
            cst = [sb([128, 4, 512], BF16, sc, "cst%d" % i) for i in range(2)]
            for c in range(2):
                for k in range(31):
                    S.op("dve", lambda: nc.vector.tensor_scalar(out=cdiag[:, c, k, :], in0=ident_f[:], scalar1=colT[:, c, R_CW + k:R_CW + k + 1],
                                                                 scalar2=None, op0=ALU.mult), reads=[b_identf, b_colT], writes=[b_cdiag])
            for k in range(2):
                load_cvt(pw[:, k, :], b_pw, conv_pw_w[l, k * 128:(k + 1) * 128, :], [256])
            S.op("dve", lambda: nc.vector.memset(pbdf[:], 0.0), writes=[b_pbdf])
            for g in range(4):
                S.dma("sp", pbdf[(g % 2) * 64:(g % 2) * 64 + 64, g // 2, (g % 2) * 64:(g % 2) * 64 + 64], pool_w[l, g], writes=[b_pbdf])
            S.op("dve", lambda: nc.vector.tensor_copy(out=pbd[:], in_=pbdf[:]), reads=[b_pbdf], writes=[b_pbd])
            for bi, (t0, nt, seg) in enumerate(blocks):
                s0, s1 = seg_rng[seg]
                cint, b_cin = cin[bi % 2]
                cstt, b_cst2 = cst[bi % 2]
                lo = max(t0 - 15, s0)
                hi = min(t0 + nt + 15, s1)
                o0 = lo - (t0 - 15)
                if lo > t0 - 15:
                    S.op("dve", lambda: nc.vector.memset(cint[:, :, 0:15], 0.0), writes=[b_cin])
                if hi < t0 + nt + 15:
                    S.op("dve", lambda: nc.vector.memset(cint[:, :, nt + 15:nt + 30], 0.0), writes=[b_cin])
                S.dma("sp", cint[:, 0:4, o0:o0 + hi - lo], projT_d[0:4, :, lo:hi].rearrange("c p t -> p c t"), writes=[b_cin])
                S.dma("sp", cint[:, 4:6, o0:o0 + hi - lo], projT_d[16:18, :, lo:hi].rearrange("c p t -> p c t"), writes=[b_cin])
                W = nt + 30
                S.op("act", lambda: nc.scalar.activation(out=sig[:, :, 0:W], in_=cint[:, 2:4, 0:W], func=AF.Sigmoid), reads=[b_cin], writes=[b_sig])
                S.op("dve", lambda: nc.vector.tensor_tensor(out=ug[:, :, 0:W], in0=cint[:, 0:2, 0:W], in1=sig[:, :, 0:W], op=ALU.mult),
                     reads=[b_cin, b_sig], writes=[b_ug])
                for c in range(2):
                    pf, pb_ = ph(c)
                    S.group("pe", [MM(pf(0, nt), cdiag[:, c, k, :], ug[:, c, k:k + nt], k == 0, k == 30) for k in range(31)],
                            reads=[b_cdiag, b_ug], writes=[pb_])
                    S.op("act", lambda: nc.scalar.activation(out=uc[:, c, 0:nt], in_=pf(0, nt), func=AF.Identity, bias=colT[:, c, R_CB:R_CB + 1], scale=1.0),
                         reads=[pb_, b_colT], writes=[b_uc])
                S.op("dve", lambda: nc.vector.tensor_copy(out=ucb[:, :, 0:nt], in_=uc[:, :, 0:nt]), reads=[b_uc], writes=[b_ucb])
                S.op("act", lambda: nc.scalar.activation(out=usq[:, :, 0:nt], in_=uc[:, :, 0:nt], func=AF.Square), reads=[b_uc], writes=[b_usq])
                pm, pbm = ph(2)
                pq, pbq = ph(3)
                S.group("pe", [MM(pm(0, nt), onesm256[:], ucb[:, c, 0:nt], c == 0, c == 1) for c in range(2)], reads=[b_o256, b_ucb], writes=[pbm])
                S.group("pe", [MM(pq(0, nt), onesm256[:], usq[:, c, 0:nt], c == 0, c == 1) for c in range(2)], reads=[b_o256, b_usq], writes=[pbq])
                S.op("act", lambda: nc.scalar.activation(out=m2[:, 0:nt], in_=pm(0, nt), func=AF.Square), reads=[pbm], writes=[b_m2])
                S.op("dve", lambda: nc.vector.tensor_tensor(out=m2[:, 0:nt], in0=pq(0, nt), in1=m2[:, 0:nt], op=ALU.subtract), reads=[pbq, b_m2], writes=[b_m2])
                S.op("act", lambda: nc.scalar.activation(out=rstd[:, 0:nt], in_=m2[:, 0:nt], func=AF.Sqrt, bias=epsc[:, 0:1], scale=1.0),
                     reads=[b_m2, b_eps], writes=[b_rstd])
                S.op("dve", lambda: nc.vector.reciprocal(out=rstd[:, 0:nt], in_=rstd[:, 0:nt]), reads=[b_rstd], writes=[b_rstd])
                for c in range(2):
                    S.op("dve", lambda: nc.vector.tensor_tensor(out=tn[:, c, 0:nt], in0=uc[:, c, 0:nt], in1=pm(0, nt), op=ALU.subtract),
                         reads=[b_uc, pbm], writes=[b_tn])
                    S.op("dve", lambda: nc.vector.tensor_tensor(out=tn[:, c, 0:nt], in0=tn[:, c, 0:nt], in1=rstd[:, 0:nt], op=ALU.mult),
                         reads=[b_tn, b_rstd], writes=[b_tn])
                    S.op("act", lambda: nc.scalar.activation(out=un[:, c, 0:nt], in_=tn[:, c, 0:nt], func=AF.Silu,
                                                             scale=colT[:, c, R_CG:R_CG + 1], bias=colT[:, c, R_CLB:R_CLB + 1]),
                         reads=[b_tn, b_colT], writes=[b_un])
                for oc in range(2):
                    pf, pb_ = ph(4 + oc)
                    S.group("pe", [MM(pf(0, nt), pw[:, c, oc * 128:(oc + 1) * 128], un[:, c, 0:nt], c == 0, c == 1) for c in range(2)],
                            reads=[b_pw, b_un], writes=[pb_])
                    S.op("act", lambda: nc.scalar.activation(out=cstt[:, oc, 0:nt], in_=pf(0, nt), func=AF.Identity,
                                                             bias=colT[:, oc, R_PWB:R_PWB + 1], scale=1.0), reads=[pb_, b_colT], writes=[b_cst2])
                for c in range(2):
                    pf, pb_ = ph(6 + c)
                    S.group("pe", [MM(pf(0, nt), ptap_bf[:, c, d, :], cint[:, 4 + c, 7 + d:7 + d + nt], d == 0, d == 15) for d in range(16)],
                            reads=[b_ptap, b_cin], writes=[pb_])
                    S.op("dve", lambda: nc.vector.scalar_tensor_tensor(out=rp[:, c, 0:nt], in0=pf(0, nt), scalar=cst_s[:, c:c + 1],
                                                                       in1=cint[:, 4 + c, 15:15 + nt], op0=ALU.mult, op1=ALU.subtract),
                         reads=[pb_, b_cst, b_cin], writes=[b_rp])
                    if t0 == s0:
                        S.op("dve", lambda: nc.vector.tensor_tensor(out=ed[:], in0=pf(0, 8), in1=cst_s[:, 2 + c * 8:2 + c * 8 + 8], op=ALU.mult),
                             reads=[pb_, b_cst], writes=[b_ed])
                        S.op("dve", lambda: nc.vector.tensor_tensor(out=rp[:, c, 0:8], in0=ed[:], in1=cint[:, 4 + c, 15:23], op=ALU.subtract),
                             reads=[b_ed, b_cin], writes=[b_rp])
                    if t0 + nt == s1:
                        S.op("dve", lambda: nc.vector.tensor_tensor(out=ed[:], in0=pf(nt - 8, nt), in1=cst_s[:, 18 + c * 8:18 + c * 8 + 8], op=ALU.mult),
                             reads=[pb_, b_cst], writes=[b_ed])
                        S.op("dve", lambda: nc.vector.tensor_tensor(out=rp[:, c, nt - 8:nt], in0=ed[:], in1=cint[:, 4 + c, 15 + nt - 8:15 + nt], op=ALU.subtract),
                             reads=[b_ed, b_cin], writes=[b_rp])
                for c in range(2):
                    pf, pb_ = ph(c)
                    S.group("pe", [MM(pf(0, nt), pbd[:, c, :], rp[:, c, 0:nt], True, True)], reads=[b_pbd, b_rp], writes=[pb_])
                    S.op("act", lambda: nc.scalar.activation(out=cstt[:, 2 + c, 0:nt], in_=pf(0, nt), func=AF.Identity, scale=colT[:, c, R_PS:R_PS + 1]),
                         reads=[pb_, b_colT], writes=[b_cst2])
                S.dma("pool", catT_d[0:2, :, t0:t0 + nt].rearrange("c p t -> p c t"), cstt[:, 0:2, 0:nt], reads=[b_cst2], writes=[])
                S.dma("pool", catT_d[6:8, :, t0:t0 + nt].rearrange("c p t -> p c t"), cstt[:, 2:4, 0:nt], reads=[b_cst2], writes=[])
            S.barrier()
        if stop_after == "B1":
            break

        with ExitStack() as scy:
            ybuf, b_y = sb([128, 4, T], F32, scy, "ybuf")
            with ExitStack() as sc:
                sdiag, b_sdiag = sb([128, 2, 8, 4, 128], BF16, sc, "sdiag")
                ddiag, b_ddiag = sb([128, 16, 128], BF16, sc, "ddiag")
                dsk, b_dsk = sb([128, 16], F32, sc, "dsk")
                cbrow, b_cbrow = sb([1, 2, 1024], BF16, sc, "cbrow")
                cbrowf, b_cbrowf = sb([1, 2, 1024], F32, sc, "cbrowf")
                S.dma("sp", dsk[:], ssm_D[l:l + 1, :].partition_broadcast(128), writes=[b_dsk])
                S.dma("sp", cbrowf[:], ssm_conv_b[l:l + 1, :, :], writes=[b_cbrowf])
                S.op("dve", lambda: nc.vector.tensor_scalar(out=cbrow[:], in0=cbrowf[:], scalar1=0.5, scalar2=None, op0=ALU.mult),
                     reads=[b_cbrowf], writes=[b_cbrow])
                for d in range(2):
                    for c in range(8):
                        for k in range(4):
                            S.op("dve", lambda: nc.vector.tensor_scalar(out=sdiag[:, d, c, k, :], in0=ident_f[:],
                                                                         scalar1=colT[:, c, R_SW + d * 4 + k:R_SW + d * 4 + k + 1], scalar2=None, op0=ALU.mult),
                                 reads=[b_identf, b_colT], writes=[b_sdiag])
                    for h in range(8):
                        S.op("dve", lambda: nc.vector.tensor_scalar(out=ddiag[:, d * 8 + h, :], in0=ident_f[:], scalar1=dsk[:, d * 8 + h:d * 8 + h + 1],
                                                                     scalar2=None, op0=ALU.mult), reads=[b_identf, b_dsk], writes=[b_ddiag])
                dbufs = []
                for d in range(2):
                    B_ = {}
                    B_["xbc"] = [sb([128, 8, 131], BF16, sc, "xbc%d" % d) for _ in range(2)]
                    B_["th"] = [sb([128, 512], F32, sc, "th%d_%d" % (d, i)) for i in range(3)]
                    B_["xs"] = sb([128, 512], BF16, sc, "xs%d" % d)
                    B_["xdt"] = sb([128, 512], BF16, sc, "xdt%d" % d)
                    B_["xdtw"] = sb([128, 512], BF16, sc, "xdtw%d" % d)
                    B_["btok"] = sb([128, 256], BF16, sc, "btok%d" % d)
                    B_["bct"] = sb([128, 4, 128], BF16, sc, "bct%d" % d)
                    B_["eX"] = sb([128, 1024], BF16, sc, "eX%d" % d)
                    B_["L"] = sb([128, 1024], BF16, sc, "L%d" % d)
                    B_["M"] = sb([128, 1024], BF16, sc, "M%d" % d)
                    B_["Cw"] = sb([128, 1024], BF16, sc, "Cw%d" % d)
                    B_["S32"] = sb([128, 512], F32, sc, "S32%d" % d)
                    B_["Sbf"] = sb([128, 512], BF16, sc, "Sbf%d" % d)
                    dbufs.append(B_)
                    S.op("dve", lambda: nc.vector.memset(B_["S32"][0][:], 0.0), writes=[B_["S32"][1]])
                    S.op("dve", lambda: nc.vector.memset(B_["Sbf"][0][:], 0.0), writes=[B_["Sbf"][1]])
                order = [list(range(NT)), [1, 0] + list(range(NT - 1, 1, -1))]
                visit = [{x: i for i, x in enumerate(order[d])} for d in range(2)]
                qs = [(step, d) for step in range(NT) for d in range(2)]

                def ctxq(q):
                    step, d = qs[q]
                    X = order[d][step]
                    seg = 0 if X < 2 else 1
                    return step, d, X, seg, X * 128, dbufs[d]

                def stage_ab(q):
                    step, d, X, seg, tt0, B_ = ctxq(q)
                    s0, s1 = seg_rng[seg]
                    xbc, b_xbc = B_["xbc"][step % 2]
                    if d == 0:
                        lo, hi, base = max(tt0 - 3, s0), tt0 + 128, tt0 - 3
                        if lo > tt0 - 3:
                            S.op("dve", lambda: nc.vector.memset(xbc[:, :, 0:3], 0.0), writes=[b_xbc])
                        offs = [0, 1, 2, 3]
                    else:
                        lo, hi, base = tt0, min(tt0 + 131, s1), tt0
                        if hi < tt0 + 131:
                            S.op("dve", lambda: nc.vector.memset(xbc[:, :, 128:131], 0.0), writes=[b_xbc])
                        offs = [3, 2, 1, 0]
                    S.dma("sp", xbc[:, :, lo - base:hi - base], projT_d[8:16, :, lo:hi].rearrange("c p t -> p c t"), writes=[b_xbc])
                    pxs, pb_xs = ph(0)
                    fns = []
                    for c in range(4):
                        for k in range(4):
                            fns.append(MM(pxs(c * 128, (c + 1) * 128), xbc[:, c, offs[k]:offs[k] + 128], sdiag[:, d, c, k, :], k == 0, False))
                        fns.append(MM(pxs(c * 128, (c + 1) * 128), ones_bf[0:1, :], cbrow[0:1, d, c * 128:(c + 1) * 128], False, True))
                    S.group("pe", fns, reads=[b_xbc, b_sdiag, b_ones, b_cbrow], writes=[pb_xs])
                    xs, b_xs = B_["xs"]
                    th0, b_th0 = B_["th"][0]
                    S.op("act", lambda: nc.scalar.activation(out=th0[:], in_=pxs(0, 512), func=AF.Tanh), reads=[pb_xs], writes=[b_th0])
                    S.op("dve", lambda: nc.vector.scalar_tensor_tensor(out=xs[:], in0=th0[:], scalar=1.0, in1=pxs(0, 512), op0=ALU.add, op1=ALU.mult),
                         reads=[b_th0, pb_xs], writes=[b_xs])
                    pbt, pb_bt = ph(1)
                    fns = []
                    for c in range(4, 6):
                        o = (c - 4) * 128
                        for k in range(4):
                            fns.append(MM(pbt(o, o + 128), xbc[:, c, offs[k]:offs[k] + 128], sdiag[:, d, c, k, :], k == 0, False))
                        fns.append(MM(pbt(o, o + 128), ones_bf[0:1, :], cbrow[0:1, d, c * 128:(c + 1) * 128], False, True))
                    S.group("pe", fns, reads=[b_xbc, b_sdiag, b_ones, b_cbrow], writes=[pb_bt])
                    btok, b_btok = B_["btok"]
                    th1, b_th1 = B_["th"][1]
                    S.op("act", lambda: nc.scalar.activation(out=th1[:, 0:256], in_=pbt(0, 256), func=AF.Tanh), reads=[pb_bt], writes=[b_th1])
                    S.op("dve", lambda: nc.vector.scalar_tensor_tensor(out=btok[:], in0=th1[:, 0:256], scalar=1.0, in1=pbt(0, 256), op0=ALU.add, op1=ALU.mult),
                         reads=[b_th1, pb_bt], writes=[b_btok])
                    pct, pb_ct = ph(2)
                    fns = []
                    for c in range(4, 8):
                        o = (c - 4) * 128
                        for k in range(4):
                            fns.append(MM(pct(o, o + 128), sdiag[:, d, c, k, :], xbc[:, c, offs[k]:offs[k] + 128], k == 0, False))
                        fns.append(MM(pct(o, o + 128), cbrow[0:1, d, c * 128:(c + 1) * 128], ones_bf[0:1, :], False, True))
                    S.group("pe", fns, reads=[b_xbc, b_sdiag, b_ones, b_cbrow], writes=[pb_ct])
                    bct, b_bct = B_["bct"]
                    th2, b_th2 = B_["th"][2]
                    S.op("act", lambda: nc.scalar.activation(out=th2[:], in_=pct(0, 512), func=AF.Tanh), reads=[pb_ct], writes=[b_th2])
                    S.op("dve", lambda: nc.vector.scalar_tensor_tensor(out=bct[:].rearrange("p a b -> p (a b)"), in0=th2[:], scalar=1.0, in1=pct(0, 512),
                                                                       op0=ALU.add, op1=ALU.mult), reads=[b_th2, pb_ct], writes=[b_bct])
                    pDX, pb_DX = PS[2]
                    fns = []
                    for h in range(8):
                        for part in range(2):
                            fns.append(MM(pDX[:, h * 128:(h + 1) * 128], ahl[:, X, part, d * 8 + h:d * 8 + h + 1].to_broadcast([128, 128]),
                                          U_bf[:, d, :], part == 0, part == 1))
                    S.group("pe", fns, reads=[b_ahl, b_U], writes=[pb_DX])
                    eX, b_eX = B_["eX"]
                    for hf in range(2):
                        S.op("act", lambda: nc.scalar.activation(out=eX[:, hf * 512:(hf + 1) * 512], in_=pDX[:, hf * 512:(hf + 1) * 512], func=AF.Exp),
                             reads=[pb_DX], writes=[b_eX])

                def stage_c(q):
                    step, d, X, seg, tt0, B_ = ctxq(q)
                    pDX, pb_DX = PS[2]
                    fns = []
                    for hf in range(2):
                        fns.append(MM(pDX[:, hf * 512:(hf + 1) * 512], ident_bf[:], mask_bf[:, d, :].unsqueeze(1).to_broadcast([128, 4, 128]), True, False))
                        for part in range(2):
                            fns.append(MM(pDX[:, hf * 512:(hf + 1) * 512], nU_bf[:, d, :],
                                          ahl[:, X, part, d * 8 + hf * 4:d * 8 + hf * 4 + 4].unsqueeze(2).to_broadcast([128, 4, 128]), False, False))
                    for h in range(8):
                        for part in range(2):
                            fns.append(MM(pDX[:, h * 128:(h + 1) * 128], ahl[:, X, part, d * 8 + h:d * 8 + h + 1].to_broadcast([128, 128]),
                                          U_bf[:, d, :], False, part == 1))
                    S.group("pe", fns, reads=[b_ahl, b_U, b_nU, b_mask, b_identb], writes=[pb_DX])
                    Lt, b_L = B_["L"]
                    for hf in range(2):
                        S.op("act", lambda: nc.scalar.activation(out=Lt[:, hf * 512:(hf + 1) * 512], in_=pDX[:, hf * 512:(hf + 1) * 512], func=AF.Exp),
                             reads=[pb_DX], writes=[b_L])
                    bct, b_bct = B_["bct"]
                    xs, b_xs = B_["xs"]
                    eX, b_eX = B_["eX"]
                    psc, pb_sc = ph(7)
                    S.group("pe", [MM(psc(g * 128, (g + 1) * 128), bct[:, g, :], bct[:, 2 + g, :], True, True) for g in range(2)],
                            reads=[b_bct], writes=[pb_sc])
                    Mt, b_M = B_["M"]
                    S.op("dve", lambda: nc.vector.tensor_tensor(
                        out=Mt[:].rearrange("p (g h l) -> p g h l", g=2, h=4),
                        in0=Lt[:].rearrange("p (g h l) -> p g h l", g=2, h=4),
                        in1=psc(0, 256).rearrange("p (g l) -> p g l", g=2).unsqueeze(2).to_broadcast([128, 2, 4, 128]), op=ALU.mult),
                        reads=[b_L, pb_sc], writes=[b_M])
                    xdt, b_xdt = B_["xdt"]
                    S.op("dve", lambda: nc.vector.tensor_tensor(
                        out=xdt[:].rearrange("p (h q) -> p h q", h=8), in0=xs[:].rearrange("p (h q) -> p h q", h=8),
                        in1=dta[:, X, d * 8:d * 8 + 8].unsqueeze(2).to_broadcast([128, 8, 64]), op=ALU.mult),
                        reads=[b_xs, b_dta], writes=[b_xdt])
                    ll = 127 if d == 0 else 0
                    xdtw, b_xdtw = B_["xdtw"]
                    S.op("dve", lambda: nc.vector.tensor_tensor(
                        out=xdtw[:].rearrange("p (h q) -> p h q", h=8), in0=xdt[:].rearrange("p (h q) -> p h q", h=8),
                        in1=Lt[:].rearrange("p (h l) -> p h l", h=8)[:, :, ll:ll + 1].to_broadcast([128, 8, 64]), op=ALU.mult),
                        reads=[b_xdt, b_L], writes=[b_xdtw])
                    Cw, b_Cw = B_["Cw"]
                    S.op("dve", lambda: nc.vector.tensor_tensor(
                        out=Cw[:].rearrange("p (g h l) -> p g h l", g=2, h=4),
                        in0=eX[:].rearrange("p (g h l) -> p g h l", g=2, h=4),
                        in1=bct[:, 2:4, :].unsqueeze(2).to_broadcast([128, 2, 4, 128]), op=ALU.mult),
                        reads=[b_eX, b_bct], writes=[b_Cw])

                def stage_d(q):
                    step, d, X, seg, tt0, B_ = ctxq(q)
                    ll = 127 if d == 0 else 0
                    xs, b_xs = B_["xs"]
                    xdt, b_xdt = B_["xdt"]
                    xdtw, b_xdtw = B_["xdtw"]
                    btok, b_btok = B_["btok"]
                    eX, b_eX = B_["eX"]
                    Mt, b_M = B_["M"]
                    Cw, b_Cw = B_["Cw"]
                    S32, b_S32 = B_["S32"]
                    Sbf, b_Sbf = B_["Sbf"]
                    py, pb_y = ph(3)
                    pyt = PH[3][0]
                    fns = []
                    for h in range(8):
                        pr, hf = h // 2, h % 2
                        o = pyt[hf * 64:(hf + 1) * 64, 512 + pr * 128:512 + (pr + 1) * 128]
                        fns.append(MM(o, xdt[:, h * 64:(h + 1) * 64], Mt[:, h * 128:(h + 1) * 128], True, False))
                        fns.append(MM(o, xs[:, h * 64:(h + 1) * 64], ddiag[:, d * 8 + h, :], False, False))
                        fns.append(MM(o, Sbf[:, h * 64:(h + 1) * 64], Cw[:, h * 128:(h + 1) * 128], False, True))
                    S.group("pe", fns, reads=[b_xdt, b_M, b_xs, b_ddiag, b_Sbf, b_Cw], writes=[pb_y])
                    first = visit[d][X] < visit[1 - d][X] or (visit[d][X] == visit[1 - d][X] and d == 0)
                    yv = ybuf[:, :, tt0:tt0 + 128]
                    if first:
                        S.op("act", lambda: nc.scalar.copy(out=yv, in_=py(0, 512).rearrange("p (a b) -> p a b", a=4)), reads=[pb_y], writes=[b_y])
                    else:
                        S.op("dve", lambda: nc.vector.tensor_tensor(out=yv, in0=yv, in1=py(0, 512).rearrange("p (a b) -> p a b", a=4), op=ALU.add),
                             reads=[pb_y, b_y], writes=[b_y])
                    pst, pb_st = ph(6)
                    S.group("pe", [MM(pst(g * 256, (g + 1) * 256), btok[:, g * 128:(g + 1) * 128], xdtw[:, g * 256:(g + 1) * 256], True, True)
                                   for g in range(2)], reads=[b_btok, b_xdtw], writes=[pb_st])
                    S.op("dve", lambda: nc.vector.tensor_tensor(
                        out=S32[:].rearrange("p (h q) -> p h q", h=8), in0=S32[:].rearrange("p (h q) -> p h q", h=8),
                        in1=eX[:].rearrange("p (h l) -> p h l", h=8)[:, :, ll:ll + 1].to_broadcast([128, 8, 64]), op=ALU.mult),
                        reads=[b_S32, b_eX], writes=[b_S32])
                    S.op("dve", lambda: nc.vector.tensor_tensor(out=S32[:], in0=S32[:], in1=pst(0, 512), op=ALU.add), reads=[b_S32, pb_st], writes=[b_S32])
                    S.op("act", lambda: nc.scalar.copy(out=Sbf[:], in_=S32[:]), reads=[b_S32], writes=[b_Sbf])

                NQ = len(qs)
                stage_ab(0)
                stage_c(0)
                for q in range(NQ):
                    if q + 1 < NQ:
                        stage_ab(q + 1)
                    stage_d(q)
                    if q + 1 < NQ:
                        stage_c(q + 1)
                S.barrier()
            if stop_after == "B2":
                break
            with ExitStack() as sc:
                zb = [sb([128, 4, 512], BF16, sc, "zb%d" % i) for i in range(2)]
                sz, b_sz = sb([128, 4, 512], F32, sc, "sz")
                vv, b_vv = sb([128, 4, 512], F32, sc, "vv")
                vsq, b_vsq = sb([128, 4, 512], BF16, sc, "vsq")
                rs, b_rs = sb([128, 512], F32, sc, "rs")
                cst = [sb([128, 4, 512], BF16, sc, "cstB%d" % i) for i in range(2)]
                for bi, (t0, nt, seg) in enumerate(blocks):
                    zt, b_z = zb[bi % 2]
                    S.dma("sp", zt[:, :, 0:nt], projT_d[4:8, :, t0:t0 + nt].rearrange("c p t -> p c t"), writes=[b_z])
                    S.op("act", lambda: nc.scalar.activation(out=sz[:, :, 0:nt], in_=zt[:, :, 0:nt], func=AF.Silu), reads=[b_z], writes=[b_sz])
                    S.op("dve", lambda: nc.vector.tensor_tensor(out=ybuf[:, :, t0:t0 + nt], in0=ybuf[:, :, t0:t0 + nt], in1=sz[:, :, 0:nt], op=ALU.mult),
                         reads=[b_y, b_sz], writes=[b_y])
                for bi, (t0, nt, seg) in enumerate(blocks):
                    cstt, b_cst2 = cst[bi % 2]
                    S.op("act", lambda: nc.scalar.activation(out=vsq[:, :, 0:nt], in_=ybuf[:, :, t0:t0 + nt], func=AF.Square), reads=[b_y], writes=[b_vsq])
                    pm, pbm = ph(bi % 2)
                    S.group("pe", [MM(pm(0, nt), onesm512[:], vsq[:, c, 0:nt], c == 0, c == 3) for c in range(4)], reads=[b_o512, b_vsq], writes=[pbm])
                    S.op("act", lambda: nc.scalar.activation(out=rs[:, 0:nt], in_=pm(0, nt), func=AF.Sqrt, bias=epsc[:, 2:3], scale=1.0),
                         reads=[pbm, b_eps], writes=[b_rs])
                    S.op("dve", lambda: nc.vector.reciprocal(out=rs[:, 0:nt], in_=rs[:, 0:nt]), reads=[b_rs], writes=[b_rs])
                    for c in range(4):
                        S.op("dve", lambda: nc.vector.scalar_tensor_tensor(out=cstt[:, c, 0:nt], in0=ybuf[:, c, t0:t0 + nt], scalar=colT[:, c, R_NG:R_NG + 1],
                                                                           in1=rs[:, 0:nt], op0=ALU.mult, op1=ALU.mult),
                             reads=[b_y, b_colT, b_rs], writes=[b_cst2])
                    S.dma("pool", catT_d[2:6, :, t0:t0 + nt].rearrange("c p t -> p c t"), cstt[:, :, 0:nt], reads=[b_cst2], writes=[])
                S.barrier()
        if stop_after == "B":
            break

        def epilogue_block(po, pbo, xtt, bxt, ntile, gate, b_gate, lng, lnb, b_ln, tmps, scbs, after_gm=None):
            for i in range(ntile):
                tmp, b_tmp = tmps[i]
                for hf in range(2):
                    S.op("dve", lambda: nc.vector.tensor_tensor(out=tmp[:, hf * 512:(hf + 1) * 512], in0=po[i][hf](0, 512),
                                                                 in1=gate[:, hf * 512:(hf + 1) * 512], op=ALU.mult),
                         reads=[pbo[i][hf], b_gate], writes=[b_tmp])
            if after_gm is not None:
                after_gm()
            for i in range(ntile):
                tmp, b_tmp = tmps[i]
                S.op("dve", lambda: nc.vector.scalar_tensor_tensor(out=tmp[:], in0=xtt[:, i, :], scalar=ALPHA, in1=tmp[:], op0=ALU.mult, op1=ALU.add),
                     reads=[bxt[i], b_tmp], writes=[b_tmp])
            for i in range(ntile):
                tmp, b_tmp = tmps[i]
                st6, b_st6, mv, b_mv, rsd, b_rsd = scbs[i]
                for hf in range(2):
                    S.op("dve", lambda: nc.vector.bn_stats(out=st6[:, hf, :], in_=tmp[:, hf * 512:(hf + 1) * 512]), reads=[b_tmp], writes=[b_st6])
            for i in range(ntile):
                st6, b_st6, mv, b_mv, rsd, b_rsd = scbs[i]
                S.op("dve", lambda: nc.vector.bn_aggr(out=mv[:], in_=st6[:]), reads=[b_st6], writes=[b_mv])
            for i in range(ntile):
                st6, b_st6, mv, b_mv, rsd, b_rsd = scbs[i]
                S.op("act", lambda: nc.scalar.activation(out=rsd[:, 0:1], in_=mv[:, 1:2], func=AF.Sqrt, bias=epsc[:, 0:1], scale=1.0),
                     reads=[b_mv, b_eps], writes=[b_rsd])
            for i in range(ntile):
                st6, b_st6, mv, b_mv, rsd, b_rsd = scbs[i]
                S.op("dve", lambda: nc.vector.reciprocal(out=rsd[:, 0:1], in_=rsd[:, 0:1]), reads=[b_rsd], writes=[b_rsd])
            for i in range(ntile):
                st6, b_st6, mv, b_mv, rsd, b_rsd = scbs[i]
                S.op("dve", lambda: nc.vector.scalar_tensor_tensor(out=rsd[:, 1:2], in0=mv[:, 0:1], scalar=-1.0, in1=rsd[:, 0:1], op0=ALU.mult, op1=ALU.mult),
                     reads=[b_mv, b_rsd], writes=[b_rsd])
            for i in range(ntile):
                tmp, b_tmp = tmps[i]
                st6, b_st6, mv, b_mv, rsd, b_rsd = scbs[i]
                S.op("act", lambda: nc.scalar.activation(out=tmp[:], in_=tmp[:], func=AF.Identity, scale=rsd[:, 0:1], bias=rsd[:, 1:2]),
                     reads=[b_tmp, b_rsd], writes=[b_tmp])
            for i in range(ntile):
                tmp, b_tmp = tmps[i]
                S.op("pool", lambda: nc.gpsimd.tensor_tensor(out=tmp[:], in0=tmp[:], in1=lng[:], op=ALU.mult), reads=[b_tmp, b_ln], writes=[b_tmp])
            for i in range(ntile):
                tmp, b_tmp = tmps[i]
                S.op("pool", lambda: nc.gpsimd.tensor_tensor(out=xtt[:, i, :], in0=tmp[:], in1=lnb[:], op=ALU.add), reads=[b_tmp, b_ln], writes=[bxt[i]])

        def mk_scbs(sc, n):
            r = []
            for i in range(n):
                a, ba = sb([128, 2, 6], F32, sc, "st6")
                b2, bb = sb([128, 2], F32, sc, "mv")
                c2, bc = sb([128, 2], F32, sc, "rsd")
                r.append((a, ba, b2, bb, c2, bc))
            return r

        with ExitStack() as sch:
            h2T, b_h2T = sb([128, 8, T + 128], BF16, sch, "h2T")
            with ExitStack() as sc:
                wout, b_wout = sb([128, 8, D], BF16, sc, "wout")
                cat = [sb([128, 8, 512], BF16, sc, "cat%d" % i) for i in range(1)]
                xt = [sb([128, 4, D], F32, sc, "xtC%d" % i)[0] for i in range(2)]
                bxts = [[Buf("xtC") for _ in range(4)] for _ in range(2)]
                tmps = [sb([128, D], F32, sc, "tmpC%d" % i) for i in range(4)]
                gate, b_gate = sb([128, D], F32, sc, "gateC")
                lng, b_ln = sb([128, D], F32, sc, "lngC")
                lnb, _ = sb([128, D], F32, sc, "lnbC")
                scbs = mk_scbs(sc, 4)
                S.op("dve", lambda: nc.vector.memset(h2T[:, :, T:T + 128], 0.0), writes=[b_h2T])
                S.dma("sp", lng[:], ln1_g[l:l + 1, :].partition_broadcast(128), writes=[b_ln])
                S.dma("sp", lnb[:], ln1_b[l:l + 1, :].partition_broadcast(128), writes=[b_ln])
                for k in range(8):
                    load_cvt(wout[:, k, :], b_wout, w_out[l, k * 128:(k + 1) * 128, :], [D])

                def transposes(pblk):
                    pbi, (pt0, pnt, pseg) = pblk
                    pxt, pbx = xt[pbi % 2], bxts[pbi % 2]
                    pn = pnt // 128
                    for c in range(8):
                        pf, pb_ = ph(c % 2)
                        S.group("pe", [TR(pf(i * 128, (i + 1) * 128), pxt[:, i, c * 128:(c + 1) * 128], ident_f[:]) for i in range(pn)],
                                reads=pbx[0:pn] + [b_identf], writes=[pb_])
                        S.op("act", lambda: nc.scalar.activation(out=h2T[:, c, pt0:pt0 + pnt], in_=pf(0, pnt), func=AF.Identity,
                                                                 scale=modT[:, 3, c, pseg:pseg + 1], bias=modT[:, 2, c, pseg:pseg + 1]),
                             reads=[pb_, b_modT], writes=[b_h2T])

                prev = None
                cur_seg = None
                for bi, (t0, nt, seg) in enumerate(blocks):
                    if last and seg == 0:
                        continue
                    ntile = nt // 128
                    if seg != cur_seg:
                        S.dma("sp", gate[:], gate_d[seg:seg + 1, :].partition_broadcast(128), writes=[b_gate])
                        cur_seg = seg
                    catt, b_cat = cat[0]
                    xtt, bxt = xt[bi % 2], bxts[bi % 2]
                    S.dma("sp", catt[:, :, 0:nt], catT_d[:, :, t0:t0 + nt].rearrange("c p t -> p c t"), writes=[b_cat])
                    S.dma("sp", xtt[:, 0:ntile, :], xsrc[t0:t0 + nt, :].rearrange("(i p) d -> p i d", p=128), writes=bxt[0:ntile])
                    po, pbo = [], []
                    for i in range(ntile):
                        pi, pbi_ = [], []
                        for hf in range(2):
                            pf, pb_ = ph(i * 2 + hf)
                            S.group("pe", [MM(pf(0, 512), catt[:, k, i * 128:(i + 1) * 128], wout[:, k, hf * 512:(hf + 1) * 512], k == 0, k == 7)
                                           for k in range(8)], reads=[b_cat, b_wout], writes=[pb_])
                            pi.append(pf)
                            pbi_.append(pb_)
                        po.append(pi)
                        pbo.append(pbi_)
                    pv_ = prev
                    epilogue_block(po, pbo, xtt, bxt, ntile, gate, b_gate, lng, lnb, b_ln, tmps, scbs,
                                   after_gm=(lambda: transposes(pv_)) if pv_ is not None else None)
                    S.dma("pool", xs_d[t0:t0 + nt, :].rearrange("(i p) d -> p i d", p=128), xtt[:, 0:ntile, :], reads=bxt[0:ntile], writes=[])
                    prev = (bi, (t0, nt, seg))
                transposes(prev)
                S.barrier()
            if stop_after == "C":
                break
            with ExitStack() as sc:
                wj = [sb([128, 8, 256], BF16, sc, "wj%d" % i) for i in range(3)]
                fd = [sb([128, 9, 128], BF16, sc, "fd%d" % i) for i in range(3)]
                gb = [[sb([128, 642], BF16, sc, "g%d_%d" % (i, q)) for q in range(3)] for i in range(2)]
                ge = [sb([128, 512], F32, sc, "ge%d" % i) for i in range(2)]
                ast = [sb([128, 512], BF16, sc, "ast%d" % i) for i in range(3)]
                for i in range(2):
                    for q in range(3):
                        S.op("dve", lambda: nc.vector.memset(gb[i][q][0][:], 0.0), writes=[gb[i][q][1]])

                def prep(j):
                    wjt, b_wj = wj[j % 3]
                    fdt, b_fd = fd[j % 3]
                    st_, bst = wst[wsti[0] % 2]
                    wsti[0] += 1
                    sv = st_[:, 0:2048].rearrange("p (k n) -> p k n", k=8)
                    S.dma("sp", sv[:, :, 0:128], w_up[l, :, j * 128:(j + 1) * 128].rearrange("(k p) n -> p k n", p=128), writes=[bst])
                    S.dma("sp", sv[:, :, 128:256], w_up[l, :, DFF + j * 128:DFF + (j + 1) * 128].rearrange("(k p) n -> p k n", p=128), writes=[bst])
                    S.op("pool", lambda: nc.gpsimd.tensor_copy(out=wjt[:], in_=sv), reads=[bst], writes=[b_wj])
                    for q in range(9):
                        S.op("pool", lambda: nc.gpsimd.tensor_scalar(out=fdt[:, q, :], in0=ident_f[:], scalar1=colT[:, j, R_FW + q:R_FW + q + 1], scalar2=None,
                                                                      op0=ALU.mult), reads=[b_identf, b_colT], writes=[b_fd])

                items = []
                for j in range(NJ):
                    first_of_j = True
                    for bi, (t0, nt, seg) in enumerate(blocks):
                        if last and seg == 0:
                            continue
                        items.append((j, t0, nt, seg, first_of_j))
                        first_of_j = False

                def stage1(n):
                    j, t0, nt, seg, first_of_j = items[n]
                    if first_of_j and j + 1 < NJ:
                        prep(j + 1)
                    wjt, b_wj = wj[j % 3]
                    (g0, b_g0), (gL, b_gL), (gR, b_gR) = gb[n % 2]
                    s0, s1 = seg_rng[seg]
                    pgt, pbg = PS[n % 2]
                    if seg == 0:
                        S.group("pe", [MM(pgt[:, 0:nt], wjt[:, k, 128:256], h2T[:, k, t0:t0 + nt], k == 0, k == 7) for k in range(8)],
                                reads=[b_wj, b_h2T], writes=[pbg])
                        S.op("act", lambda: nc.scalar.copy(out=g0[:, 1:1 + nt], in_=pgt[:, 0:nt]), reads=[pbg], writes=[b_g0])
                        S.op("dve", lambda: nc.vector.memset(g0[:, 1 + nt:2 + nt], 0.0), writes=[b_g0])
                    else:
                        base = t0 - 64
                        S.group("pe", [MM(pgt[:, 0:512], wjt[:, k, 128:256], h2T[:, k, base:base + 512], k == 0, k == 7) for k in range(8)] +
                                [MM(pgt[:, 512:640], wjt[:, k, 128:256], h2T[:, k, base + 512:base + 640], k == 0, k == 7) for k in range(8)],
                                reads=[b_wj, b_h2T], writes=[pbg])
                        S.op("act", lambda: nc.scalar.copy(out=g0[:, 1:513], in_=pgt[:, 0:512]), reads=[pbg], writes=[b_g0])
                        S.op("act", lambda: nc.scalar.copy(out=g0[:, 513:641], in_=pgt[:, 512:640]), reads=[pbg], writes=[b_g0])
                        if t0 == s0:
                            S.op("dve", lambda: nc.vector.memset(g0[:, 1:65], 0.0), writes=[b_g0])
                        if t0 + nt == s1:
                            S.op("dve", lambda: nc.vector.memset(g0[:, 577:641], 0.0), writes=[b_g0])
                        S.op("dve", lambda: nc.vector.tensor_copy(out=gL[:], in_=g0[:]), reads=[b_g0], writes=[b_gL])
                        S.op("dve", lambda: nc.vector.memset(gL[:, 64:641:64], 0.0), writes=[b_gL])
                        S.op("dve", lambda: nc.vector.tensor_copy(out=gR[:], in_=g0[:]), reads=[b_g0], writes=[b_gR])
                        S.op("dve", lambda: nc.vector.memset(gR[:, 1:641:64], 0.0), writes=[b_gR])

                def stage2(n):
                    j, t0, nt, seg, first_of_j = items[n]
                    wjt, b_wj = wj[j % 3]
                    fdt, b_fd = fd[j % 3]
                    (g0, b_g0), (gL, b_gL), (gR, b_gR) = gb[n % 2]
                    get, b_ge = ge[n % 2]
                    astt, b_ast = ast[n % 3]
                    pv, pbv = ph(6 + n % 2)
                    S.group("pe", [MM(pv(0, nt), wjt[:, k, 0:128], h2T[:, k, t0:t0 + nt], k == 0, k == 7) for k in range(8)],
                            reads=[b_wj, b_h2T], writes=[pbv])
                    pd, pbd_ = ph(4 + n % 2)
                    if seg == 0:
                        S.group("pe", [MM(pd(0, nt), fdt[:, 3 + kx, :], g0[:, kx:kx + nt], kx == 0, kx == 2) for kx in range(3)],
                                reads=[b_fd, b_g0], writes=[pbd_])
                    else:
                        fns = []
                        for ky in range(3):
                            for kx in range(3):
                                src = (gL, g0, gR)[kx]
                                off = 65 + 64 * (ky - 1) + (kx - 1)
                                fns.append(MM(pd(0, nt), fdt[:, ky * 3 + kx, :], src[:, off:off + nt], ky == 0 and kx == 0, ky == 2 and kx == 2))
                        S.group("pe", fns, reads=[b_fd, b_g0, b_gL, b_gR], writes=[pbd_])
                    S.op("act", lambda: nc.scalar.activation(out=get[:, 0:nt], in_=pd(0, nt), func=AF.Gelu_apprx_tanh,
                                                             bias=colT[:, j, R_FB:R_FB + 1], scale=1.0), reads=[pbd_, b_colT], writes=[b_ge])
                    S.op("dve", lambda: nc.vector.tensor_tensor(out=astt[:, 0:nt], in0=pv(0, nt), in1=get[:, 0:nt], op=ALU.mult),
                         reads=[pbv, b_ge], writes=[b_ast])
                    S.dma("pool", actT_d[j, :, t0:t0 + nt], astt[:, 0:nt], reads=[b_ast], writes=[])

                prep(0)
                stage1(0)
                for n in range(len(items)):
                    if n + 1 < len(items):
                        stage1(n + 1)
                    stage2(n)
                S.barrier()
        if stop_after == "D1":
            break

        with ExitStack() as sc:
            wdn, b_wdn = sb([128, NJ, D], BF16, sc, "wdn")
            actb = [sb([128, NJ, 512], BF16, sc, "actb%d" % i) for i in range(2)]
            xt = [sb([128, 4, D], F32, sc, "xtD%d" % i)[0] for i in range(2)]
            bxts = [[Buf("xtD") for _ in range(4)] for _ in range(2)]
            tmps = [sb([128, D], F32, sc, "tmpD%d" % i) for i in range(4)]
            gate, b_gate = sb([128, D], F32, sc, "gateD")
            lng, b_ln = sb([128, D], F32, sc, "lngD")
            lnb, _ = sb([128, D], F32, sc, "lnbD")
            scbs = mk_scbs(sc, 4)
            S.dma("sp", lng[:], ln2_g[l:l + 1, :].partition_broadcast(128), writes=[b_ln])
            S.dma("sp", lnb[:], ln2_b[l:l + 1, :].partition_broadcast(128), writes=[b_ln])
            for jp in range(0, NJ, 2):
                st_, bst_ = wst[(jp // 2) % 2]
                sv_ = st_[:, 0:2048].rearrange("p (a b) -> p a b", a=2)
                S.dma("sp" if (jp // 2) % 2 == 0 else "pool", sv_, w_down[l, jp * 128:(jp + 2) * 128, :].rearrange("(a p) n -> p a n", p=128), writes=[bst_])
                if (jp // 2) % 2 == 0:
                    S.op("pool", lambda: nc.gpsimd.tensor_copy(out=wdn[:, jp:jp + 2, :], in_=sv_), reads=[bst_], writes=[b_wdn])
                else:
                    S.op("dve", lambda: nc.vector.tensor_copy(out=wdn[:, jp:jp + 2, :], in_=sv_), reads=[bst_], writes=[b_wdn])
            cur_seg = None
            for bi, (t0, nt, seg) in enumerate(blocks):
                if last and seg == 0:
                    continue
                ntile = nt // 128
                if seg != cur_seg:
                    S.dma("sp", gate[:], gate_d[2 + seg:3 + seg, :].partition_broadcast(128), writes=[b_gate])
                    cur_seg = seg
                at, b_at = actb[bi % 2]
                xtt, bxt = xt[bi % 2], bxts[bi % 2]
                S.dma("sp", at[:, :, 0:nt], actT_d[:, :, t0:t0 + nt].rearrange("c p t -> p c t"), writes=[b_at])
                S.dma("sp", xtt[:, 0:ntile, :], xs_d[t0:t0 + nt, :].rearrange("(i p) d -> p i d", p=128), writes=bxt[0:ntile])
                po, pbo = [], []
                for i in range(ntile):
                    pi, pbi_ = [], []
                    for hf in range(2):
                        pf, pb_ = ph(i * 2 + hf)
                        S.group("pe", [MM(pf(0, 512), at[:, j, i * 128:(i + 1) * 128], wdn[:, j, hf * 512:(hf + 1) * 512], j == 0, j == NJ - 1)
                                       for j in range(NJ)], reads=[b_at, b_wdn], writes=[pb_])
                        pi.append(pf)
                        pbi_.append(pb_)
                    po.append(pi)
                    pbo.append(pbi_)
                epilogue_block(po, pbo, xtt, bxt, ntile, gate, b_gate, lng, lnb, b_ln, tmps, scbs)
                if last:
                    S.dma("pool", out_d[t0 - TC:t0 - TC + nt, :].rearrange("(i p) d -> p i d", p=128), xtt[:, 0:ntile, :], reads=bxt[0:ntile], writes=[])
                else:
                    S.dma("pool", xs_d[t0:t0 + nt, :].rearrange("(i p) d -> p i d", p=128), xtt[:, 0:ntile, :], reads=bxt[0:ntile], writes=[])
            S.barrier()

    S.barrier()
    es.close()
    return nc


_CACHE = {}


def kernel(**inputs):
    x = np.asarray(inputs["x"], np.float32)
    c = np.asarray(inputs["c"], np.float32)
    ctx = np.asarray(inputs["ctx"], np.float32)
    c_ctx = np.asarray(inputs["c_ctx"], np.float32)
    B = x.shape[0]
    if "nc" not in _CACHE:
        _CACHE["nc"] = build_program()
    nc = _CACHE["nc"]
    consts = make_consts()
    shared = {"consts": consts}
    for k in ("w_mod", "b_mod", "w_in", "conv_dw_w", "conv_dw_b", "conv_ln_g", "conv_ln_b", "conv_pw_w", "conv_pw_b", "ssm_conv_w",
              "ssm_conv_b", "ssm_norm_g", "pool_w", "pool_scale", "w_out", "ln1_g", "ln1_b", "w_up", "ffn_dw_b", "w_down", "ln2_g", "ln2_b"):
        shared[k] = np.ascontiguousarray(np.asarray(inputs[k], np.float32))
    shared["ffn_dw_w"] = np.ascontiguousarray(np.asarray(inputs["ffn_dw_w"], np.float32).reshape(DEPTH, 9, DFF))
    for k in ("ssm_dt_bias", "ssm_A_log", "ssm_D"):
        shared[k] = np.ascontiguousarray(np.asarray(inputs[k], np.float32).reshape(DEPTH, 16))
    in_maps = []
    for b in range(B):
        m = dict(shared)
        m["xs"] = np.ascontiguousarray(np.concatenate([ctx[b], x[b]], axis=0))
        m["cvec"] = np.ascontiguousarray(np.stack([c_ctx, c[b]], axis=0))
        in_maps.append(m)
    res = run_bass_kernel_spmd(nc, in_maps, core_ids=list(range(B)))
    return np.stack([np.asarray(r["out"], np.float32) for r in res.results], axis=0)
```

```python
import numpy as np
import concourse.bass as bass
import concourse.mybir as mybir
from concourse.bass_utils import run_bass_kernel_spmd
from contextlib import ExitStack

F32 = mybir.dt.float32
BF16 = mybir.dt.bfloat16
AF = mybir.ActivationFunctionType
ALU = mybir.AluOpType

D = 1024
TC = 256
TL = 4096
T = TC + TL
NT = T // 128
DEPTH = 4
DFF = 2816
NJ = 22
INC = 2320
ALPHA = float((2.0 * DEPTH) ** 0.25)
LN_EPS = 1e-5
RMS_EPS = 1e-5
NEG = -30000.0
POOL_W = (2, 4, 8, 16)

C_ID, C_UF, C_UB, C_MF, C_MB, C_PT = 0, 128, 256, 384, 512, 640
C_IW = C_PT + 2 * 16 * 128
C_EL = C_IW + 2
C_ER = C_EL + 16
NCONST = C_ER + 16

R_FW, R_FB, R_CW, R_CB, R_CG, R_CLB, R_PWB, R_PS, R_SW, R_SB, R_NG, R_SD = 0, 9, 10, 41, 42, 43, 44, 45, 46, 54, 56, 57
NROW = 64


def make_consts():
    c = np.zeros((128, NCONST), np.float32)
    k = np.arange(128)
    c[:, C_ID:C_ID + 128] = np.eye(128, dtype=np.float32)
    U = (k[:, None] <= k[None, :]).astype(np.float32)
    c[:, C_UF:C_UF + 128] = U
    c[:, C_UB:C_UB + 128] = U.T
    c[:, C_MF:C_MF + 128] = np.where(k[None, :] < k[:, None], NEG, 0.0)
    c[:, C_MB:C_MB + 128] = np.where(k[None, :] > k[:, None], NEG, 0.0)
    pt = np.zeros((128, 2, 16, 128), np.float32)
    iw = np.zeros((128, 2), np.float32)
    el = np.zeros((128, 2, 8), np.float32)
    er = np.zeros((128, 2, 8), np.float32)
    for ch in range(2):
        for p in range(128):
            w = POOL_W[2 * ch + p // 64]
            iw[p, ch] = 1.0 / w
            for d in range(-8, 8):
                if -(w // 2) <= d < w - w // 2:
                    pt[p, ch, d + 8, p] = 1.0
            for t in range(8):
                el[p, ch, t] = 1.0 / (min(t + (w - w // 2), 100000) - max(t - w // 2, 0))
                i = t
                er[p, ch, i] = 1.0 / (min(w - w // 2, 8 - i) + w // 2)
    c[:, C_PT:C_IW] = pt.reshape(128, -1)
    c[:, C_IW:C_EL] = iw
    c[:, C_EL:C_ER] = el.reshape(128, -1)
    c[:, C_ER:NCONST] = er.reshape(128, -1)
    return c


class Buf:
    __slots__ = ("name", "w", "r")

    def __init__(self, name=""):
        self.name = name
        self.w = None
        self.r = {}


class Sched:
    def __init__(self, nc, es):
        self.nc = nc
        self.eng = {"pe": nc.tensor, "act": nc.scalar, "dve": nc.vector, "pool": nc.gpsimd, "sp": nc.sync}
        self.sem = {}
        self.cnt = {}
        for k in self.eng:
            self.sem[k] = es.enter_context(nc.semaphore("s_" + k))
            self.cnt[k] = 0
        self.known = {k: {} for k in self.eng}
        self.dsem = {"sp": [], "pool": [], "act": []}
        self.drr = {"sp": 0, "pool": 0, "act": 0}
        for q, n in (("sp", 16), ("pool", 8)):
            for i in range(n):
                key = "d_%s%d" % (q, i)
                self.sem[key] = es.enter_context(nc.semaphore("s_" + key))
                self.cnt[key] = 0
                self.dsem[q].append(key)

    def _wait(self, e, key, val):
        if self.known[e].get(key, 0) >= val:
            return
        self.eng[e].wait_ge(self.sem[key], val)
        self.known[e][key] = val

    def _deps(self, e, reads, writes, skip_self=False):
        deps = {}
        for b in reads:
            if b.w is not None and deps.get(b.w[0], 0) < b.w[1]:
                deps[b.w[0]] = b.w[1]
        for b in writes:
            if b.w is not None and deps.get(b.w[0], 0) < b.w[1]:
                deps[b.w[0]] = b.w[1]
            for k, v in b.r.items():
                if deps.get(k, 0) < v:
                    deps[k] = v
        for k, v in deps.items():
            if skip_self and k == e:
                continue
            self._wait(e, k, v)

    def _commit(self, key, val, reads, writes):
        for b in writes:
            b.w = (key, val)
            b.r = {}
        for b in reads:
            if b.r.get(key, 0) < val:
                b.r[key] = val

    def op(self, e, fn, reads=(), writes=()):
        self._deps(e, reads, writes, skip_self=(e == "pe"))
        ins = fn()
        self.cnt[e] += 1
        ins.then_inc(self.sem[e], 1)
        self._commit(e, self.cnt[e], reads, writes)

    def group(self, e, fns, reads=(), writes=()):
        self._deps(e, reads, writes, skip_self=(e == "pe"))
        ins = None
        for fn in fns:
            ins = fn()
        self.cnt[e] += 1
        ins.then_inc(self.sem[e], 1)
        self._commit(e, self.cnt[e], reads, writes)

    def dma(self, q, out, in_, reads=(), writes=(), **kw):
        lst = self.dsem[q]
        key = lst[self.drr[q]]
        self.drr[q] = (self.drr[q] + 1) % len(lst)
        if self.cnt[key] > 0:
            self._wait(q, key, self.cnt[key])
        self._deps(q, reads, writes)
        ins = self.eng[q].dma_start(out=out, in_=in_, **kw)
        self.cnt[key] += 16
        ins.then_inc(self.sem[key], 16)
        self._commit(key, self.cnt[key], reads, writes)

    def barrier(self):
        for e in self.eng:
            for k, v in self.cnt.items():
                if k != e and v > 0:
                    self._wait(e, k, v)


def build_program(n_layers=DEPTH, stop_after=None, dbg=False):
    nc = bass.Bass("TRN2", target_bir_lowering=False)

    def din(name, shape, dt=F32):
        return nc.dram_tensor(name, list(shape), dt, kind="ExternalInput").ap()

    def dscr(name, shape, dt, kind="Internal"):
        return nc.dram_tensor(name, list(shape), dt, kind=kind).ap()

    xs_in = din("xs", [T, D])
    cvec = din("cvec", [2, D])
    consts_d = din("consts", [128, NCONST])
    w_mod = din("w_mod", [DEPTH, D, 6 * D])
    b_mod = din("b_mod", [DEPTH, 6 * D])
    w_in = din("w_in", [DEPTH, D, INC])
    conv_dw_w = din("conv_dw_w", [DEPTH, 31, 256])
    conv_dw_b = din("conv_dw_b", [DEPTH, 256])
    conv_ln_g = din("conv_ln_g", [DEPTH, 256])
    conv_ln_b = din("conv_ln_b", [DEPTH, 256])
    conv_pw_w = din("conv_pw_w", [DEPTH, 256, 256])
    conv_pw_b = din("conv_pw_b", [DEPTH, 256])
    ssm_conv_w = din("ssm_conv_w", [DEPTH, 2, 4, 1024])
    ssm_conv_b = din("ssm_conv_b", [DEPTH, 2, 1024])
    ssm_dt_bias = din("ssm_dt_bias", [DEPTH, 16])
    ssm_A_log = din("ssm_A_log", [DEPTH, 16])
    ssm_D = din("ssm_D", [DEPTH, 16])
    ssm_norm_g = din("ssm_norm_g", [DEPTH, 512])
    pool_w = din("pool_w", [DEPTH, 4, 64, 64])
    pool_scale = din("pool_scale", [DEPTH, 256])
    w_out = din("w_out", [DEPTH, D, D])
    ln1_g = din("ln1_g", [DEPTH, D])
    ln1_b = din("ln1_b", [DEPTH, D])
    w_up = din("w_up", [DEPTH, D, 2 * DFF])
    ffn_dw_w = din("ffn_dw_w", [DEPTH, 9, DFF])
    ffn_dw_b = din("ffn_dw_b", [DEPTH, DFF])
    w_down = din("w_down", [DEPTH, DFF, D])
    ln2_g = din("ln2_g", [DEPTH, D])
    ln2_b = din("ln2_b", [DEPTH, D])

    out_d = nc.dram_tensor("out", [TL, D], F32, kind="ExternalOutput").ap()
    kd = "ExternalOutput" if dbg else "Internal"
    xs_d = dscr("xs_d", [T, D], F32, kd)
    projT_d = dscr("projT_d", [18, 128, T], BF16, kd)
    catT_d = dscr("catT_d", [8, 128, T], BF16, kd)
    actT_d = dscr("actT_d", [NJ, 128, T], BF16, kd)
    gate_d = dscr("gate_d", [4, D], F32, kd)
    dta_d = dscr("dta_d", [128, NT * 32], F32, kd) if dbg else None

    es = ExitStack()
    S = Sched(nc, es)
    uid = [0]

    def sb(shape, dt, scope=None, name=None):
        uid[0] += 1
        t = (scope or es).enter_context(nc.sbuf_tensor("%s_%d" % (name or "t", uid[0]), list(shape), dt))
        return t, Buf(name or "t")

    def MM(out, lhsT, rhs, start, stop):
        return lambda: nc.tensor.matmul(out, lhsT=lhsT, rhs=rhs, start=start, stop=stop)

    def TR(out, in_, ident):
        return lambda: nc.tensor.transpose(out, in_, ident)

    PS = []
    for i in range(4):
        uid[0] += 1
        t = es.enter_context(nc.psum_tensor("ps2_%d" % i, [128, 1024], F32))
        PS.append((t, Buf("ps2_%d" % i)))
    PH = []
    for i in range(4):
        for h in range(2):
            PH.append((PS[i][0], h * 512, Buf("ph%d_%d" % (i, h))))

    def ph(i):
        t, o, b = PH[i]
        return (lambda a, c, t=t, o=o: t[:, o + a:o + c]), b

    ident_f, b_identf = sb([128, 128], F32, name="identf")
    ident_bf, b_identb = sb([128, 128], BF16, name="identb")
    ones_bf, b_ones = sb([128, 128], BF16, name="ones")
    onesm256, b_o256 = sb([128, 128], BF16, name="o256")
    onesm512, b_o512 = sb([128, 128], BF16, name="o512")
    U_bf, b_U = sb([128, 2, 128], BF16, name="U")
    nU_bf, b_nU = sb([128, 2, 128], BF16, name="nU")
    mask_bf, b_mask = sb([128, 2, 128], BF16, name="mask")
    ptap_bf, b_ptap = sb([128, 2, 16, 128], BF16, name="ptap")
    cst_s, b_cst = sb([128, 34], F32, name="csts")
    epsc, b_eps = sb([128, 4], F32, name="eps")
    dta, b_dta = sb([128, NT, 32], F32, name="dta")
    ahl, b_ahl = sb([128, NT, 2, 16], BF16, name="ahl")
    wst = [sb([128, 2816], F32, name="wst%d" % i) for i in range(2)]
    wsti = [0]
    modT, b_modT = sb([128, 4, 8, 2], F32, name="modT")
    colT, b_colT = sb([128, NJ, NROW], F32, name="colT")

    with ExitStack() as sc:
        cf, b_cf = sb([128, NCONST], F32, sc, "constf")
        S.dma("sp", cf[:], consts_d[:, :], writes=[b_cf])
        S.op("dve", lambda: nc.vector.tensor_copy(out=ident_f[:], in_=cf[:, C_ID:C_ID + 128]), reads=[b_cf], writes=[b_identf])
        S.op("dve", lambda: nc.vector.tensor_copy(out=ident_bf[:], in_=cf[:, C_ID:C_ID + 128]), reads=[b_cf], writes=[b_identb])
        S.op("dve", lambda: nc.vector.memset(ones_bf[:], 1.0), writes=[b_ones])
        S.op("dve", lambda: nc.vector.memset(onesm256[:], 1.0 / 256), writes=[b_o256])
        S.op("dve", lambda: nc.vector.memset(onesm512[:], 1.0 / 512), writes=[b_o512])
        S.op("dve", lambda: nc.vector.tensor_copy(out=U_bf[:].rearrange("p a b -> p (a b)"), in_=cf[:, C_UF:C_UF + 256]), reads=[b_cf], writes=[b_U])
        S.op("dve", lambda: nc.vector.tensor_scalar(out=nU_bf[:].rearrange("p a b -> p (a b)"), in0=cf[:, C_UF:C_UF + 256], scalar1=-1.0, scalar2=None, op0=ALU.mult), reads=[b_cf], writes=[b_nU])
        S.op("dve", lambda: nc.vector.tensor_copy(out=mask_bf[:].rearrange("p a b -> p (a b)"), in_=cf[:, C_MF:C_MF + 256]), reads=[b_cf], writes=[b_mask])
        S.op("dve", lambda: nc.vector.tensor_copy(out=ptap_bf[:].rearrange("p a b c -> p (a b c)"), in_=cf[:, C_PT:C_IW]), reads=[b_cf], writes=[b_ptap])
        S.op("dve", lambda: nc.vector.tensor_copy(out=cst_s[:], in_=cf[:, C_IW:NCONST]), reads=[b_cf], writes=[b_cst])
        S.op("dve", lambda: nc.vector.memset(epsc[:, 0:1], LN_EPS), writes=[b_eps])
        S.op("dve", lambda: nc.vector.memset(epsc[:, 1:2], 1.0), writes=[b_eps])
        S.op("dve", lambda: nc.vector.memset(epsc[:, 2:3], RMS_EPS), writes=[b_eps])
        S.op("dve", lambda: nc.vector.memset(epsc[:, 3:4], LN_EPS / (ALPHA * ALPHA)), writes=[b_eps])
        S.barrier()

    def load_cvt(dst_ap, dst_buf, src_ap, shape):
        st, bst = wst[wsti[0] % 2]
        wsti[0] += 1
        n = int(np.prod(shape))
        sv = st[:, 0:n]
        if len(shape) == 2:
            sv = sv.rearrange("p (a b) -> p a b", a=shape[0])
        S.dma("sp", sv, src_ap, writes=[bst])
        S.op("pool", lambda: nc.gpsimd.tensor_copy(out=dst_ap, in_=sv), reads=[bst], writes=[dst_buf])

    blocks = [(0, TC, 0)] + [(TC + 512 * i, 512, 1) for i in range(8)]
    seg_rng = [(0, TC), (TC, T)]

    for l in range(n_layers):
        last = (l == DEPTH - 1)
        xsrc = xs_in if l == 0 else xs_d

        with ExitStack() as sc:
            rows, b_rows = sb([NROW, 2816], F32, sc, "rows")
            S.op("dve", lambda: nc.vector.memset(rows[:], 0.0), writes=[b_rows])
            S.dma("sp", rows[R_FW:R_FW + 9, :], ffn_dw_w[l], writes=[b_rows])
            S.dma("sp", rows[R_FB:R_FB + 1, :], ffn_dw_b[l:l + 1, :], writes=[b_rows])
            S.dma("sp", rows[R_CW:R_CW + 31, 0:256], conv_dw_w[l], writes=[b_rows])
            S.dma("sp", rows[R_CB:R_CB + 1, 0:256], conv_dw_b[l:l + 1, :], writes=[b_rows])
            S.dma("sp", rows[R_CG:R_CG + 1, 0:256], conv_ln_g[l:l + 1, :], writes=[b_rows])
            S.dma("sp", rows[R_CLB:R_CLB + 1, 0:256], conv_ln_b[l:l + 1, :], writes=[b_rows])
            S.dma("sp", rows[R_PWB:R_PWB + 1, 0:256], conv_pw_b[l:l + 1, :], writes=[b_rows])
            S.dma("sp", rows[R_PS:R_PS + 1, 0:256], pool_scale[l:l + 1, :], writes=[b_rows])
            S.dma("sp", rows[R_SW:R_SW + 8, 0:1024], ssm_conv_w[l].rearrange("a k c -> (a k) c"), writes=[b_rows])
            S.dma("sp", rows[R_SB:R_SB + 2, 0:1024], ssm_conv_b[l], writes=[b_rows])
            S.dma("sp", rows[R_NG:R_NG + 1, 0:512], ssm_norm_g[l:l + 1, :], writes=[b_rows])
            pf, pb_ = ph(0)
            for j0 in range(0, NJ, 8):
                nj = min(8, NJ - j0)
                S.group("pe", [TR(pf(jj * 64, jj * 64 + 64), rows[:, (j0 + jj) * 128:(j0 + jj + 1) * 128], ident_f[0:NROW, 0:NROW])
                               for jj in range(nj)], reads=[b_rows, b_identf], writes=[pb_])
                S.op("dve", lambda: nc.vector.tensor_copy(out=colT[:, j0:j0 + nj, :].rearrange("p a b -> p (a b)"), in_=pf(0, nj * 64)),
                     reads=[pb_], writes=[b_colT])
            S.op("dve", lambda: nc.vector.tensor_scalar(out=colT[:, 0:8, R_SW:R_SW + 8], in0=colT[:, 0:8, R_SW:R_SW + 8], scalar1=0.5, scalar2=None, op0=ALU.mult),
                 reads=[b_colT], writes=[b_colT])
            S.op("dve", lambda: nc.vector.tensor_scalar(out=colT[:, 0:2, R_CW:R_CW + 31], in0=colT[:, 0:2, R_CW:R_CW + 31], scalar1=0.5, scalar2=None, op0=ALU.mult),
                 reads=[b_colT], writes=[b_colT])

            sT, b_sT = sb([128, 2, 8], F32, sc, "sT")
            bmT, b_bmT = sb([128, 48], F32, sc, "bmT")
            brow, b_brow = sb([1, 6 * D], F32, sc, "brow")
            wm = [sb([128, 8, 512], F32, sc, "wm%d" % i) for i in range(2)]
            grow, b_grow = sb([128, 512], F32, sc, "grow")
            onesf, b_onesf = sb([1, 128], F32, sc, "onesf")
            S.op("dve", lambda: nc.vector.memset(onesf[:], 1.0), writes=[b_onesf])
            for s_ in range(2):
                S.dma("sp", sT[:, s_, :], cvec[s_].rearrange("(k p) -> p k", p=128), writes=[b_sT], allow_slow_non_contiguous=True)
            S.dma("sp", bmT[:], b_mod[l].rearrange("(c p) -> p c", p=128), writes=[b_bmT], allow_slow_non_contiguous=True)
            S.dma("sp", brow[:], b_mod[l:l + 1, :], writes=[b_brow])
            S.op("act", lambda: nc.scalar.activation(out=sT[:], in_=sT[:], func=AF.Silu), reads=[b_sT], writes=[b_sT])
            for blk in range(12):
                wmt, b_wm = wm[blk % 2]
                S.dma("sp", wmt[:], w_mod[l, :, blk * 512:(blk + 1) * 512].rearrange("(k p) n -> p k n", p=128), writes=[b_wm])
                which = blk // 2
                if which in (2, 5):
                    gi = 0 if which == 2 else 1
                    for s in range(2):
                        pf, pb_ = ph(1 + s)
                        fns = [MM(pf(0, 512), sT[:, s, k:k + 1].to_broadcast([128, 128]), wmt[:, k, :], k == 0, False) for k in range(8)]
                        fns.append(MM(pf(0, 512), onesf[0:1, :], brow[0:1, blk * 512:(blk + 1) * 512], False, True))
                        S.group("pe", fns, reads=[b_sT, b_wm, b_onesf, b_brow], writes=[pb_])
                        S.op("dve", lambda: nc.vector.tensor_copy(out=grow[:], in_=pf(0, 512)), reads=[pb_], writes=[b_grow])
                        half = blk % 2
                        S.dma("sp", gate_d[gi * 2 + s:gi * 2 + s + 1, half * 512:(half + 1) * 512], grow[0:1, :], reads=[b_grow], writes=[])
                else:
                    mi = {0: 0, 1: 1, 3: 2, 4: 3}[which]
                    pf, pb_ = ph(3)
                    fns = []
                    for oc in range(4):
                        for k in range(8):
                            fns.append(MM(pf(oc * 2, oc * 2 + 2), wmt[:, k, oc * 128:(oc + 1) * 128], sT[:, :, k], k == 0, k == 7))
                    S.group("pe", fns, reads=[b_sT, b_wm], writes=[pb_])
                    c0 = (blk % 2) * 4
                    for oc in range(4):
                        gc = blk * 4 + oc
                        S.op("dve", lambda: nc.vector.tensor_scalar(out=modT[:, mi, c0 + oc, :], in0=pf(oc * 2, oc * 2 + 2), scalar1=bmT[:, gc:gc + 1],
                                                                     scalar2=None, op0=ALU.add), reads=[pb_, b_bmT], writes=[b_modT])
                        if mi in (1, 3):
                            S.op("dve", lambda: nc.vector.tensor_scalar(out=modT[:, mi, c0 + oc, :], in0=modT[:, mi, c0 + oc, :], scalar1=1.0,
                                                                         scalar2=None, op0=ALU.add), reads=[b_modT], writes=[b_modT])
            S.barrier()
        if stop_after == "mod":
            break

        with ExitStack() as sc:
            win, b_win = sb([128, 8, INC], BF16, sc, "win")
            xt = [sb([128, 4, D], F32, sc, "xtA%d" % i) for i in range(2)]
            hT = [sb([128, 8, 512], BF16, sc, "hT%d" % i) for i in range(2)]
            stg = [sb([128, 18, 512], BF16, sc, "stgA%d" % i) for i in range(2)]
            dtb, b_dtb = sb([128, 16], F32, sc, "dtb")
            expA, b_expA = sb([128, 16], F32, sc, "expA")
            tmp16 = [sb([128, 16], F32, sc, "tmp16_%d" % i) for i in range(2)]
            S.dma("sp", dtb[:], ssm_dt_bias[l:l + 1, :].partition_broadcast(128), writes=[b_dtb])
            S.dma("sp", expA[:], ssm_A_log[l:l + 1, :].partition_broadcast(128), writes=[b_expA])
            S.op("act", lambda: nc.scalar.activation(out=expA[:], in_=expA[:], func=AF.Exp), reads=[b_expA], writes=[b_expA])
            for k in range(8):
                load_cvt(win[:, k, :], b_win, w_in[l, k * 128:(k + 1) * 128, :], [INC])
            for bi, (t0, nt, seg) in enumerate(blocks):
                ntile = nt // 128
                xtt, b_xt = xt[bi % 2]
                hTt, b_hT = hT[bi % 2]
                stt, b_st = stg[bi % 2]
                S.dma("sp", xtt[:, 0:ntile, :], xsrc[t0:t0 + nt, :].rearrange("(i p) d -> p i d", p=128), writes=[b_xt])
                for c in range(8):
                    pf, pb_ = ph(c % 2)
                    S.group("pe", [TR(pf(i * 128, (i + 1) * 128), xtt[:, i, c * 128:(c + 1) * 128], ident_f[:]) for i in range(ntile)],
                            reads=[b_xt, b_identf], writes=[pb_])
                    S.op("act", lambda: nc.scalar.activation(out=hTt[:, c, 0:nt], in_=pf(0, nt), func=AF.Identity,
                                                             scale=modT[:, 1, c, seg:seg + 1], bias=modT[:, 0, c, seg:seg + 1]),
                         reads=[pb_, b_modT], writes=[b_hT])
                for oc in range(18):
                    col0 = oc * 128 if oc < 16 else 2064 + (oc - 16) * 128
                    pf, pb_ = ph(2 + oc % 4)
                    S.group("pe", [MM(pf(0, nt), win[:, k, col0:col0 + 128], hTt[:, k, 0:nt], k == 0, k == 7) for k in range(8)],
                            reads=[b_win, b_hT], writes=[pb_])
                    if oc % 2 == 0:
                        S.op("act", lambda: nc.scalar.copy(out=stt[:, oc, 0:nt], in_=pf(0, nt)), reads=[pb_], writes=[b_st])
                    else:
                        S.op("dve", lambda: nc.vector.tensor_copy(out=stt[:, oc, 0:nt], in_=pf(0, nt)), reads=[pb_], writes=[b_st])
                S.dma("pool", projT_d[:, :, t0:t0 + nt].rearrange("c p t -> p c t"), stt[:, :, 0:nt], reads=[b_st], writes=[])
                for i in range(ntile):
                    ti = t0 // 128 + i
                    pf, pb_ = ph(6 + i % 2)
                    S.group("pe", [MM(pf(0, 16), hTt[:, k, i * 128:(i + 1) * 128], win[:, k, 2048:2064], k == 0, k == 7) for k in range(8)],
                            reads=[b_win, b_hT], writes=[pb_])
                    t16, b_t16 = tmp16[i % 2]
                    S.op("dve", lambda: nc.vector.tensor_tensor(out=t16[:], in0=pf(0, 16), in1=dtb[:], op=ALU.add), reads=[pb_, b_dtb], writes=[b_t16])
                    S.op("act", lambda: nc.scalar.activation(out=dta[:, ti, 0:16], in_=t16[:], func=AF.Exp), reads=[b_t16], writes=[b_dta])
            S.op("act", lambda: nc.scalar.activation(out=dta[:, :, 0:16], in_=dta[:, :, 0:16], func=AF.Ln, bias=epsc[:, 1:2], scale=1.0),
                 reads=[b_dta, b_eps], writes=[b_dta])
            S.op("dve", lambda: nc.vector.scalar_tensor_tensor(out=dta[:, :, 16:32], in0=dta[:, :, 0:16], scalar=-1.0,
                                                               in1=expA[:].unsqueeze(1).to_broadcast([128, NT, 16]), op0=ALU.mult, op1=ALU.mult),
                 reads=[b_dta, b_expA], writes=[b_dta])
            alo, b_alo = sb([128, NT, 16], F32, sc, "alo")
            S.op("dve", lambda: nc.vector.tensor_copy(out=ahl[:, :, 0, :], in_=dta[:, :, 16:32]), reads=[b_dta], writes=[b_ahl])
            S.op("dve", lambda: nc.vector.tensor_tensor(out=alo[:], in0=dta[:, :, 16:32], in1=ahl[:, :, 0, :], op=ALU.subtract),
                 reads=[b_dta, b_ahl], writes=[b_alo])
            S.op("dve", lambda: nc.vector.tensor_copy(out=ahl[:, :, 1, :], in_=alo[:]), reads=[b_alo], writes=[b_ahl])
            if dbg:
                S.dma("sp", dta_d[:, :], dta[:].rearrange("p a b -> p (a b)"), reads=[b_dta], writes=[])
            S.barrier()
        if stop_after == "A":
            break

        with ExitStack() as sc:
            cdiag, b_cdiag = sb([128, 2, 31, 128], BF16, sc, "cdiag")
            pw, b_pw = sb([128, 2, 256], BF16, sc, "pw")
            pbd, b_pbd = sb([128, 2, 128], BF16, sc, "pbd")
            pbdf, b_pbdf = sb([128, 2, 128], F32, sc, "pbdf")
            cin = [sb([128, 6, 542], BF16, sc, "cin%d" % i) for i in range(2)]
            sig, b_sig = sb([128, 2, 542], BF16, sc, "sig")
            ug, b_ug = sb([128, 2, 542], BF16, sc, "ug")
            uc, b_uc = sb([128, 2, 512], F32, sc, "uc")
            ucb, b_ucb = sb([128, 2, 512], BF16, sc, "ucb")
            usq, b_usq = sb([128, 2, 512], BF16, sc, "usq")
            m2, b_m2 = sb([128, 512], F32, sc, "m2")
            rstd, b_rstd = sb([128, 512], F32, sc, "rstd")
            tn, b_tn = sb([128, 2, 512], F32, sc, "tn")
            un, b_un = sb([128, 2, 512], BF16, sc, "un")
            rp, b_rp = sb([128, 2, 512], BF16, sc, "rp")
            ed, b_ed = sb([128, 8], F32, sc, "ed")
            cst = [sb([128, 4, 512], BF16, sc, "cst%d" % i) for i in range(2)]
            for c in range(2):
                for k in range(31):
                    S.op("dve", lambda: nc.vector.tensor_scalar(out=cdiag[:, c, k, :], in0=ident_f[:], scalar1=colT[:, c, R_CW + k:R_CW + k + 1],
                                                                 scalar2=None, op0=ALU.mult), reads=[b_identf, b_colT], writes=[b_cdiag])
            for k in range(2):
                load_cvt(pw[:, k, :], b_pw, conv_pw_w[l, k * 128:(k + 1) * 128, :], [256])
            S.op("dve", lambda: nc.vector.memset(pbdf[:], 0.0), writes=[b_pbdf])
            for g in range(4):
                S.dma("sp", pbdf[(g % 2) * 64:(g % 2) * 64 + 64, g // 2, (g % 2) * 64:(g % 2) * 64 + 64], pool_w[l, g], writes=[b_pbdf])
            S.op("dve", lambda: nc.vector.tensor_copy(out=pbd[:], in_=pbdf[:]), reads=[b_pbdf], writes=[b_pbd])
            for bi, (t0, nt, seg) in enumerate(blocks):
                s0, s1 = seg_rng[seg]
                cint, b_cin = cin[bi % 2]
                cstt, b_cst2 = cst[bi % 2]
                lo = max(t0 - 15, s0)
                hi = min(t0 + nt + 15, s1)
                o0 = lo - (t0 - 15)
                if lo > t0 - 15:
                    S.op("dve", lambda: nc.vector.memset(cint[:, :, 0:15], 0.0), writes=[b_cin])
                if hi < t0 + nt + 15:
                    S.op("dve", lambda: nc.vector.memset(cint[:, :, nt + 15:nt + 30], 0.0), writes=[b_cin])
                S.dma("sp", cint[:, 0:4, o0:o0 + hi - lo], projT_d[0:4, :, lo:hi].rearrange("c p t -> p c t"), writes=[b_cin])
                S.dma("sp", cint[:, 4:6, o0:o0 + hi - lo], projT_d[16:18, :, lo:hi].rearrange("c p t -> p c t"), writes=[b_cin])
                W = nt + 30
                S.op("act", lambda: nc.scalar.activation(out=sig[:, :, 0:W], in_=cint[:, 2:4, 0:W], func=AF.Tanh, scale=0.5), reads=[b_cin], writes=[b_sig])
                S.op("dve", lambda: nc.vector.scalar_tensor_tensor(out=ug[:, :, 0:W], in0=sig[:, :, 0:W], scalar=1.0, in1=cint[:, 0:2, 0:W],
                                                                   op0=ALU.add, op1=ALU.mult), reads=[b_cin, b_sig], writes=[b_ug])
                for c in range(2):
                    pf, pb_ = ph(c)
                    S.group("pe", [MM(pf(0, nt), cdiag[:, c, k, :], ug[:, c, k:k + nt], k == 0, k == 30) for k in range(31)],
                            reads=[b_cdiag, b_ug], writes=[pb_])
                    S.op("act", lambda: nc.scalar.activation(out=uc[:, c, 0:nt], in_=pf(0, nt), func=AF.Identity, bias=colT[:, c, R_CB:R_CB + 1], scale=1.0),
                         reads=[pb_, b_colT], writes=[b_uc])
                S.op("dve", lambda: nc.vector.tensor_copy(out=ucb[:, :, 0:nt], in_=uc[:, :, 0:nt]), reads=[b_uc], writes=[b_ucb])
                S.op("act", lambda: nc.scalar.activation(out=usq[:, :, 0:nt], in_=uc[:, :, 0:nt], func=AF.Square), reads=[b_uc], writes=[b_usq])
                pm, pbm = ph(2)
                pq, pbq = ph(3)
                S.group("pe", [MM(pm(0, nt), onesm256[:], ucb[:, c, 0:nt], c == 0, c == 1) for c in range(2)], reads=[b_o256, b_ucb], writes=[pbm])
                S.group("pe", [MM(pq(0, nt), onesm256[:], usq[:, c, 0:nt], c == 0, c == 1) for c in range(2)], reads=[b_o256, b_usq], writes=[pbq])
                S.op("act", lambda: nc.scalar.activation(out=m2[:, 0:nt], in_=pm(0, nt), func=AF.Square), reads=[pbm], writes=[b_m2])
                S.op("dve", lambda: nc.vector.tensor_tensor(out=m2[:, 0:nt], in0=pq(0, nt), in1=m2[:, 0:nt], op=ALU.subtract), reads=[pbq, b_m2], writes=[b_m2])
                S.op("act", lambda: nc.scalar.activation(out=rstd[:, 0:nt], in_=m2[:, 0:nt], func=AF.Sqrt, bias=epsc[:, 0:1], scale=1.0),
                     reads=[b_m2, b_eps], writes=[b_rstd])
                S.op("dve", lambda: nc.vector.reciprocal(out=rstd[:, 0:nt], in_=rstd[:, 0:nt]), reads=[b_rstd], writes=[b_rstd])
                for c in range(2):
                    S.op("dve", lambda: nc.vector.tensor_tensor(out=tn[:, c, 0:nt], in0=uc[:, c, 0:nt], in1=pm(0, nt), op=ALU.subtract),
                         reads=[b_uc, pbm], writes=[b_tn])
                    S.op("dve", lambda: nc.vector.tensor_tensor(out=tn[:, c, 0:nt], in0=tn[:, c, 0:nt], in1=rstd[:, 0:nt], op=ALU.mult),
                         reads=[b_tn, b_rstd], writes=[b_tn])
                    S.op("act", lambda: nc.scalar.activation(out=un[:, c, 0:nt], in_=tn[:, c, 0:nt], func=AF.Silu,
                                                             scale=colT[:, c, R_CG:R_CG + 1], bias=colT[:, c, R_CLB:R_CLB + 1]),
                         reads=[b_tn, b_colT], writes=[b_un])
                for oc in range(2):
                    pf, pb_ = ph(4 + oc)
                    S.group("pe", [MM(pf(0, nt), pw[:, c, oc * 128:(oc + 1) * 128], un[:, c, 0:nt], c == 0, c == 1) for c in range(2)],
                            reads=[b_pw, b_un], writes=[pb_])
                    S.op("act", lambda: nc.scalar.activation(out=cstt[:, oc, 0:nt], in_=pf(0, nt), func=AF.Identity,
                                                             bias=colT[:, oc, R_PWB:R_PWB + 1], scale=1.0), reads=[pb_, b_colT], writes=[b_cst2])
                for c in range(2):
                    pf, pb_ = ph(6 + c)
                    S.group("pe", [MM(pf(0, nt), ptap_bf[:, c, d, :], cint[:, 4 + c, 7 + d:7 + d + nt], d == 0, d == 15) for d in range(16)],
                            reads=[b_ptap, b_cin], writes=[pb_])
                    S.op("dve", lambda: nc.vector.scalar_tensor_tensor(out=rp[:, c, 0:nt], in0=pf(0, nt), scalar=cst_s[:, c:c + 1],
                                                                       in1=cint[:, 4 + c, 15:15 + nt], op0=ALU.mult, op1=ALU.subtract),
                         reads=[pb_, b_cst, b_cin], writes=[b_rp])
                    if t0 == s0:
                        S.op("dve", lambda: nc.vector.tensor_tensor(out=ed[:], in0=pf(0, 8), in1=cst_s[:, 2 + c * 8:2 + c * 8 + 8], op=ALU.mult),
                             reads=[pb_, b_cst], writes=[b_ed])
                        S.op("dve", lambda: nc.vector.tensor_tensor(out=rp[:, c, 0:8], in0=ed[:], in1=cint[:, 4 + c, 15:23], op=ALU.subtract),
                             reads=[b_ed, b_cin], writes=[b_rp])
                    if t0 + nt == s1:
                        S.op("dve", lambda: nc.vector.tensor_tensor(out=ed[:], in0=pf(nt - 8, nt), in1=cst_s[:, 18 + c * 8:18 + c * 8 + 8], op=ALU.mult),
                             reads=[pb_, b_cst], writes=[b_ed])
                        S.op("dve", lambda: nc.vector.tensor_tensor(out=rp[:, c, nt - 8:nt], in0=ed[:], in1=cint[:, 4 + c, 15 + nt - 8:15 + nt], op=ALU.subtract),
                             reads=[b_ed, b_cin], writes=[b_rp])
                for c in range(2):
                    pf, pb_ = ph(c)
                    S.group("pe", [MM(pf(0, nt), pbd[:, c, :], rp[:, c, 0:nt], True, True)], reads=[b_pbd, b_rp], writes=[pb_])
                    S.op("act", lambda: nc.scalar.activation(out=cstt[:, 2 + c, 0:nt], in_=pf(0, nt), func=AF.Identity, scale=colT[:, c, R_PS:R_PS + 1]),
                         reads=[pb_, b_colT], writes=[b_cst2])
                S.dma("pool", catT_d[0:2, :, t0:t0 + nt].rearrange("c p t -> p c t"), cstt[:, 0:2, 0:nt], reads=[b_cst2], writes=[])
                S.dma("pool", catT_d[6:8, :, t0:t0 + nt].rearrange("c p t -> p c t"), cstt[:, 2:4, 0:nt], reads=[b_cst2], writes=[])
            S.barrier()
        if stop_after == "B1":
            break

        with ExitStack() as scy:
            ybuf, b_y = sb([128, 4, T], F32, scy, "ybuf")
            with ExitStack() as sc:
                sdiag, b_sdiag = sb([128, 2, 8, 4, 128], BF16, sc, "sdiag")
                ddiag, b_ddiag = sb([128, 16, 128], BF16, sc, "ddiag")
                dsk, b_dsk = sb([128, 16], F32, sc, "dsk")
                cbrow, b_cbrow = sb([1, 2, 1024], BF16, sc, "cbrow")
                cbrowf, b_cbrowf = sb([1, 2, 1024], F32, sc, "cbrowf")
                S.dma("sp", dsk[:], ssm_D[l:l + 1, :].partition_broadcast(128), writes=[b_dsk])
                S.dma("sp", cbrowf[:], ssm_conv_b[l:l + 1, :, :], writes=[b_cbrowf])
                S.op("dve", lambda: nc.vector.tensor_scalar(out=cbrow[:], in0=cbrowf[:], scalar1=0.5, scalar2=None, op0=ALU.mult),
                     reads=[b_cbrowf], writes=[b_cbrow])
                for d in range(2):
                    for c in range(8):
                        for k in range(4):
                            S.op("dve", lambda: nc.vector.tensor_scalar(out=sdiag[:, d, c, k, :], in0=ident_f[:],
                                                                         scalar1=colT[:, c, R_SW + d * 4 + k:R_SW + d * 4 + k + 1], scalar2=None, op0=ALU.mult),
                                 reads=[b_identf, b_colT], writes=[b_sdiag])
                    for h in range(8):
                        S.op("dve", lambda: nc.vector.tensor_scalar(out=ddiag[:, d * 8 + h, :], in0=ident_f[:], scalar1=dsk[:, d * 8 + h:d * 8 + h + 1],
                                                                     scalar2=None, op0=ALU.mult), reads=[b_identf, b_dsk], writes=[b_ddiag])
                dbufs = []
                for d in range(2):
                    B_ = {}
                    B_["xbc"] = [sb([128, 8, 131], BF16, sc, "xbc%d" % d) for _ in range(2)]
                    B_["th"] = [sb([128, 512], F32, sc, "th%d_%d" % (d, i)) for i in range(3)]
                    B_["xs"] = sb([128, 512], BF16, sc, "xs%d" % d)
                    B_["xdt"] = sb([128, 512], BF16, sc, "xdt%d" % d)
                    B_["xdtw"] = sb([128, 512], BF16, sc, "xdtw%d" % d)
                    B_["btok"] = sb([128, 256], BF16, sc, "btok%d" % d)
                    B_["bct"] = sb([128, 4, 128], BF16, sc, "bct%d" % d)
                    B_["eX"] = sb([128, 1024], BF16, sc, "eX%d" % d)
                    B_["L"] = sb([128, 1024], BF16, sc, "L%d" % d)
                    B_["M"] = sb([128, 1024], BF16, sc, "M%d" % d)
                    B_["Cw"] = sb([128, 1024], BF16, sc, "Cw%d" % d)
                    B_["S32"] = sb([128, 512], F32, sc, "S32%d" % d)
                    B_["Sbf"] = sb([128, 512], BF16, sc, "Sbf%d" % d)
                    dbufs.append(B_)
                    S.op("dve", lambda: nc.vector.memset(B_["S32"][0][:], 0.0), writes=[B_["S32"][1]])
                    S.op("dve", lambda: nc.vector.memset(B_["Sbf"][0][:], 0.0), writes=[B_["Sbf"][1]])
                order = [list(range(NT)), [1, 0] + list(range(NT - 1, 1, -1))]
                visit = [{x: i for i, x in enumerate(order[d])} for d in range(2)]
                qs = [(step, d) for step in range(NT) for d in range(2)]

                def ctxq(q):
                    step, d = qs[q]
                    X = order[d][step]
                    seg = 0 if X < 2 else 1
                    return step, d, X, seg, X * 128, dbufs[d]

                def stage_ab(q):
                    step, d, X, seg, tt0, B_ = ctxq(q)
                    s0, s1 = seg_rng[seg]
                    xbc, b_xbc = B_["xbc"][step % 2]
                    if d == 0:
                        lo, hi, base = max(tt0 - 3, s0), tt0 + 128, tt0 - 3
                        if lo > tt0 - 3:
                            S.op("dve", lambda: nc.vector.memset(xbc[:, :, 0:3], 0.0), writes=[b_xbc])
                        offs = [0, 1, 2, 3]
                    else:
                        lo, hi, base = tt0, min(tt0 + 131, s1), tt0
                        if hi < tt0 + 131:
                            S.op("dve", lambda: nc.vector.memset(xbc[:, :, 128:131], 0.0), writes=[b_xbc])
                        offs = [3, 2, 1, 0]
                    S.dma("sp", xbc[:, :, lo - base:hi - base], projT_d[8:16, :, lo:hi].rearrange("c p t -> p c t"), writes=[b_xbc])
                    pxs, pb_xs = ph(0)
                    fns = []
                    for c in range(4):
                        for k in range(4):
                            fns.append(MM(pxs(c * 128, (c + 1) * 128), xbc[:, c, offs[k]:offs[k] + 128], sdiag[:, d, c, k, :], k == 0, False))
                        fns.append(MM(pxs(c * 128, (c + 1) * 128), ones_bf[0:1, :], cbrow[0:1, d, c * 128:(c + 1) * 128], False, True))
                    S.group("pe", fns, reads=[b_xbc, b_sdiag, b_ones, b_cbrow], writes=[pb_xs])
                    xs, b_xs = B_["xs"]
                    th0, b_th0 = B_["th"][0]
                    S.op("act", lambda: nc.scalar.activation(out=th0[:], in_=pxs(0, 512), func=AF.Tanh), reads=[pb_xs], writes=[b_th0])
                    S.op("dve", lambda: nc.vector.scalar_tensor_tensor(out=xs[:], in0=th0[:], scalar=1.0, in1=pxs(0, 512), op0=ALU.add, op1=ALU.mult),
                         reads=[b_th0, pb_xs], writes=[b_xs])
                    pbt, pb_bt = ph(1)
                    fns = []
                    for c in range(4, 6):
                        o = (c - 4) * 128
                        for k in range(4):
                            fns.append(MM(pbt(o, o + 128), xbc[:, c, offs[k]:offs[k] + 128], sdiag[:, d, c, k, :], k == 0, False))
                        fns.append(MM(pbt(o, o + 128), ones_bf[0:1, :], cbrow[0:1, d, c * 128:(c + 1) * 128], False, True))
                    S.group("pe", fns, reads=[b_xbc, b_sdiag, b_ones, b_cbrow], writes=[pb_bt])
                    btok, b_btok = B_["btok"]
                    th1, b_th1 = B_["th"][1]
                    S.op("act", lambda: nc.scalar.activation(out=th1[:, 0:256], in_=pbt(0, 256), func=AF.Tanh), reads=[pb_bt], writes=[b_th1])
                    S.op("dve", lambda: nc.vector.scalar_tensor_tensor(out=btok[:], in0=th1[:, 0:256], scalar=1.0, in1=pbt(0, 256), op0=ALU.add, op1=ALU.mult),
                         reads=[b_th1, pb_bt], writes=[b_btok])
                    pct, pb_ct = ph(2)
                    fns = []
                    for c in range(4, 8):
                        o = (c - 4) * 128
                        for k in range(4):
                            fns.append(MM(pct(o, o + 128), sdiag[:, d, c, k, :], xbc[:, c, offs[k]:offs[k] + 128], k == 0, False))
                        fns.append(MM(pct(o, o + 128), cbrow[0:1, d, c * 128:(c + 1) * 128], ones_bf[0:1, :], False, True))
                    S.group("pe", fns, reads=[b_xbc, b_sdiag, b_ones, b_cbrow], writes=[pb_ct])
                    bct, b_bct = B_["bct"]
                    th2, b_th2 = B_["th"][2]
                    S.op("act", lambda: nc.scalar.activation(out=th2[:], in_=pct(0, 512), func=AF.Tanh), reads=[pb_ct], writes=[b_th2])
                    S.op("dve", lambda: nc.vector.scalar_tensor_tensor(out=bct[:].rearrange("p a b -> p (a b)"), in0=th2[:], scalar=1.0, in1=pct(0, 512),
                                                                       op0=ALU.add, op1=ALU.mult), reads=[b_th2, pb_ct], writes=[b_bct])
                    pDX, pb_DX = PS[2]
                    fns = []
                    for h in range(8):
                        for part in range(2):
                            fns.append(MM(pDX[:, h * 128:(h + 1) * 128], ahl[:, X, part, d * 8 + h:d * 8 + h + 1].to_broadcast([128, 128]),
                                          U_bf[:, d, :], part == 0, part == 1))
                    S.group("pe", fns, reads=[b_ahl, b_U], writes=[pb_DX])
                    eX, b_eX = B_["eX"]
                    for hf in range(2):
                        S.op("act", lambda: nc.scalar.activation(out=eX[:, hf * 512:(hf + 1) * 512], in_=pDX[:, hf * 512:(hf + 1) * 512], func=AF.Exp),
                             reads=[pb_DX], writes=[b_eX])

                def stage_c(q):
                    step, d, X, seg, tt0, B_ = ctxq(q)
                    pDX, pb_DX = PS[2]
                    fns = []
                    for hf in range(2):
                        fns.append(MM(pDX[:, hf * 512:(hf + 1) * 512], ident_bf[:], mask_bf[:, d, :].unsqueeze(1).to_broadcast([128, 4, 128]), True, False))
                        for part in range(2):
                            fns.append(MM(pDX[:, hf * 512:(hf + 1) * 512], nU_bf[:, d, :],
                                          ahl[:, X, part, d * 8 + hf * 4:d * 8 + hf * 4 + 4].unsqueeze(2).to_broadcast([128, 4, 128]), False, False))
                    for h in range(8):
                        for part in range(2):
                            fns.append(MM(pDX[:, h * 128:(h + 1) * 128], ahl[:, X, part, d * 8 + h:d * 8 + h + 1].to_broadcast([128, 128]),
                                          U_bf[:, d, :], False, part == 1))
                    S.group("pe", fns, reads=[b_ahl, b_U, b_nU, b_mask, b_identb], writes=[pb_DX])
                    Lt, b_L = B_["L"]
                    for hf in range(2):
                        S.op("act", lambda: nc.scalar.activation(out=Lt[:, hf * 512:(hf + 1) * 512], in_=pDX[:, hf * 512:(hf + 1) * 512], func=AF.Exp),
                             reads=[pb_DX], writes=[b_L])
                    bct, b_bct = B_["bct"]
                    xs, b_xs = B_["xs"]
                    eX, b_eX = B_["eX"]
                    psc, pb_sc = ph(7)
                    S.group("pe", [MM(psc(g * 128, (g + 1) * 128), bct[:, g, :], bct[:, 2 + g, :], True, True) for g in range(2)],
                            reads=[b_bct], writes=[pb_sc])
                    Mt, b_M = B_["M"]
                    S.op("dve", lambda: nc.vector.tensor_tensor(
                        out=Mt[:].rearrange("p (g h l) -> p g h l", g=2, h=4),
                        in0=Lt[:].rearrange("p (g h l) -> p g h l", g=2, h=4),
                        in1=psc(0, 256).rearrange("p (g l) -> p g l", g=2).unsqueeze(2).to_broadcast([128, 2, 4, 128]), op=ALU.mult),
                        reads=[b_L, pb_sc], writes=[b_M])
                    xdt, b_xdt = B_["xdt"]
                    S.op("dve", lambda: nc.vector.tensor_tensor(
                        out=xdt[:].rearrange("p (h q) -> p h q", h=8), in0=xs[:].rearrange("p (h q) -> p h q", h=8),
                        in1=dta[:, X, d * 8:d * 8 + 8].unsqueeze(2).to_broadcast([128, 8, 64]), op=ALU.mult),
                        reads=[b_xs, b_dta], writes=[b_xdt])
                    ll = 127 if d == 0 else 0
                    xdtw, b_xdtw = B_["xdtw"]
                    S.op("dve", lambda: nc.vector.tensor_tensor(
                        out=xdtw[:].rearrange("p (h q) -> p h q", h=8), in0=xdt[:].rearrange("p (h q) -> p h q", h=8),
                        in1=Lt[:].rearrange("p (h l) -> p h l", h=8)[:, :, ll:ll + 1].to_broadcast([128, 8, 64]), op=ALU.mult),
                        reads=[b_xdt, b_L], writes=[b_xdtw])
                    Cw, b_Cw = B_["Cw"]
                    S.op("dve", lambda: nc.vector.tensor_tensor(
                        out=Cw[:].rearrange("p (g h l) -> p g h l", g=2, h=4),
                        in0=eX[:].rearrange("p (g h l) -> p g h l", g=2, h=4),
                        in1=bct[:, 2:4, :].unsqueeze(2).to_broadcast([128, 2, 4, 128]), op=ALU.mult),
                        reads=[b_eX, b_bct], writes=[b_Cw])

                def stage_d(q):
                    step, d, X, seg, tt0, B_ = ctxq(q)
                    ll = 127 if d == 0 else 0
                    xs, b_xs = B_["xs"]
                    xdt, b_xdt = B_["xdt"]
                    xdtw, b_xdtw = B_["xdtw"]
                    btok, b_btok = B_["btok"]
                    eX, b_eX = B_["eX"]
                    Mt, b_M = B_["M"]
                    Cw, b_Cw = B_["Cw"]
                    S32, b_S32 = B_["S32"]
                    Sbf, b_Sbf = B_["Sbf"]
                    py, pb_y = ph(3)
                    pyt = PH[3][0]
                    fns = []
                    for h in range(8):
                        pr, hf = h // 2, h % 2
                        o = pyt[hf * 64:(hf + 1) * 64, 512 + pr * 128:512 + (pr + 1) * 128]
                        fns.append(MM(o, xdt[:, h * 64:(h + 1) * 64], Mt[:, h * 128:(h + 1) * 128], True, False))
                        fns.append(MM(o, xs[:, h * 64:(h + 1) * 64], ddiag[:, d * 8 + h, :], False, False))
                        fns.append(MM(o, Sbf[:, h * 64:(h + 1) * 64], Cw[:, h * 128:(h + 1) * 128], False, True))
                    S.group("pe", fns, reads=[b_xdt, b_M, b_xs, b_ddiag, b_Sbf, b_Cw], writes=[pb_y])
                    first = visit[d][X] < visit[1 - d][X] or (visit[d][X] == visit[1 - d][X] and d == 0)
                    yv = ybuf[:, :, tt0:tt0 + 128]
                    if first:
                        S.op("act", lambda: nc.scalar.copy(out=yv, in_=py(0, 512).rearrange("p (a b) -> p a b", a=4)), reads=[pb_y], writes=[b_y])
                    else:
                        S.op("dve", lambda: nc.vector.tensor_tensor(out=yv, in0=yv, in1=py(0, 512).rearrange("p (a b) -> p a b", a=4), op=ALU.add),
                             reads=[pb_y, b_y], writes=[b_y])
                    pst, pb_st = ph(6)
                    S.group("pe", [MM(pst(g * 256, (g + 1) * 256), btok[:, g * 128:(g + 1) * 128], xdtw[:, g * 256:(g + 1) * 256], True, True)
                                   for g in range(2)], reads=[b_btok, b_xdtw], writes=[pb_st])
                    S.op("dve", lambda: nc.vector.tensor_tensor(
                        out=S32[:].rearrange("p (h q) -> p h q", h=8), in0=S32[:].rearrange("p (h q) -> p h q", h=8),
                        in1=eX[:].rearrange("p (h l) -> p h l", h=8)[:, :, ll:ll + 1].to_broadcast([128, 8, 64]), op=ALU.mult),
                        reads=[b_S32, b_eX], writes=[b_S32])
                    S.op("dve", lambda: nc.vector.tensor_tensor(out=S32[:], in0=S32[:], in1=pst(0, 512), op=ALU.add), reads=[b_S32, pb_st], writes=[b_S32])
                    S.op("act", lambda: nc.scalar.copy(out=Sbf[:], in_=S32[:]), reads=[b_S32], writes=[b_Sbf])

                NQ = len(qs)
                stage_ab(0)
                stage_c(0)
                for q in range(NQ):
                    if q + 1 < NQ:
                        stage_ab(q + 1)
                    stage_d(q)
                    if q + 1 < NQ:
                        stage_c(q + 1)
                S.barrier()
            if stop_after == "B2":
                break
            with ExitStack() as sc:
                zb = [sb([128, 4, 512], BF16, sc, "zb%d" % i) for i in range(2)]
                sz, b_sz = sb([128, 4, 512], F32, sc, "sz")
                vv, b_vv = sb([128, 4, 512], F32, sc, "vv")
                vsq, b_vsq = sb([128, 4, 512], BF16, sc, "vsq")
                rs, b_rs = sb([128, 512], F32, sc, "rs")
                cst = [sb([128, 4, 512], BF16, sc, "cstB%d" % i) for i in range(2)]
                for bi, (t0, nt, seg) in enumerate(blocks):
                    zt, b_z = zb[bi % 2]
                    S.dma("sp", zt[:, :, 0:nt], projT_d[4:8, :, t0:t0 + nt].rearrange("c p t -> p c t"), writes=[b_z])
                    S.op("act", lambda: nc.scalar.activation(out=sz[:, :, 0:nt], in_=zt[:, :, 0:nt], func=AF.Silu), reads=[b_z], writes=[b_sz])
                    S.op("dve", lambda: nc.vector.tensor_tensor(out=ybuf[:, :, t0:t0 + nt], in0=ybuf[:, :, t0:t0 + nt], in1=sz[:, :, 0:nt], op=ALU.mult),
                         reads=[b_y, b_sz], writes=[b_y])
                for bi, (t0, nt, seg) in enumerate(blocks):
                    cstt, b_cst2 = cst[bi % 2]
                    S.op("act", lambda: nc.scalar.activation(out=vsq[:, :, 0:nt], in_=ybuf[:, :, t0:t0 + nt], func=AF.Square), reads=[b_y], writes=[b_vsq])
                    pm, pbm = ph(bi % 2)
                    S.group("pe", [MM(pm(0, nt), onesm512[:], vsq[:, c, 0:nt], c == 0, c == 3) for c in range(4)], reads=[b_o512, b_vsq], writes=[pbm])
                    S.op("act", lambda: nc.scalar.activation(out=rs[:, 0:nt], in_=pm(0, nt), func=AF.Sqrt, bias=epsc[:, 2:3], scale=1.0),
                         reads=[pbm, b_eps], writes=[b_rs])
                    S.op("dve", lambda: nc.vector.reciprocal(out=rs[:, 0:nt], in_=rs[:, 0:nt]), reads=[b_rs], writes=[b_rs])
                    for c in range(4):
                        S.op("dve", lambda: nc.vector.scalar_tensor_tensor(out=cstt[:, c, 0:nt], in0=ybuf[:, c, t0:t0 + nt], scalar=colT[:, c, R_NG:R_NG + 1],
                                                                           in1=rs[:, 0:nt], op0=ALU.mult, op1=ALU.mult),
                             reads=[b_y, b_colT, b_rs], writes=[b_cst2])
                    S.dma("pool", catT_d[2:6, :, t0:t0 + nt].rearrange("c p t -> p c t"), cstt[:, :, 0:nt], reads=[b_cst2], writes=[])
                S.barrier()
        if stop_after == "B":
            break

        def epilogue_block(po, pbo, xtt, bxt, ntile, gate, b_gate, lng, lnb, b_ln, tmps, scbs, after_gm=None):
            for i in range(ntile):
                tmp, b_tmp = tmps[i]
                for hf in range(2):
                    S.op("dve", lambda: nc.vector.tensor_tensor(out=tmp[:, hf * 512:(hf + 1) * 512], in0=po[i][hf](0, 512),
                                                                 in1=gate[:, hf * 512:(hf + 1) * 512], op=ALU.mult),
                         reads=[pbo[i][hf], b_gate], writes=[b_tmp])
            if after_gm is not None:
                after_gm()
            for i in range(ntile):
                tmp, b_tmp = tmps[i]
                S.op("dve", lambda: nc.vector.scalar_tensor_tensor(out=tmp[:], in0=xtt[:, i, :], scalar=ALPHA, in1=tmp[:], op0=ALU.mult, op1=ALU.add),
                     reads=[bxt[i], b_tmp], writes=[b_tmp])
            for i in range(ntile):
                tmp, b_tmp = tmps[i]
                st6, b_st6, mv, b_mv, rsd, b_rsd = scbs[i]
                for hf in range(2):
                    S.op("dve", lambda: nc.vector.bn_stats(out=st6[:, hf, :], in_=tmp[:, hf * 512:(hf + 1) * 512]), reads=[b_tmp], writes=[b_st6])
            for i in range(ntile):
                st6, b_st6, mv, b_mv, rsd, b_rsd = scbs[i]
                S.op("dve", lambda: nc.vector.bn_aggr(out=mv[:], in_=st6[:]), reads=[b_st6], writes=[b_mv])
            for i in range(ntile):
                st6, b_st6, mv, b_mv, rsd, b_rsd = scbs[i]
                S.op("act", lambda: nc.scalar.activation(out=rsd[:, 0:1], in_=mv[:, 1:2], func=AF.Sqrt, bias=epsc[:, 0:1], scale=1.0),
                     reads=[b_mv, b_eps], writes=[b_rsd])
            for i in range(ntile):
                st6, b_st6, mv, b_mv, rsd, b_rsd = scbs[i]
                S.op("dve", lambda: nc.vector.reciprocal(out=rsd[:, 0:1], in_=rsd[:, 0:1]), reads=[b_rsd], writes=[b_rsd])
            for i in range(ntile):
                st6, b_st6, mv, b_mv, rsd, b_rsd = scbs[i]
                S.op("dve", lambda: nc.vector.scalar_tensor_tensor(out=rsd[:, 1:2], in0=mv[:, 0:1], scalar=-1.0, in1=rsd[:, 0:1], op0=ALU.mult, op1=ALU.mult),
                     reads=[b_mv, b_rsd], writes=[b_rsd])
            for i in range(ntile):
                tmp, b_tmp = tmps[i]
                st6, b_st6, mv, b_mv, rsd, b_rsd = scbs[i]
                S.op("act", lambda: nc.scalar.activation(out=tmp[:], in_=tmp[:], func=AF.Identity, scale=rsd[:, 0:1], bias=rsd[:, 1:2]),
                     reads=[b_tmp, b_rsd], writes=[b_tmp])
            for i in range(ntile):
                tmp, b_tmp = tmps[i]
                S.op("pool", lambda: nc.gpsimd.tensor_tensor(out=tmp[:], in0=tmp[:], in1=lng[:], op=ALU.mult), reads=[b_tmp, b_ln], writes=[b_tmp])
            for i in range(ntile):
                tmp, b_tmp = tmps[i]
                S.op("pool", lambda: nc.gpsimd.tensor_tensor(out=xtt[:, i, :], in0=tmp[:], in1=lnb[:], op=ALU.add), reads=[b_tmp, b_ln], writes=[bxt[i]])

        def mk_scbs(sc, n):
            r = []
            for i in range(n):
                a, ba = sb([128, 2, 6], F32, sc, "st6")
                b2, bb = sb([128, 2], F32, sc, "mv")
                c2, bc = sb([128, 2], F32, sc, "rsd")
                r.append((a, ba, b2, bb, c2, bc))
            return r

        with ExitStack() as sch:
            h2T, b_h2T = sb([128, 8, T + 128], BF16, sch, "h2T")
            with ExitStack() as sc:
                wout, b_wout = sb([128, 8, D], BF16, sc, "wout")
                cat = [sb([128, 8, 512], BF16, sc, "cat%d" % i) for i in range(1)]
                xt = [sb([128, 4, D], F32, sc, "xtC%d" % i)[0] for i in range(2)]
                bxts = [[Buf("xtC") for _ in range(4)] for _ in range(2)]
                tmps = [sb([128, D], F32, sc, "tmpC%d" % i) for i in range(4)]
                gate, b_gate = sb([128, D], F32, sc, "gateC")
                lng, b_ln = sb([128, D], F32, sc, "lngC")
                lnb, _ = sb([128, D], F32, sc, "lnbC")
                scbs = mk_scbs(sc, 4)
                S.op("dve", lambda: nc.vector.memset(h2T[:, :, T:T + 128], 0.0), writes=[b_h2T])
                S.dma("sp", lng[:], ln1_g[l:l + 1, :].partition_broadcast(128), writes=[b_ln])
                S.dma("sp", lnb[:], ln1_b[l:l + 1, :].partition_broadcast(128), writes=[b_ln])
                for k in range(8):
                    load_cvt(wout[:, k, :], b_wout, w_out[l, k * 128:(k + 1) * 128, :], [D])

                def transposes(pblk):
                    pbi, (pt0, pnt, pseg) = pblk
                    pxt, pbx = xt[pbi % 2], bxts[pbi % 2]
                    pn = pnt // 128
                    for c in range(8):
                        pf, pb_ = ph(c % 2)
                        S.group("pe", [TR(pf(i * 128, (i + 1) * 128), pxt[:, i, c * 128:(c + 1) * 128], ident_f[:]) for i in range(pn)],
                                reads=pbx[0:pn] + [b_identf], writes=[pb_])
                        S.op("act", lambda: nc.scalar.activation(out=h2T[:, c, pt0:pt0 + pnt], in_=pf(0, pnt), func=AF.Identity,
                                                                 scale=modT[:, 3, c, pseg:pseg + 1], bias=modT[:, 2, c, pseg:pseg + 1]),
                             reads=[pb_, b_modT], writes=[b_h2T])

                prev = None
                cur_seg = None
                for bi, (t0, nt, seg) in enumerate(blocks):
                    if last and seg == 0:
                        continue
                    ntile = nt // 128
                    if seg != cur_seg:
                        S.dma("sp", gate[:], gate_d[seg:seg + 1, :].partition_broadcast(128), writes=[b_gate])
                        cur_seg = seg
                    catt, b_cat = cat[0]
                    xtt, bxt = xt[bi % 2], bxts[bi % 2]
                    S.dma("sp", catt[:, :, 0:nt], catT_d[:, :, t0:t0 + nt].rearrange("c p t -> p c t"), writes=[b_cat])
                    S.dma("sp", xtt[:, 0:ntile, :], xsrc[t0:t0 + nt, :].rearrange("(i p) d -> p i d", p=128), writes=bxt[0:ntile])
                    po, pbo = [], []
                    for i in range(ntile):
                        pi, pbi_ = [], []
                        for hf in range(2):
                            pf, pb_ = ph(i * 2 + hf)
                            S.group("pe", [MM(pf(0, 512), catt[:, k, i * 128:(i + 1) * 128], wout[:, k, hf * 512:(hf + 1) * 512], k == 0, k == 7)
                                           for k in range(8)], reads=[b_cat, b_wout], writes=[pb_])
                            pi.append(pf)
                            pbi_.append(pb_)
                        po.append(pi)
                        pbo.append(pbi_)
                    pv_ = prev
                    epilogue_block(po, pbo, xtt, bxt, ntile, gate, b_gate, lng, lnb, b_ln, tmps, scbs,
                                   after_gm=(lambda: transposes(pv_)) if pv_ is not None else None)
                    S.dma("pool", xs_d[t0:t0 + nt, :].rearrange("(i p) d -> p i d", p=128), xtt[:, 0:ntile, :], reads=bxt[0:ntile], writes=[])
                    prev = (bi, (t0, nt, seg))
                transposes(prev)
                S.barrier()
            if stop_after == "C":
                break
            with ExitStack() as sc:
                wj = [sb([128, 8, 256], BF16, sc, "wj%d" % i) for i in range(3)]
                fd = [sb([128, 9, 128], BF16, sc, "fd%d" % i) for i in range(3)]
                gb = [[sb([128, 642], BF16, sc, "g%d_%d" % (i, q)) for q in range(3)] for i in range(2)]
                ge = [sb([128, 512], F32, sc, "ge%d" % i) for i in range(2)]
                ast = [sb([128, 512], BF16, sc, "ast%d" % i) for i in range(3)]
                for i in range(2):
                    for q in range(3):
                        S.op("dve", lambda: nc.vector.memset(gb[i][q][0][:], 0.0), writes=[gb[i][q][1]])

                def prep(j):
                    wjt, b_wj = wj[j % 3]
                    fdt, b_fd = fd[j % 3]
                    st_, bst = wst[wsti[0] % 2]
                    wsti[0] += 1
                    sv = st_[:, 0:2048].rearrange("p (k n) -> p k n", k=8)
                    S.dma("sp", sv[:, :, 0:128], w_up[l, :, j * 128:(j + 1) * 128].rearrange("(k p) n -> p k n", p=128), writes=[bst])
                    S.dma("sp", sv[:, :, 128:256], w_up[l, :, DFF + j * 128:DFF + (j + 1) * 128].rearrange("(k p) n -> p k n", p=128), writes=[bst])
                    S.op("pool", lambda: nc.gpsimd.tensor_copy(out=wjt[:], in_=sv), reads=[bst], writes=[b_wj])
                    for q in range(9):
                        S.op("pool", lambda: nc.gpsimd.tensor_scalar(out=fdt[:, q, :], in0=ident_f[:], scalar1=colT[:, j, R_FW + q:R_FW + q + 1], scalar2=None,
                                                                      op0=ALU.mult), reads=[b_identf, b_colT], writes=[b_fd])

                items = []
                for j in range(NJ):
                    first_of_j = True
                    for bi, (t0, nt, seg) in enumerate(blocks):
                        if last and seg == 0:
                            continue
                        items.append((j, t0, nt, seg, first_of_j))
                        first_of_j = False

                def stage1(n):
                    j, t0, nt, seg, first_of_j = items[n]
                    if first_of_j and j + 1 < NJ:
                        prep(j + 1)
                    wjt, b_wj = wj[j % 3]
                    (g0, b_g0), (gL, b_gL), (gR, b_gR) = gb[n % 2]
                    s0, s1 = seg_rng[seg]
                    pgt, pbg = PS[n % 2]
                    if seg == 0:
                        S.group("pe", [MM(pgt[:, 0:nt], wjt[:, k, 128:256], h2T[:, k, t0:t0 + nt], k == 0, k == 7) for k in range(8)],
                                reads=[b_wj, b_h2T], writes=[pbg])
                        S.op("act", lambda: nc.scalar.copy(out=g0[:, 1:1 + nt], in_=pgt[:, 0:nt]), reads=[pbg], writes=[b_g0])
                        S.op("dve", lambda: nc.vector.memset(g0[:, 1 + nt:2 + nt], 0.0), writes=[b_g0])
                    else:
                        base = t0 - 64
                        S.group("pe", [MM(pgt[:, 0:512], wjt[:, k, 128:256], h2T[:, k, base:base + 512], k == 0, k == 7) for k in range(8)] +
                                [MM(pgt[:, 512:640], wjt[:, k, 128:256], h2T[:, k, base + 512:base + 640], k == 0, k == 7) for k in range(8)],
                                reads=[b_wj, b_h2T], writes=[pbg])
                        S.op("act", lambda: nc.scalar.copy(out=g0[:, 1:513], in_=pgt[:, 0:512]), reads=[pbg], writes=[b_g0])
                        S.op("act", lambda: nc.scalar.copy(out=g0[:, 513:641], in_=pgt[:, 512:640]), reads=[pbg], writes=[b_g0])
                        if t0 == s0:
                            S.op("dve", lambda: nc.vector.memset(g0[:, 1:65], 0.0), writes=[b_g0])
                        if t0 + nt == s1:
                            S.op("dve", lambda: nc.vector.memset(g0[:, 577:641], 0.0), writes=[b_g0])
                        S.op("dve", lambda: nc.vector.tensor_copy(out=gL[:], in_=g0[:]), reads=[b_g0], writes=[b_gL])
                        S.op("dve", lambda: nc.vector.memset(gL[:, 64:641:64], 0.0), writes=[b_gL])
                        S.op("dve", lambda: nc.vector.tensor_copy(out=gR[:], in_=g0[:]), reads=[b_g0], writes=[b_gR])
                        S.op("dve", lambda: nc.vector.memset(gR[:, 1:641:64], 0.0), writes=[b_gR])

                def stage2(n):
                    j, t0, nt, seg, first_of_j = items[n]
                    wjt, b_wj = wj[j % 3]
                    fdt, b_fd = fd[j % 3]
                    (g0, b_g0), (gL, b_gL), (gR, b_gR) = gb[n % 2]
                    get, b_ge = ge[n % 2]
                    astt, b_ast = ast[n % 3]
                    pv, pbv = ph(6 + n % 2)
                    S.group("pe", [MM(pv(0, nt), wjt[:, k, 0:128], h2T[:, k, t0:t0 + nt], k == 0, k == 7) for k in range(8)],
                            reads=[b_wj, b_h2T], writes=[pbv])
                    pd, pbd_ = ph(4 + n % 2)
                    if seg == 0:
                        S.group("pe", [MM(pd(0, nt), fdt[:, 3 + kx, :], g0[:, kx:kx + nt], kx == 0, kx == 2) for kx in range(3)],
                                reads=[b_fd, b_g0], writes=[pbd_])
                    else:
                        fns = []
                        for ky in range(3):
                            for kx in range(3):
                                src = (gL, g0, gR)[kx]
                                off = 65 + 64 * (ky - 1) + (kx - 1)
                                fns.append(MM(pd(0, nt), fdt[:, ky * 3 + kx, :], src[:, off:off + nt], ky == 0 and kx == 0, ky == 2 and kx == 2))
                        S.group("pe", fns, reads=[b_fd, b_g0, b_gL, b_gR], writes=[pbd_])
                    S.op("act", lambda: nc.scalar.activation(out=get[:, 0:nt], in_=pd(0, nt), func=AF.Gelu_apprx_tanh,
                                                             bias=colT[:, j, R_FB:R_FB + 1], scale=1.0), reads=[pbd_, b_colT], writes=[b_ge])
                    S.op("dve", lambda: nc.vector.tensor_tensor(out=astt[:, 0:nt], in0=pv(0, nt), in1=get[:, 0:nt], op=ALU.mult),
                         reads=[pbv, b_ge], writes=[b_ast])
                    S.dma("pool", actT_d[j, :, t0:t0 + nt], astt[:, 0:nt], reads=[b_ast], writes=[])

                prep(0)
                stage1(0)
                for n in range(len(items)):
                    if n + 1 < len(items):
                        stage1(n + 1)
                    stage2(n)
                S.barrier()
        if stop_after == "D1":
            break

        with ExitStack() as sc:
            wdn, b_wdn = sb([128, NJ, D], BF16, sc, "wdn")
            actb = [sb([128, NJ, 512], BF16, sc, "actb%d" % i) for i in range(2)]
            xt = [sb([128, 4, D], F32, sc, "xtD%d" % i)[0] for i in range(2)]
            bxts = [[Buf("xtD") for _ in range(4)] for _ in range(2)]
            tmps = [sb([128, D], F32, sc, "tmpD%d" % i) for i in range(4)]
            gate, b_gate = sb([128, D], F32, sc, "gateD")
            lng, b_ln = sb([128, D], F32, sc, "lngD")
            lnb, _ = sb([128, D], F32, sc, "lnbD")
            scbs = mk_scbs(sc, 4)
            S.dma("sp", lng[:], ln2_g[l:l + 1, :].partition_broadcast(128), writes=[b_ln])
            S.dma("sp", lnb[:], ln2_b[l:l + 1, :].partition_broadcast(128), writes=[b_ln])
            for jp in range(0, NJ, 2):
                st_, bst_ = wst[(jp // 2) % 2]
                sv_ = st_[:, 0:2048].rearrange("p (a b) -> p a b", a=2)
                S.dma("sp" if (jp // 2) % 2 == 0 else "pool", sv_, w_down[l, jp * 128:(jp + 2) * 128, :].rearrange("(a p) n -> p a n", p=128), writes=[bst_])
                if (jp // 2) % 2 == 0:
                    S.op("pool", lambda: nc.gpsimd.tensor_copy(out=wdn[:, jp:jp + 2, :], in_=sv_), reads=[bst_], writes=[b_wdn])
                else:
                    S.op("dve", lambda: nc.vector.tensor_copy(out=wdn[:, jp:jp + 2, :], in_=sv_), reads=[bst_], writes=[b_wdn])
            cur_seg = None
            for bi, (t0, nt, seg) in enumerate(blocks):
                if last and seg == 0:
                    continue
                ntile = nt // 128
                if seg != cur_seg:
                    S.dma("sp", gate[:], gate_d[2 + seg:3 + seg, :].partition_broadcast(128), writes=[b_gate])
                    cur_seg = seg
                at, b_at = actb[bi % 2]
                xtt, bxt = xt[bi % 2], bxts[bi % 2]
                S.dma("sp", at[:, :, 0:nt], actT_d[:, :, t0:t0 + nt].rearrange("c p t -> p c t"), writes=[b_at])
                S.dma("sp", xtt[:, 0:ntile, :], xs_d[t0:t0 + nt, :].rearrange("(i p) d -> p i d", p=128), writes=bxt[0:ntile])
                po, pbo = [], []
                for i in range(ntile):
                    pi, pbi_ = [], []
                    for hf in range(2):
                        pf, pb_ = ph(i * 2 + hf)
                        S.group("pe", [MM(pf(0, 512), at[:, j, i * 128:(i + 1) * 128], wdn[:, j, hf * 512:(hf + 1) * 512], j == 0, j == NJ - 1)
                                       for j in range(NJ)], reads=[b_at, b_wdn], writes=[pb_])
                        pi.append(pf)
                        pbi_.append(pb_)
                    po.append(pi)
                    pbo.append(pbi_)
                epilogue_block(po, pbo, xtt, bxt, ntile, gate, b_gate, lng, lnb, b_ln, tmps, scbs)
                if last:
                    S.dma("pool", out_d[t0 - TC:t0 - TC + nt, :].rearrange("(i p) d -> p i d", p=128), xtt[:, 0:ntile, :], reads=bxt[0:ntile], writes=[])
                else:
                    S.dma("pool", xs_d[t0:t0 + nt, :].rearrange("(i p) d -> p i d", p=128), xtt[:, 0:ntile, :], reads=bxt[0:ntile], writes=[])
            S.barrier()

    S.barrier()
    es.close()
    return nc


_CACHE = {}


def kernel(**inputs):
    x = np.asarray(inputs["x"], np.float32)
    c = np.asarray(inputs["c"], np.float32)
    ctx = np.asarray(inputs["ctx"], np.float32)
    c_ctx = np.asarray(inputs["c_ctx"], np.float32)
    B = x.shape[0]
    if "nc" not in _CACHE:
        _CACHE["nc"] = build_program()
    nc = _CACHE["nc"]
    consts = make_consts()
    shared = {"consts": consts}
    for k in ("w_mod", "b_mod", "w_in", "conv_dw_w", "conv_dw_b", "conv_ln_g", "conv_ln_b", "conv_pw_w", "conv_pw_b", "ssm_conv_w",
              "ssm_conv_b", "ssm_norm_g", "pool_w", "pool_scale", "w_out", "ln1_g", "ln1_b", "w_up", "ffn_dw_b", "w_down", "ln2_g", "ln2_b"):
        shared[k] = np.ascontiguousarray(np.asarray(inputs[k], np.float32))
    shared["ffn_dw_w"] = np.ascontiguousarray(np.asarray(inputs["ffn_dw_w"], np.float32).reshape(DEPTH, 9, DFF))
    for k in ("ssm_dt_bias", "ssm_A_log", "ssm_D"):
        shared[k] = np.ascontiguousarray(np.asarray(inputs[k], np.float32).reshape(DEPTH, 16))
    in_maps = []
    for b in range(B):
        m = dict(shared)
        m["xs"] = np.ascontiguousarray(np.concatenate([ctx[b], x[b]], axis=0))
        m["cvec"] = np.ascontiguousarray(np.stack([c_ctx, c[b]], axis=0))
        in_maps.append(m)
    res = run_bass_kernel_spmd(nc, in_maps, core_ids=list(range(B)))
    return np.stack([np.asarray(r["out"], np.float32) for r in res.results], axis=0)
```

```python
import numpy as np
import concourse.bass as bass
import concourse.mybir as mybir
from concourse.bass_utils import run_bass_kernel_spmd
from contextlib import ExitStack

F32 = mybir.dt.float32
BF16 = mybir.dt.bfloat16
AF = mybir.ActivationFunctionType
ALU = mybir.AluOpType

D = 1024
TC = 256
TL = 4096
T = TC + TL
NT = T // 128
DEPTH = 4
DFF = 2816
NJ = 22
INC = 2320
ALPHA = float((2.0 * DEPTH) ** 0.25)
LN_EPS = 1e-5
RMS_EPS = 1e-5
NEG = -30000.0
POOL_W = (2, 4, 8, 16)

C_ID, C_UF, C_UB, C_MF, C_MB, C_PT = 0, 128, 256, 384, 512, 640
C_IW = C_PT + 2 * 16 * 128
C_EL = C_IW + 2
C_ER = C_EL + 16
NCONST = C_ER + 16

R_FW, R_FB, R_CW, R_CB, R_CG, R_CLB, R_PWB, R_PS, R_SW, R_SB, R_NG, R_SD = 0, 9, 10, 41, 42, 43, 44, 45, 46, 54, 56, 57
NROW = 64


def make_consts():
    c = np.zeros((128, NCONST), np.float32)
    k = np.arange(128)
    c[:, C_ID:C_ID + 128] = np.eye(128, dtype=np.float32)
    U = (k[:, None] <= k[None, :]).astype(np.float32)
    c[:, C_UF:C_UF + 128] = U
    c[:, C_UB:C_UB + 128] = U.T
    c[:, C_MF:C_MF + 128] = np.where(k[None, :] < k[:, None], NEG, 0.0)
    c[:, C_MB:C_MB + 128] = np.where(k[None, :] > k[:, None], NEG, 0.0)
    pt = np.zeros((128, 2, 16, 128), np.float32)
    iw = np.zeros((128, 2), np.float32)
    el = np.zeros((128, 2, 8), np.float32)
    er = np.zeros((128, 2, 8), np.float32)
    for ch in range(2):
        for p in range(128):
            w = POOL_W[2 * ch + p // 64]
            iw[p, ch] = 1.0 / w
            for d in range(-8, 8):
                if -(w // 2) <= d < w - w // 2:
                    pt[p, ch, d + 8, p] = 1.0
            for t in range(8):
                el[p, ch, t] = 1.0 / (min(t + (w - w // 2), 100000) - max(t - w // 2, 0))
                i = t
                er[p, ch, i] = 1.0 / (min(w - w // 2, 8 - i) + w // 2)
    c[:, C_PT:C_IW] = pt.reshape(128, -1)
    c[:, C_IW:C_EL] = iw
    c[:, C_EL:C_ER] = el.reshape(128, -1)
    c[:, C_ER:NCONST] = er.reshape(128, -1)
    return c


class Buf:
    __slots__ = ("name", "w", "r")

    def __init__(self, name=""):
        self.name = name
        self.w = None
        self.r = {}


class Sched:
    def __init__(self, nc, es):
        self.nc = nc
        self.eng = {"pe": nc.tensor, "act": nc.scalar, "dve": nc.vector, "pool": nc.gpsimd, "sp": nc.sync}
        self.sem = {}
        self.cnt = {}
        for k in self.eng:
            self.sem[k] = es.enter_context(nc.semaphore("s_" + k))
            self.cnt[k] = 0
        self.known = {k: {} for k in self.eng}
        self.dsem = {"sp": [], "pool": [], "act": []}
        self.drr = {"sp": 0, "pool": 0, "act": 0}
        for q, n in (("sp", 16), ("pool", 8)):
            for i in range(n):
                key = "d_%s%d" % (q, i)
                self.sem[key] = es.enter_context(nc.semaphore("s_" + key))
                self.cnt[key] = 0
                self.dsem[q].append(key)

    def _wait(self, e, key, val):
        if self.known[e].get(key, 0) >= val:
            return
        self.eng[e].wait_ge(self.sem[key], val)
        self.known[e][key] = val

    def _deps(self, e, reads, writes, skip_self=False):
        deps = {}
        for b in reads:
            if b.w is not None and deps.get(b.w[0], 0) < b.w[1]:
                deps[b.w[0]] = b.w[1]
        for b in writes:
            if b.w is not None and deps.get(b.w[0], 0) < b.w[1]:
                deps[b.w[0]] = b.w[1]
            for k, v in b.r.items():
                if deps.get(k, 0) < v:
                    deps[k] = v
        for k, v in deps.items():
            if skip_self and k == e:
                continue
            self._wait(e, k, v)

    def _commit(self, key, val, reads, writes):
        for b in writes:
            b.w = (key, val)
            b.r = {}
        for b in reads:
            if b.r.get(key, 0) < val:
                b.r[key] = val

    def op(self, e, fn, reads=(), writes=()):
        self._deps(e, reads, writes, skip_self=(e == "pe"))
        ins = fn()
        self.cnt[e] += 1
        ins.then_inc(self.sem[e], 1)
        self._commit(e, self.cnt[e], reads, writes)

    def group(self, e, fns, reads=(), writes=()):
        self._deps(e, reads, writes, skip_self=(e == "pe"))
        ins = None
        for fn in fns:
            ins = fn()
        self.cnt[e] += 1
        ins.then_inc(self.sem[e], 1)
        self._commit(e, self.cnt[e], reads, writes)

    def dma(self, q, out, in_, reads=(), writes=(), **kw):
        lst = self.dsem[q]
        key = lst[self.drr[q]]
        self.drr[q] = (self.drr[q] + 1) % len(lst)
        if self.cnt[key] > 0:
            self._wait(q, key, self.cnt[key])
        self._deps(q, reads, writes)
        ins = self.eng[q].dma_start(out=out, in_=in_, **kw)
        self.cnt[key] += 16
        ins.then_inc(self.sem[key], 16)
        self._commit(key, self.cnt[key], reads, writes)

    def barrier(self):
        for e in self.eng:
            for k, v in self.cnt.items():
                if k != e and v > 0:
                    self._wait(e, k, v)


def build_program(n_layers=DEPTH, stop_after=None, dbg=False):
    nc = bass.Bass("TRN2", target_bir_lowering=False)

    def din(name, shape, dt=F32):
        return nc.dram_tensor(name, list(shape), dt, kind="ExternalInput").ap()

    def dscr(name, shape, dt, kind="Internal"):
        return nc.dram_tensor(name, list(shape), dt, kind=kind).ap()

    xs_in = din("xs", [T, D])
    cvec = din("cvec", [2, D])
    consts_d = din("consts", [128, NCONST])
    w_mod = din("w_mod", [DEPTH, D, 6 * D])
    b_mod = din("b_mod", [DEPTH, 6 * D])
    w_in = din("w_in", [DEPTH, D, INC])
    conv_dw_w = din("conv_dw_w", [DEPTH, 31, 256])
    conv_dw_b = din("conv_dw_b", [DEPTH, 256])
    conv_ln_g = din("conv_ln_g", [DEPTH, 256])
    conv_ln_b = din("conv_ln_b", [DEPTH, 256])
    conv_pw_w = din("conv_pw_w", [DEPTH, 256, 256])
    conv_pw_b = din("conv_pw_b", [DEPTH, 256])
    ssm_conv_w = din("ssm_conv_w", [DEPTH, 2, 4, 1024])
    ssm_conv_b = din("ssm_conv_b", [DEPTH, 2, 1024])
    ssm_dt_bias = din("ssm_dt_bias", [DEPTH, 16])
    ssm_A_log = din("ssm_A_log", [DEPTH, 16])
    ssm_D = din("ssm_D", [DEPTH, 16])
    ssm_norm_g = din("ssm_norm_g", [DEPTH, 512])
    pool_w = din("pool_w", [DEPTH, 4, 64, 64])
    pool_scale = din("pool_scale", [DEPTH, 256])
    w_out = din("w_out", [DEPTH, D, D])
    ln1_g = din("ln1_g", [DEPTH, D])
    ln1_b = din("ln1_b", [DEPTH, D])
    w_up = din("w_up", [DEPTH, D, 2 * DFF])
    ffn_dw_w = din("ffn_dw_w", [DEPTH, 9, DFF])
    ffn_dw_b = din("ffn_dw_b", [DEPTH, DFF])
    w_down = din("w_down", [DEPTH, DFF, D])
    ln2_g = din("ln2_g", [DEPTH, D])
    ln2_b = din("ln2_b", [DEPTH, D])

    out_d = nc.dram_tensor("out", [TL, D], F32, kind="ExternalOutput").ap()
    kd = "ExternalOutput" if dbg else "Internal"
    xs_d = dscr("xs_d", [T, D], F32, kd)
    projT_d = dscr("projT_d", [18, 128, T], BF16, kd)
    catT_d = dscr("catT_d", [8, 128, T], BF16, kd)
    actT_d = dscr("actT_d", [NJ, 128, T], BF16, kd)
    gate_d = dscr("gate_d", [4, D], F32, kd)
    dta_d = dscr("dta_d", [128, NT * 32], F32, kd) if dbg else None

    es = ExitStack()
    S = Sched(nc, es)
    uid = [0]

    def sb(shape, dt, scope=None, name=None):
        uid[0] += 1
        t = (scope or es).enter_context(nc.sbuf_tensor("%s_%d" % (name or "t", uid[0]), list(shape), dt))
        return t, Buf(name or "t")

    def MM(out, lhsT, rhs, start, stop):
        return lambda: nc.tensor.matmul(out, lhsT=lhsT, rhs=rhs, start=start, stop=stop)

    def TR(out, in_, ident):
        return lambda: nc.tensor.transpose(out, in_, ident)

    PS = []
    for i in range(4):
        uid[0] += 1
        t = es.enter_context(nc.psum_tensor("ps2_%d" % i, [128, 1024], F32))
        PS.append((t, Buf("ps2_%d" % i)))
    PH = []
    for i in range(4):
        for h in range(2):
            PH.append((PS[i][0], h * 512, Buf("ph%d_%d" % (i, h))))

    def ph(i):
        t, o, b = PH[i]
        return (lambda a, c, t=t, o=o: t[:, o + a:o + c]), b

    ident_f, b_identf = sb([128, 128], F32, name="identf")
    ident_bf, b_identb = sb([128, 128], BF16, name="identb")
    ones_bf, b_ones = sb([128, 128], BF16, name="ones")
    onesm256, b_o256 = sb([128, 128], BF16, name="o256")
    onesm512, b_o512 = sb([128, 128], BF16, name="o512")
    U_bf, b_U = sb([128, 2, 128], BF16, name="U")
    nU_bf, b_nU = sb([128, 2, 128], BF16, name="nU")
    mask_bf, b_mask = sb([128, 2, 128], BF16, name="mask")
    ptap_bf, b_ptap = sb([128, 2, 16, 128], BF16, name="ptap")
    cst_s, b_cst = sb([128, 34], F32, name="csts")
    epsc, b_eps = sb([128, 4], F32, name="eps")
    dta, b_dta = sb([128, NT, 32], F32, name="dta")
    ahl, b_ahl = sb([128, NT, 2, 16], BF16, name="ahl")
    wst = [sb([128, 2816], F32, name="wst%d" % i) for i in range(2)]
    wsti = [0]
    modT, b_modT = sb([128, 4, 8, 2], F32, name="modT")
    colT, b_colT = sb([128, NJ, NROW], F32, name="colT")

    with ExitStack() as sc:
        cf, b_cf = sb([128, NCONST], F32, sc, "constf")
        S.dma("sp", cf[:], consts_d[:, :], writes=[b_cf])
        S.op("dve", lambda: nc.vector.tensor_copy(out=ident_f[:], in_=cf[:, C_ID:C_ID + 128]), reads=[b_cf], writes=[b_identf])
        S.op("dve", lambda: nc.vector.tensor_copy(out=ident_bf[:], in_=cf[:, C_ID:C_ID + 128]), reads=[b_cf], writes=[b_identb])
        S.op("dve", lambda: nc.vector.memset(ones_bf[:], 1.0), writes=[b_ones])
        S.op("dve", lambda: nc.vector.memset(onesm256[:], 1.0 / 256), writes=[b_o256])
        S.op("dve", lambda: nc.vector.memset(onesm512[:], 1.0 / 512), writes=[b_o512])
        S.op("dve", lambda: nc.vector.tensor_copy(out=U_bf[:].rearrange("p a b -> p (a b)"), in_=cf[:, C_UF:C_UF + 256]), reads=[b_cf], writes=[b_U])
        S.op("dve", lambda: nc.vector.tensor_scalar(out=nU_bf[:].rearrange("p a b -> p (a b)"), in0=cf[:, C_UF:C_UF + 256], scalar1=-1.0, scalar2=None, op0=ALU.mult), reads=[b_cf], writes=[b_nU])
        S.op("dve", lambda: nc.vector.tensor_copy(out=mask_bf[:].rearrange("p a b -> p (a b)"), in_=cf[:, C_MF:C_MF + 256]), reads=[b_cf], writes=[b_mask])
        S.op("dve", lambda: nc.vector.tensor_copy(out=ptap_bf[:].rearrange("p a b c -> p (a b c)"), in_=cf[:, C_PT:C_IW]), reads=[b_cf], writes=[b_ptap])
        S.op("dve", lambda: nc.vector.tensor_copy(out=cst_s[:], in_=cf[:, C_IW:NCONST]), reads=[b_cf], writes=[b_cst])
        S.op("dve", lambda: nc.vector.memset(epsc[:, 0:1], LN_EPS), writes=[b_eps])
        S.op("dve", lambda: nc.vector.memset(epsc[:, 1:2], 1.0), writes=[b_eps])
        S.op("dve", lambda: nc.vector.memset(epsc[:, 2:3], RMS_EPS), writes=[b_eps])
        S.op("dve", lambda: nc.vector.memset(epsc[:, 3:4], LN_EPS / (ALPHA * ALPHA)), writes=[b_eps])
        S.barrier()

    def load_cvt(dst_ap, dst_buf, src_ap, shape):
        st, bst = wst[wsti[0] % 2]
        wsti[0] += 1
        n = int(np.prod(shape))
        sv = st[:, 0:n]
        if len(shape) == 2:
            sv = sv.rearrange("p (a b) -> p a b", a=shape[0])
        S.dma("sp", sv, src_ap, writes=[bst])
        S.op("pool", lambda: nc.gpsimd.tensor_copy(out=dst_ap, in_=sv), reads=[bst], writes=[dst_buf])

    blocks = [(0, TC, 0)] + [(TC + 512 * i, 512, 1) for i in range(8)]
    seg_rng = [(0, TC), (TC, T)]

    for l in range(n_layers):
        last = (l == DEPTH - 1)
        xsrc = xs_in if l == 0 else xs_d

        with ExitStack() as sc:
            rows, b_rows = sb([NROW, 2816], F32, sc, "rows")
            S.op("dve", lambda: nc.vector.memset(rows[:], 0.0), writes=[b_rows])
            S.dma("sp", rows[R_FW:R_FW + 9, :], ffn_dw_w[l], writes=[b_rows])
            S.dma("sp", rows[R_FB:R_FB + 1, :], ffn_dw_b[l:l + 1, :], writes=[b_rows])
            S.dma("sp", rows[R_CW:R_CW + 31, 0:256], conv_dw_w[l], writes=[b_rows])
            S.dma("sp", rows[R_CB:R_CB + 1, 0:256], conv_dw_b[l:l + 1, :], writes=[b_rows])
            S.dma("sp", rows[R_CG:R_CG + 1, 0:256], conv_ln_g[l:l + 1, :], writes=[b_rows])
            S.dma("sp", rows[R_CLB:R_CLB + 1, 0:256], conv_ln_b[l:l + 1, :], writes=[b_rows])
            S.dma("sp", rows[R_PWB:R_PWB + 1, 0:256], conv_pw_b[l:l + 1, :], writes=[b_rows])
            S.dma("sp", rows[R_PS:R_PS + 1, 0:256], pool_scale[l:l + 1, :], writes=[b_rows])
            S.dma("sp", rows[R_SW:R_SW + 8, 0:1024], ssm_conv_w[l].rearrange("a k c -> (a k) c"), writes=[b_rows])
            S.dma("sp", rows[R_SB:R_SB + 2, 0:1024], ssm_conv_b[l], writes=[b_rows])
            S.dma("sp", rows[R_NG:R_NG + 1, 0:512], ssm_norm_g[l:l + 1, :], writes=[b_rows])
            pf, pb_ = ph(0)
            for j0 in range(0, NJ, 8):
                nj = min(8, NJ - j0)
                S.group("pe", [TR(pf(jj * 64, jj * 64 + 64), rows[:, (j0 + jj) * 128:(j0 + jj + 1) * 128], ident_f[0:NROW, 0:NROW])
                               for jj in range(nj)], reads=[b_rows, b_identf], writes=[pb_])
                S.op("dve", lambda: nc.vector.tensor_copy(out=colT[:, j0:j0 + nj, :].rearrange("p a b -> p (a b)"), in_=pf(0, nj * 64)),
                     reads=[pb_], writes=[b_colT])
            S.op("dve", lambda: nc.vector.tensor_scalar(out=colT[:, 0:8, R_SW:R_SW + 8], in0=colT[:, 0:8, R_SW:R_SW + 8], scalar1=0.5, scalar2=None, op0=ALU.mult),
                 reads=[b_colT], writes=[b_colT])

            sT, b_sT = sb([128, 2, 8], F32, sc, "sT")
            bmT, b_bmT = sb([128, 48], F32, sc, "bmT")
            brow, b_brow = sb([1, 6 * D], F32, sc, "brow")
            wm = [sb([128, 8, 512], F32, sc, "wm%d" % i) for i in range(2)]
            grow, b_grow = sb([128, 512], F32, sc, "grow")
            mrow, b_mrow = sb([2, 512], F32, sc, "mrow")
            onesf, b_onesf = sb([1, 128], F32, sc, "onesf")
            S.op("dve", lambda: nc.vector.memset(onesf[:], 1.0), writes=[b_onesf])
            for s_ in range(2):
                S.dma("sp", sT[:, s_, :], cvec[s_].rearrange("(k p) -> p k", p=128), writes=[b_sT], allow_slow_non_contiguous=True)
            S.dma("sp", bmT[:], b_mod[l].rearrange("(c p) -> p c", p=128), writes=[b_bmT], allow_slow_non_contiguous=True)
            S.dma("sp", brow[:], b_mod[l:l + 1, :], writes=[b_brow])
            S.op("act", lambda: nc.scalar.activation(out=sT[:], in_=sT[:], func=AF.Silu), reads=[b_sT], writes=[b_sT])
            for blk in range(12):
                wmt, b_wm = wm[blk % 2]
                S.dma("sp" if blk % 2 == 0 else "pool", wmt[:], w_mod[l, :, blk * 512:(blk + 1) * 512].rearrange("(k p) n -> p k n", p=128), writes=[b_wm])
                which = blk // 2
                if which in (2, 5):
                    gi = 0 if which == 2 else 1
                    for s in range(2):
                        pf, pb_ = ph(1 + s)
                        fns = [MM(pf(0, 512), sT[:, s, k:k + 1].to_broadcast([128, 128]), wmt[:, k, :], k == 0, False) for k in range(8)]
                        fns.append(MM(pf(0, 512), onesf[0:1, :], brow[0:1, blk * 512:(blk + 1) * 512], False, True))
                        S.group("pe", fns, reads=[b_sT, b_wm, b_onesf, b_brow], writes=[pb_])
                        S.op("dve", lambda: nc.vector.tensor_copy(out=grow[:], in_=pf(0, 512)), reads=[pb_], writes=[b_grow])
                        half = blk % 2
                        S.dma("sp", gate_d[gi * 2 + s:gi * 2 + s + 1, half * 512:(half + 1) * 512], grow[0:1, :], reads=[b_grow], writes=[])
                else:
                    mi = {0: 0, 1: 1, 3: 2, 4: 3}[which]
                    pr_, pbr = ph(3)
                    S.group("pe", [MM(pr_(0, 512)[0:2, :], sT[:, :, k], wmt[:, k, :], k == 0, k == 7) for k in range(8)],
                            reads=[b_sT, b_wm], writes=[pbr])
                    S.op("act", lambda: nc.scalar.copy(out=mrow[0:2, :], in_=pr_(0, 512)[0:2, :]), reads=[pbr], writes=[b_mrow])
                    pf, pb_ = ph(4)
                    S.group("pe", [TR(pf(oc * 2, oc * 2 + 2), mrow[0:2, oc * 128:(oc + 1) * 128], ident_f[0:2, 0:2]) for oc in range(4)],
                            reads=[b_mrow, b_identf], writes=[pb_])
                    c0 = (blk % 2) * 4
                    for oc in range(4):
                        gc = blk * 4 + oc
                        S.op("dve", lambda: nc.vector.tensor_scalar(out=modT[:, mi, c0 + oc, :], in0=pf(oc * 2, oc * 2 + 2), scalar1=bmT[:, gc:gc + 1],
                                                                     scalar2=None, op0=ALU.add), reads=[pb_, b_bmT], writes=[b_modT])
                        if mi in (1, 3):
                            S.op("dve", lambda: nc.vector.tensor_scalar(out=modT[:, mi, c0 + oc, :], in0=modT[:, mi, c0 + oc, :], scalar1=1.0,
                                                                         scalar2=None, op0=ALU.add), reads=[b_modT], writes=[b_modT])
            S.barrier()
        if stop_after == "mod":
            break

        with ExitStack() as sc:
            win, b_win = sb([128, 8, INC], BF16, sc, "win")
            xt = [sb([128, 4, D], F32, sc, "xtA%d" % i) for i in range(2)]
            hT = [sb([128, 8, 512], BF16, sc, "hT%d" % i) for i in range(2)]
            stg = [sb([128, 18, 512], BF16, sc, "stgA%d" % i) for i in range(2)]
            dtb, b_dtb = sb([128, 16], F32, sc, "dtb")
            expA, b_expA = sb([128, 16], F32, sc, "expA")
            tmp16 = [sb([128, 16], F32, sc, "tmp16_%d" % i) for i in range(2)]
            S.dma("sp", dtb[:], ssm_dt_bias[l:l + 1, :].partition_broadcast(128), writes=[b_dtb])
            S.dma("sp", expA[:], ssm_A_log[l:l + 1, :].partition_broadcast(128), writes=[b_expA])
            S.op("act", lambda: nc.scalar.activation(out=expA[:], in_=expA[:], func=AF.Exp), reads=[b_expA], writes=[b_expA])
            for k in range(8):
                load_cvt(win[:, k, :], b_win, w_in[l, k * 128:(k + 1) * 128, :], [INC])
            for bi, (t0, nt, seg) in enumerate(blocks):
                ntile = nt // 128
                xtt, b_xt = xt[bi % 2]
                hTt, b_hT = hT[bi % 2]
                stt, b_st = stg[bi % 2]
                S.dma("sp", xtt[:, 0:ntile, :], xsrc[t0:t0 + nt, :].rearrange("(i p) d -> p i d", p=128), writes=[b_xt])
                for c in range(8):
                    pf, pb_ = ph(c % 2)
                    S.group("pe", [TR(pf(i * 128, (i + 1) * 128), xtt[:, i, c * 128:(c + 1) * 128], ident_f[:]) for i in range(ntile)],
                            reads=[b_xt, b_identf], writes=[pb_])
                    S.op("act", lambda: nc.scalar.activation(out=hTt[:, c, 0:nt], in_=pf(0, nt), func=AF.Identity,
                                                             scale=modT[:, 1, c, seg:seg + 1], bias=modT[:, 0, c, seg:seg + 1]),
                         reads=[pb_, b_modT], writes=[b_hT])
                for oc in range(18):
                    col0 = oc * 128 if oc < 16 else 2064 + (oc - 16) * 128
                    pf, pb_ = ph(2 + oc % 4)
                    S.group("pe", [MM(pf(0, nt), win[:, k, col0:col0 + 128], hTt[:, k, 0:nt], k == 0, k == 7) for k in range(8)],
                            reads=[b_win, b_hT], writes=[pb_])
                    if oc % 2 == 0:
                        S.op("act", lambda: nc.scalar.copy(out=stt[:, oc, 0:nt], in_=pf(0, nt)), reads=[pb_], writes=[b_st])
                    else:
                        S.op("dve", lambda: nc.vector.tensor_copy(out=stt[:, oc, 0:nt], in_=pf(0, nt)), reads=[pb_], writes=[b_st])
                S.dma("pool", projT_d[:, :, t0:t0 + nt].rearrange("c p t -> p c t"), stt[:, :, 0:nt], reads=[b_st], writes=[])
                for i in range(ntile):
                    ti = t0 // 128 + i
                    pf, pb_ = ph(6 + i % 2)
                    S.group("pe", [MM(pf(0, 16), hTt[:, k, i * 128:(i + 1) * 128], win[:, k, 2048:2064], k == 0, k == 7) for k in range(8)],
                            reads=[b_win, b_hT], writes=[pb_])
                    t16, b_t16 = tmp16[i % 2]
                    S.op("dve", lambda: nc.vector.tensor_tensor(out=t16[:], in0=pf(0, 16), in1=dtb[:], op=ALU.add), reads=[pb_, b_dtb], writes=[b_t16])
                    S.op("act", lambda: nc.scalar.activation(out=dta[:, ti, 0:16], in_=t16[:], func=AF.Exp), reads=[b_t16], writes=[b_dta])
            S.op("act", lambda: nc.scalar.activation(out=dta[:, :, 0:16], in_=dta[:, :, 0:16], func=AF.Ln, bias=epsc[:, 1:2], scale=1.0),
                 reads=[b_dta, b_eps], writes=[b_dta])
            S.op("dve", lambda: nc.vector.scalar_tensor_tensor(out=dta[:, :, 16:32], in0=dta[:, :, 0:16], scalar=-1.0,
                                                               in1=expA[:].unsqueeze(1).to_broadcast([128, NT, 16]), op0=ALU.mult, op1=ALU.mult),
                 reads=[b_dta, b_expA], writes=[b_dta])
            alo, b_alo = sb([128, NT, 16], F32, sc, "alo")
            S.op("dve", lambda: nc.vector.tensor_copy(out=ahl[:, :, 0, :], in_=dta[:, :, 16:32]), reads=[b_dta], writes=[b_ahl])
            S.op("dve", lambda: nc.vector.tensor_tensor(out=alo[:], in0=dta[:, :, 16:32], in1=ahl[:, :, 0, :], op=ALU.subtract),
                 reads=[b_dta, b_ahl], writes=[b_alo])
            S.op("dve", lambda: nc.vector.tensor_copy(out=ahl[:, :, 1, :], in_=alo[:]), reads=[b_alo], writes=[b_ahl])
            if dbg:
                S.dma("sp", dta_d[:, :], dta[:].rearrange("p a b -> p (a b)"), reads=[b_dta], writes=[])
            S.barrier()
        if stop_after == "A":
            break

        with ExitStack() as sc:
            cdiag, b_cdiag = sb([128, 2, 31, 128], BF16, sc, "cdiag")
            pw, b_pw = sb([128, 2, 256], BF16, sc, "pw")
            pbd, b_pbd = sb([128, 2, 128], BF16, sc, "pbd")
            pbdf, b_pbdf = sb([128, 2, 128], F32, sc, "pbdf")
            cin = [sb([128, 6, 542], BF16, sc, "cin%d" % i) for i in range(2)]
            sig, b_sig = sb([128, 2, 542], BF16, sc, "sig")
            ug, b_ug = sb([128, 2, 542], BF16, sc, "ug")
            uc, b_uc = sb([128, 2, 512], F32, sc, "uc")
            ucb, b_ucb = sb([128, 2, 512], BF16, sc, "ucb")
            usq, b_usq = sb([128, 2, 512], BF16, sc, "usq")
            m2, b_m2 = sb([128, 512], F32, sc, "m2")
            rstd, b_rstd = sb([128, 512], F32, sc, "rstd")
            tn, b_tn = sb([128, 2, 512], F32, sc, "tn")
            un, b_un = sb([128, 2, 512], BF16, sc, "un")
            rp, b_rp = sb([128, 2, 512], BF16, sc, "rp")
            ed, b_ed = sb([128, 8], F32, sc, "ed")
            cst = [sb([128, 4, 512], BF16, sc, "cst%d" % i) for i in range(2)]
            for c in range(2):
                for k in range(31):
                    S.op("dve", lambda: nc.vector.tensor_scalar(out=cdiag[:, c, k, :], in0=ident_f[:], scalar1=colT[:, c, R_CW + k:R_CW + k + 1],
                                                                 scalar2=None, op0=ALU.mult), reads=[b_identf, b_colT], writes=[b_cdiag])
            for k in range(2):
                load_cvt(pw[:, k, :], b_pw, conv_pw_w[l, k * 128:(k + 1) * 128, :], [256])
            S.op("dve", lambda: nc.vector.memset(pbdf[:], 0.0), writes=[b_pbdf])
            for g in range(4):
                S.dma("sp", pbdf[(g % 2) * 64:(g % 2) * 64 + 64, g // 2, (g % 2) * 64:(g % 2) * 64 + 64], pool_w[l, g], writes=[b_pbdf])
            S.op("dve", lambda: nc.vector.tensor_copy(out=pbd[:], in_=pbdf[:]), reads=[b_pbdf], writes=[b_pbd])
            for bi, (t0, nt, seg) in enumerate(blocks):
                s0, s1 = seg_rng[seg]
                cint, b_cin = cin[bi % 2]
                cstt, b_cst2 = cst[bi % 2]
                lo = max(t0 - 15, s0)
                hi = min(t0 + nt + 15, s1)
                o0 = lo - (t0 - 15)
                if lo > t0 - 15:
                    S.op("dve", lambda: nc.vector.memset(cint[:, :, 0:15], 0.0), writes=[b_cin])
                if hi < t0 + nt + 15:
                    S.op("dve", lambda: nc.vector.memset(cint[:, :, nt + 15:nt + 30], 0.0), writes=[b_cin])
                S.dma("sp", cint[:, 0:4, o0:o0 + hi - lo], projT_d[0:4, :, lo:hi].rearrange("c p t -> p c t"), writes=[b_cin])
                S.dma("sp", cint[:, 4:6, o0:o0 + hi - lo], projT_d[16:18, :, lo:hi].rearrange("c p t -> p c t"), writes=[b_cin])
                W = nt + 30
                S.op("act", lambda: nc.scalar.activation(out=sig[:, :, 0:W], in_=cint[:, 2:4, 0:W], func=AF.Sigmoid), reads=[b_cin], writes=[b_sig])
                S.op("dve", lambda: nc.vector.tensor_tensor(out=ug[:, :, 0:W], in0=cint[:, 0:2, 0:W], in1=sig[:, :, 0:W], op=ALU.mult),
                     reads=[b_cin, b_sig], writes=[b_ug])
                for c in range(2):
                    pf, pb_ = ph(c)
                    S.group("pe", [MM(pf(0, nt), cdiag[:, c, k, :], ug[:, c, k:k + nt], k == 0, k == 30) for k in range(31)],
                            reads=[b_cdiag, b_ug], writes=[pb_])
                    S.op("act", lambda: nc.scalar.activation(out=uc[:, c, 0:nt], in_=pf(0, nt), func=AF.Identity, bias=colT[:, c, R_CB:R_CB + 1], scale=1.0),
                         reads=[pb_, b_colT], writes=[b_uc])
                S.op("dve", lambda: nc.vector.tensor_copy(out=ucb[:, :, 0:nt], in_=uc[:, :, 0:nt]), reads=[b_uc], writes=[b_ucb])
                S.op("act", lambda: nc.scalar.activation(out=usq[:, :, 0:nt], in_=uc[:, :, 0:nt], func=AF.Square), reads=[b_uc], writes=[b_usq])
                pm, pbm = ph(2)
                pq, pbq = ph(3)
                S.group("pe", [MM(pm(0, nt), onesm256[:], ucb[:, c, 0:nt], c == 0, c == 1) for c in range(2)], reads=[b_o256, b_ucb], writes=[pbm])
                S.group("pe", [MM(pq(0, nt), onesm256[:], usq[:, c, 0:nt], c == 0, c == 1) for c in range(2)], reads=[b_o256, b_usq], writes=[pbq])
                S.op("act", lambda: nc.scalar.activation(out=m2[:, 0:nt], in_=pm(0, nt), func=AF.Square), reads=[pbm], writes=[b_m2])
                S.op("dve", lambda: nc.vector.tensor_tensor(out=m2[:, 0:nt], in0=pq(0, nt), in1=m2[:, 0:nt], op=ALU.subtract), reads=[pbq, b_m2], writes=[b_m2])
                S.op("act", lambda: nc.scalar.activation(out=rstd[:, 0:nt], in_=m2[:, 0:nt], func=AF.Sqrt, bias=epsc[:, 0:1], scale=1.0),
                     reads=[b_m2, b_eps], writes=[b_rstd])
                S.op("dve", lambda: nc.vector.reciprocal(out=rstd[:, 0:nt], in_=rstd[:, 0:nt]), reads=[b_rstd], writes=[b_rstd])
                for c in range(2):
                    S.op("dve", lambda: nc.vector.tensor_tensor(out=tn[:, c, 0:nt], in0=uc[:, c, 0:nt], in1=pm(0, nt), op=ALU.subtract),
                         reads=[b_uc, pbm], writes=[b_tn])
                    S.op("dve", lambda: nc.vector.tensor_tensor(out=tn[:, c, 0:nt], in0=tn[:, c, 0:nt], in1=rstd[:, 0:nt], op=ALU.mult),
                         reads=[b_tn, b_rstd], writes=[b_tn])
                    S.op("act", lambda: nc.scalar.activation(out=un[:, c, 0:nt], in_=tn[:, c, 0:nt], func=AF.Silu,
                                                             scale=colT[:, c, R_CG:R_CG + 1], bias=colT[:, c, R_CLB:R_CLB + 1]),
                         reads=[b_tn, b_colT], writes=[b_un])
                for oc in range(2):
                    pf, pb_ = ph(4 + oc)
                    S.group("pe", [MM(pf(0, nt), pw[:, c, oc * 128:(oc + 1) * 128], un[:, c, 0:nt], c == 0, c == 1) for c in range(2)],
                            reads=[b_pw, b_un], writes=[pb_])
                    S.op("act", lambda: nc.scalar.activation(out=cstt[:, oc, 0:nt], in_=pf(0, nt), func=AF.Identity,
                                                             bias=colT[:, oc, R_PWB:R_PWB + 1], scale=1.0), reads=[pb_, b_colT], writes=[b_cst2])
                for c in range(2):
                    pf, pb_ = ph(6 + c)
                    S.group("pe", [MM(pf(0, nt), ptap_bf[:, c, d, :], cint[:, 4 + c, 7 + d:7 + d + nt], d == 0, d == 15) for d in range(16)],
                            reads=[b_ptap, b_cin], writes=[pb_])
                    S.op("dve", lambda: nc.vector.scalar_tensor_tensor(out=rp[:, c, 0:nt], in0=pf(0, nt), scalar=cst_s[:, c:c + 1],
                                                                       in1=cint[:, 4 + c, 15:15 + nt], op0=ALU.mult, op1=ALU.subtract),
                         reads=[pb_, b_cst, b_cin], writes=[b_rp])
                    if t0 == s0:
                        S.op("dve", lambda: nc.vector.tensor_tensor(out=ed[:], in0=pf(0, 8), in1=cst_s[:, 2 + c * 8:2 + c * 8 + 8], op=ALU.mult),
                             reads=[pb_, b_cst], writes=[b_ed])
                        S.op("dve", lambda: nc.vector.tensor_tensor(out=rp[:, c, 0:8], in0=ed[:], in1=cint[:, 4 + c, 15:23], op=ALU.subtract),
                             reads=[b_ed, b_cin], writes=[b_rp])
                    if t0 + nt == s1:
                        S.op("dve", lambda: nc.vector.tensor_tensor(out=ed[:], in0=pf(nt - 8, nt), in1=cst_s[:, 18 + c * 8:18 + c * 8 + 8], op=ALU.mult),
                             reads=[pb_, b_cst], writes=[b_ed])
                        S.op("dve", lambda: nc.vector.tensor_tensor(out=rp[:, c, nt - 8:nt], in0=ed[:], in1=cint[:, 4 + c, 15 + nt - 8:15 + nt], op=ALU.subtract),
                             reads=[b_ed, b_cin], writes=[b_rp])
                for c in range(2):
                    pf, pb_ = ph(c)
                    S.group("pe", [MM(pf(0, nt), pbd[:, c, :], rp[:, c, 0:nt], True, True)], reads=[b_pbd, b_rp], writes=[pb_])
                    S.op("act", lambda: nc.scalar.activation(out=cstt[:, 2 + c, 0:nt], in_=pf(0, nt), func=AF.Identity, scale=colT[:, c, R_PS:R_PS + 1]),
                         reads=[pb_, b_colT], writes=[b_cst2])
                S.dma("pool", catT_d[0:2, :, t0:t0 + nt].rearrange("c p t -> p c t"), cstt[:, 0:2, 0:nt], reads=[b_cst2], writes=[])
                S.dma("pool", catT_d[6:8, :, t0:t0 + nt].rearrange("c p t -> p c t"), cstt[:, 2:4, 0:nt], reads=[b_cst2], writes=[])
            S.barrier()
        if stop_after == "B1":
            break

        with ExitStack() as scy:
            ybuf, b_y = sb([128, 4, T], F32, scy, "ybuf")
            with ExitStack() as sc:
                sdiag, b_sdiag = sb([128, 2, 8, 4, 128], BF16, sc, "sdiag")
                ddiag, b_ddiag = sb([128, 16, 128], BF16, sc, "ddiag")
                dsk, b_dsk = sb([128, 16], F32, sc, "dsk")
                cbrow, b_cbrow = sb([1, 2, 1024], BF16, sc, "cbrow")
                cbrowf, b_cbrowf = sb([1, 2, 1024], F32, sc, "cbrowf")
                S.dma("sp", dsk[:], ssm_D[l:l + 1, :].partition_broadcast(128), writes=[b_dsk])
                S.dma("sp", cbrowf[:], ssm_conv_b[l:l + 1, :, :], writes=[b_cbrowf])
                S.op("dve", lambda: nc.vector.tensor_scalar(out=cbrow[:], in0=cbrowf[:], scalar1=0.5, scalar2=None, op0=ALU.mult),
                     reads=[b_cbrowf], writes=[b_cbrow])
                for d in range(2):
                    for c in range(8):
                        for k in range(4):
                            S.op("dve", lambda: nc.vector.tensor_scalar(out=sdiag[:, d, c, k, :], in0=ident_f[:],
                                                                         scalar1=colT[:, c, R_SW + d * 4 + k:R_SW + d * 4 + k + 1], scalar2=None, op0=ALU.mult),
                                 reads=[b_identf, b_colT], writes=[b_sdiag])
                    for h in range(8):
                        S.op("dve", lambda: nc.vector.tensor_scalar(out=ddiag[:, d * 8 + h, :], in0=ident_f[:], scalar1=dsk[:, d * 8 + h:d * 8 + h + 1],
                                                                     scalar2=None, op0=ALU.mult), reads=[b_identf, b_dsk], writes=[b_ddiag])
                dbufs = []
                for d in range(2):
                    B_ = {}
                    B_["xbc"] = [sb([128, 8, 131], BF16, sc, "xbc%d" % d) for _ in range(2)]
                    B_["th"] = [sb([128, 512], F32, sc, "th%d_%d" % (d, i)) for i in range(3)]
                    B_["xs"] = sb([128, 512], BF16, sc, "xs%d" % d)
                    B_["xdt"] = sb([128, 512], BF16, sc, "xdt%d" % d)
                    B_["xdtw"] = sb([128, 512], BF16, sc, "xdtw%d" % d)
                    B_["btok"] = sb([128, 256], BF16, sc, "btok%d" % d)
                    B_["bct"] = sb([128, 4, 128], BF16, sc, "bct%d" % d)
                    B_["eX"] = sb([128, 1024], BF16, sc, "eX%d" % d)
                    B_["L"] = sb([128, 1024], BF16, sc, "L%d" % d)
                    B_["M"] = sb([128, 1024], BF16, sc, "M%d" % d)
                    B_["Cw"] = sb([128, 1024], BF16, sc, "Cw%d" % d)
                    B_["S32"] = sb([128, 512], F32, sc, "S32%d" % d)
                    B_["Sbf"] = sb([128, 512], BF16, sc, "Sbf%d" % d)
                    dbufs.append(B_)
                    S.op("dve", lambda: nc.vector.memset(B_["S32"][0][:], 0.0), writes=[B_["S32"][1]])
                    S.op("dve", lambda: nc.vector.memset(B_["Sbf"][0][:], 0.0), writes=[B_["Sbf"][1]])
                order = [list(range(NT)), [1, 0] + list(range(NT - 1, 1, -1))]
                visit = [{x: i for i, x in enumerate(order[d])} for d in range(2)]
                qs = [(step, d) for step in range(NT) for d in range(2)]

                def ctxq(q):
                    step, d = qs[q]
                    X = order[d][step]
                    seg = 0 if X < 2 else 1
                    return step, d, X, seg, X * 128, dbufs[d]

                def stage_ab(q):
                    step, d, X, seg, tt0, B_ = ctxq(q)
                    s0, s1 = seg_rng[seg]
                    xbc, b_xbc = B_["xbc"][step % 2]
                    if d == 0:
                        lo, hi, base = max(tt0 - 3, s0), tt0 + 128, tt0 - 3
                        if lo > tt0 - 3:
                            S.op("dve", lambda: nc.vector.memset(xbc[:, :, 0:3], 0.0), writes=[b_xbc])
                        offs = [0, 1, 2, 3]
                    else:
                        lo, hi, base = tt0, min(tt0 + 131, s1), tt0
                        if hi < tt0 + 131:
                            S.op("dve", lambda: nc.vector.memset(xbc[:, :, 128:131], 0.0), writes=[b_xbc])
                        offs = [3, 2, 1, 0]
                    S.dma("sp", xbc[:, :, lo - base:hi - base], projT_d[8:16, :, lo:hi].rearrange("c p t -> p c t"), writes=[b_xbc])
                    pxs, pb_xs = ph(0)
                    fns = []
                    for c in range(4):
                        for k in range(4):
                            fns.append(MM(pxs(c * 128, (c + 1) * 128), xbc[:, c, offs[k]:offs[k] + 128], sdiag[:, d, c, k, :], k == 0, False))
                        fns.append(MM(pxs(c * 128, (c + 1) * 128), ones_bf[0:1, :], cbrow[0:1, d, c * 128:(c + 1) * 128], False, True))
                    S.group("pe", fns, reads=[b_xbc, b_sdiag, b_ones, b_cbrow], writes=[pb_xs])
                    xs, b_xs = B_["xs"]
                    th0, b_th0 = B_["th"][0]
                    S.op("act", lambda: nc.scalar.activation(out=th0[:], in_=pxs(0, 512), func=AF.Tanh), reads=[pb_xs], writes=[b_th0])
                    S.op("dve", lambda: nc.vector.scalar_tensor_tensor(out=xs[:], in0=th0[:], scalar=1.0, in1=pxs(0, 512), op0=ALU.add, op1=ALU.mult),
                         reads=[b_th0, pb_xs], writes=[b_xs])
                    pbt, pb_bt = ph(1)
                    fns = []
                    for c in range(4, 6):
                        o = (c - 4) * 128
                        for k in range(4):
                            fns.append(MM(pbt(o, o + 128), xbc[:, c, offs[k]:offs[k] + 128], sdiag[:, d, c, k, :], k == 0, False))
                        fns.append(MM(pbt(o, o + 128), ones_bf[0:1, :], cbrow[0:1, d, c * 128:(c + 1) * 128], False, True))
                    S.group("pe", fns, reads=[b_xbc, b_sdiag, b_ones, b_cbrow], writes=[pb_bt])
                    btok, b_btok = B_["btok"]
                    th1, b_th1 = B_["th"][1]
                    S.op("act", lambda: nc.scalar.activation(out=th1[:, 0:256], in_=pbt(0, 256), func=AF.Tanh), reads=[pb_bt], writes=[b_th1])
                    S.op("dve", lambda: nc.vector.scalar_tensor_tensor(out=btok[:], in0=th1[:, 0:256], scalar=1.0, in1=pbt(0, 256), op0=ALU.add, op1=ALU.mult),
                         reads=[b_th1, pb_bt], writes=[b_btok])
                    pct, pb_ct = ph(2)
                    fns = []
                    for c in range(4, 8):
                        o = (c - 4) * 128
                        for k in range(4):
                            fns.append(MM(pct(o, o + 128), sdiag[:, d, c, k, :], xbc[:, c, offs[k]:offs[k] + 128], k == 0, False))
                        fns.append(MM(pct(o, o + 128), cbrow[0:1, d, c * 128:(c + 1) * 128], ones_bf[0:1, :], False, True))
                    S.group("pe", fns, reads=[b_xbc, b_sdiag, b_ones, b_cbrow], writes=[pb_ct])
                    bct, b_bct = B_["bct"]
                    th2, b_th2 = B_["th"][2]
                    S.op("act", lambda: nc.scalar.activation(out=th2[:], in_=pct(0, 512), func=AF.Tanh), reads=[pb_ct], writes=[b_th2])
                    S.op("dve", lambda: nc.vector.scalar_tensor_tensor(out=bct[:].rearrange("p a b -> p (a b)"), in0=th2[:], scalar=1.0, in1=pct(0, 512),
                                                                       op0=ALU.add, op1=ALU.mult), reads=[b_th2, pb_ct], writes=[b_bct])
                    pDX, pb_DX = PS[2]
                    fns = []
                    for h in range(8):
                        for part in range(2):
                            fns.append(MM(pDX[:, h * 128:(h + 1) * 128], ahl[:, X, part, d * 8 + h:d * 8 + h + 1].to_broadcast([128, 128]),
                                          U_bf[:, d, :], part == 0, part == 1))
                    S.group("pe", fns, reads=[b_ahl, b_U], writes=[pb_DX])
                    eX, b_eX = B_["eX"]
                    for hf in range(2):
                        S.op("act", lambda: nc.scalar.activation(out=eX[:, hf * 512:(hf + 1) * 512], in_=pDX[:, hf * 512:(hf + 1) * 512], func=AF.Exp),
                             reads=[pb_DX], writes=[b_eX])

                def stage_c(q):
                    step, d, X, seg, tt0, B_ = ctxq(q)
                    pDX, pb_DX = PS[2]
                    fns = []
                    for hf in range(2):
                        fns.append(MM(pDX[:, hf * 512:(hf + 1) * 512], ident_bf[:], mask_bf[:, d, :].unsqueeze(1).to_broadcast([128, 4, 128]), True, False))
                        for part in range(2):
                            fns.append(MM(pDX[:, hf * 512:(hf + 1) * 512], nU_bf[:, d, :],
                                          ahl[:, X, part, d * 8 + hf * 4:d * 8 + hf * 4 + 4].unsqueeze(2).to_broadcast([128, 4, 128]), False, False))
                    for h in range(8):
                        for part in range(2):
                            fns.append(MM(pDX[:, h * 128:(h + 1) * 128], ahl[:, X, part, d * 8 + h:d * 8 + h + 1].to_broadcast([128, 128]),
                                          U_bf[:, d, :], False, part == 1))
                    S.group("pe", fns, reads=[b_ahl, b_U, b_nU, b_mask, b_identb], writes=[pb_DX])
                    Lt, b_L = B_["L"]
                    for hf in range(2):
                        S.op("act", lambda: nc.scalar.activation(out=Lt[:, hf * 512:(hf + 1) * 512], in_=pDX[:, hf * 512:(hf + 1) * 512], func=AF.Exp),
                             reads=[pb_DX], writes=[b_L])
                    bct, b_bct = B_["bct"]
                    xs, b_xs = B_["xs"]
                    eX, b_eX = B_["eX"]
                    psc, pb_sc = ph(7)
                    S.group("pe", [MM(psc(g * 128, (g + 1) * 128), bct[:, g, :], bct[:, 2 + g, :], True, True) for g in range(2)],
                            reads=[b_bct], writes=[pb_sc])
                    Mt, b_M = B_["M"]
                    S.op("dve", lambda: nc.vector.tensor_tensor(
                        out=Mt[:].rearrange("p (g h l) -> p g h l", g=2, h=4),
                        in0=Lt[:].rearrange("p (g h l) -> p g h l", g=2, h=4),
                        in1=psc(0, 256).rearrange("p (g l) -> p g l", g=2).unsqueeze(2).to_broadcast([128, 2, 4, 128]), op=ALU.mult),
                        reads=[b_L, pb_sc], writes=[b_M])
                    xdt, b_xdt = B_["xdt"]
                    S.op("dve", lambda: nc.vector.tensor_tensor(
                        out=xdt[:].rearrange("p (h q) -> p h q", h=8), in0=xs[:].rearrange("p (h q) -> p h q", h=8),
                        in1=dta[:, X, d * 8:d * 8 + 8].unsqueeze(2).to_broadcast([128, 8, 64]), op=ALU.mult),
                        reads=[b_xs, b_dta], writes=[b_xdt])
                    ll = 127 if d == 0 else 0
                    xdtw, b_xdtw = B_["xdtw"]
                    S.op("dve", lambda: nc.vector.tensor_tensor(
                        out=xdtw[:].rearrange("p (h q) -> p h q", h=8), in0=xdt[:].rearrange("p (h q) -> p h q", h=8),
                        in1=Lt[:].rearrange("p (h l) -> p h l", h=8)[:, :, ll:ll + 1].to_broadcast([128, 8, 64]), op=ALU.mult),
                        reads=[b_xdt, b_L], writes=[b_xdtw])
                    Cw, b_Cw = B_["Cw"]
                    S.op("dve", lambda: nc.vector.tensor_tensor(
                        out=Cw[:].rearrange("p (g h l) -> p g h l", g=2, h=4),
                        in0=eX[:].rearrange("p (g h l) -> p g h l", g=2, h=4),
                        in1=bct[:, 2:4, :].unsqueeze(2).to_broadcast([128, 2, 4, 128]), op=ALU.mult),
                        reads=[b_eX, b_bct], writes=[b_Cw])

                def stage_d(q):
                    step, d, X, seg, tt0, B_ = ctxq(q)
                    ll = 127 if d == 0 else 0
                    xs, b_xs = B_["xs"]
                    xdt, b_xdt = B_["xdt"]
                    xdtw, b_xdtw = B_["xdtw"]
                    btok, b_btok = B_["btok"]
                    eX, b_eX = B_["eX"]
                    Mt, b_M = B_["M"]
                    Cw, b_Cw = B_["Cw"]
                    S32, b_S32 = B_["S32"]
                    Sbf, b_Sbf = B_["Sbf"]
                    py, pb_y = ph(3)
                    pyt = PH[3][0]
                    fns = []
                    for h in range(8):
                        pr, hf = h // 2, h % 2
                        o = pyt[hf * 64:(hf + 1) * 64, 512 + pr * 128:512 + (pr + 1) * 128]
                        fns.append(MM(o, xdt[:, h * 64:(h + 1) * 64], Mt[:, h * 128:(h + 1) * 128], True, False))
                        fns.append(MM(o, xs[:, h * 64:(h + 1) * 64], ddiag[:, d * 8 + h, :], False, False))
                        fns.append(MM(o, Sbf[:, h * 64:(h + 1) * 64], Cw[:, h * 128:(h + 1) * 128], False, True))
                    S.group("pe", fns, reads=[b_xdt, b_M, b_xs, b_ddiag, b_Sbf, b_Cw], writes=[pb_y])
                    first = visit[d][X] < visit[1 - d][X] or (visit[d][X] == visit[1 - d][X] and d == 0)
                    yv = ybuf[:, :, tt0:tt0 + 128]
                    if first:
                        S.op("act", lambda: nc.scalar.copy(out=yv, in_=py(0, 512).rearrange("p (a b) -> p a b", a=4)), reads=[pb_y], writes=[b_y])
                    else:
                        S.op("dve", lambda: nc.vector.tensor_tensor(out=yv, in0=yv, in1=py(0, 512).rearrange("p (a b) -> p a b", a=4), op=ALU.add),
                             reads=[pb_y, b_y], writes=[b_y])
                    pst, pb_st = ph(6)
                    S.group("pe", [MM(pst(g * 256, (g + 1) * 256), btok[:, g * 128:(g + 1) * 128], xdtw[:, g * 256:(g + 1) * 256], True, True)
                                   for g in range(2)], reads=[b_btok, b_xdtw], writes=[pb_st])
                    S.op("dve", lambda: nc.vector.tensor_tensor(
                        out=S32[:].rearrange("p (h q) -> p h q", h=8), in0=S32[:].rearrange("p (h q) -> p h q", h=8),
                        in1=eX[:].rearrange("p (h l) -> p h l", h=8)[:, :, ll:ll + 1].to_broadcast([128, 8, 64]), op=ALU.mult),
                        reads=[b_S32, b_eX], writes=[b_S32])
                    S.op("dve", lambda: nc.vector.tensor_tensor(out=S32[:], in0=S32[:], in1=pst(0, 512), op=ALU.add), reads=[b_S32, pb_st], writes=[b_S32])
                    S.op("act", lambda: nc.scalar.copy(out=Sbf[:], in_=S32[:]), reads=[b_S32], writes=[b_Sbf])

                NQ = len(qs)
                stage_ab(0)
                stage_c(0)
                for q in range(NQ):
                    if q + 1 < NQ:
                        stage_ab(q + 1)
                    stage_d(q)
                    if q + 1 < NQ:
                        stage_c(q + 1)
                S.barrier()
            if stop_after == "B2":
                break
            with ExitStack() as sc:
                zb = [sb([128, 4, 512], BF16, sc, "zb%d" % i) for i in range(2)]
                sz, b_sz = sb([128, 4, 512], F32, sc, "sz")
                vv, b_vv = sb([128, 4, 512], F32, sc, "vv")
                vsq, b_vsq = sb([128, 4, 512], BF16, sc, "vsq")
                rs, b_rs = sb([128, 512], F32, sc, "rs")
                cst = [sb([128, 4, 512], BF16, sc, "cstB%d" % i) for i in range(2)]
                for bi, (t0, nt, seg) in enumerate(blocks):
                    zt, b_z = zb[bi % 2]
                    S.dma("sp", zt[:, :, 0:nt], projT_d[4:8, :, t0:t0 + nt].rearrange("c p t -> p c t"), writes=[b_z])
                    S.op("act", lambda: nc.scalar.activation(out=sz[:, :, 0:nt], in_=zt[:, :, 0:nt], func=AF.Silu), reads=[b_z], writes=[b_sz])
                    S.op("dve", lambda: nc.vector.tensor_tensor(out=ybuf[:, :, t0:t0 + nt], in0=ybuf[:, :, t0:t0 + nt], in1=sz[:, :, 0:nt], op=ALU.mult),
                         reads=[b_y, b_sz], writes=[b_y])
                for bi, (t0, nt, seg) in enumerate(blocks):
                    cstt, b_cst2 = cst[bi % 2]
                    S.op("act", lambda: nc.scalar.activation(out=vsq[:, :, 0:nt], in_=ybuf[:, :, t0:t0 + nt], func=AF.Square), reads=[b_y], writes=[b_vsq])
                    pm, pbm = ph(bi % 2)
                    S.group("pe", [MM(pm(0, nt), onesm512[:], vsq[:, c, 0:nt], c == 0, c == 3) for c in range(4)], reads=[b_o512, b_vsq], writes=[pbm])
                    S.op("act", lambda: nc.scalar.activation(out=rs[:, 0:nt], in_=pm(0, nt), func=AF.Sqrt, bias=epsc[:, 2:3], scale=1.0),
                         reads=[pbm, b_eps], writes=[b_rs])
                    S.op("dve", lambda: nc.vector.reciprocal(out=rs[:, 0:nt], in_=rs[:, 0:nt]), reads=[b_rs], writes=[b_rs])
                    for c in range(4):
                        S.op("dve", lambda: nc.vector.scalar_tensor_tensor(out=cstt[:, c, 0:nt], in0=ybuf[:, c, t0:t0 + nt], scalar=colT[:, c, R_NG:R_NG + 1],
                                                                           in1=rs[:, 0:nt], op0=ALU.mult, op1=ALU.mult),
                             reads=[b_y, b_colT, b_rs], writes=[b_cst2])
                    S.dma("pool", catT_d[2:6, :, t0:t0 + nt].rearrange("c p t -> p c t"), cstt[:, :, 0:nt], reads=[b_cst2], writes=[])
                S.barrier()
        if stop_after == "B":
            break

        def epilogue_block(po, pbo, xtt, bxt, ntile, gate, b_gate, lng, lnb, b_ln, tmps, scbs, after_gm=None):
            for i in range(ntile):
                tmp, b_tmp = tmps[i]
                for hf in range(2):
                    S.op("dve", lambda: nc.vector.tensor_tensor(out=tmp[:, hf * 512:(hf + 1) * 512], in0=po[i][hf](0, 512),
                                                                 in1=gate[:, hf * 512:(hf + 1) * 512], op=ALU.mult),
                         reads=[pbo[i][hf], b_gate], writes=[b_tmp])
            if after_gm is not None:
                after_gm()
            for i in range(ntile):
                tmp, b_tmp = tmps[i]
                S.op("dve", lambda: nc.vector.scalar_tensor_tensor(out=tmp[:], in0=xtt[:, i, :], scalar=ALPHA, in1=tmp[:], op0=ALU.mult, op1=ALU.add),
                     reads=[bxt[i], b_tmp], writes=[b_tmp])
            for i in range(ntile):
                tmp, b_tmp = tmps[i]
                st6, b_st6, mv, b_mv, rsd, b_rsd = scbs[i]
                for hf in range(2):
                    S.op("dve", lambda: nc.vector.bn_stats(out=st6[:, hf, :], in_=tmp[:, hf * 512:(hf + 1) * 512]), reads=[b_tmp], writes=[b_st6])
            for i in range(ntile):
                st6, b_st6, mv, b_mv, rsd, b_rsd = scbs[i]
                S.op("dve", lambda: nc.vector.bn_aggr(out=mv[:], in_=st6[:]), reads=[b_st6], writes=[b_mv])
            for i in range(ntile):
                st6, b_st6, mv, b_mv, rsd, b_rsd = scbs[i]
                S.op("act", lambda: nc.scalar.activation(out=rsd[:, 0:1], in_=mv[:, 1:2], func=AF.Sqrt, bias=epsc[:, 0:1], scale=1.0),
                     reads=[b_mv, b_eps], writes=[b_rsd])
            for i in range(ntile):
                st6, b_st6, mv, b_mv, rsd, b_rsd = scbs[i]
                S.op("dve", lambda: nc.vector.reciprocal(out=rsd[:, 0:1], in_=rsd[:, 0:1]), reads=[b_rsd], writes=[b_rsd])
            for i in range(ntile):
                st6, b_st6, mv, b_mv, rsd, b_rsd = scbs[i]
                S.op("dve", lambda: nc.vector.scalar_tensor_tensor(out=rsd[:, 1:2], in0=mv[:, 0:1], scalar=-1.0, in1=rsd[:, 0:1], op0=ALU.mult, op1=ALU.mult),
                     reads=[b_mv, b_rsd], writes=[b_rsd])
            for i in range(ntile):
                tmp, b_tmp = tmps[i]
                st6, b_st6, mv, b_mv, rsd, b_rsd = scbs[i]
                S.op("act", lambda: nc.scalar.activation(out=tmp[:], in_=tmp[:], func=AF.Identity, scale=rsd[:, 0:1], bias=rsd[:, 1:2]),
                     reads=[b_tmp, b_rsd], writes=[b_tmp])
            for i in range(ntile):
                tmp, b_tmp = tmps[i]
                S.op("pool", lambda: nc.gpsimd.tensor_tensor(out=tmp[:], in0=tmp[:], in1=lng[:], op=ALU.mult), reads=[b_tmp, b_ln], writes=[b_tmp])
            for i in range(ntile):
                tmp, b_tmp = tmps[i]
                S.op("pool", lambda: nc.gpsimd.tensor_tensor(out=xtt[:, i, :], in0=tmp[:], in1=lnb[:], op=ALU.add), reads=[b_tmp, b_ln], writes=[bxt[i]])

        def mk_scbs(sc, n):
            r = []
            for i in range(n):
                a, ba = sb([128, 2, 6], F32, sc, "st6")
                b2, bb = sb([128, 2], F32, sc, "mv")
                c2, bc = sb([128, 2], F32, sc, "rsd")
                r.append((a, ba, b2, bb, c2, bc))
            return r

        with ExitStack() as sch:
            h2T, b_h2T = sb([128, 8, T + 128], BF16, sch, "h2T")
            with ExitStack() as sc:
                wout, b_wout = sb([128, 8, D], BF16, sc, "wout")
                cat = [sb([128, 8, 512], BF16, sc, "cat%d" % i) for i in range(1)]
                xt = [sb([128, 4, D], F32, sc, "xtC%d" % i)[0] for i in range(2)]
                bxts = [[Buf("xtC") for _ in range(4)] for _ in range(2)]
                tmps = [sb([128, D], F32, sc, "tmpC%d" % i) for i in range(4)]
                gate, b_gate = sb([128, D], F32, sc, "gateC")
                lng, b_ln = sb([128, D], F32, sc, "lngC")
                lnb, _ = sb([128, D], F32, sc, "lnbC")
                scbs = mk_scbs(sc, 4)
                S.op("dve", lambda: nc.vector.memset(h2T[:, :, T:T + 128], 0.0), writes=[b_h2T])
                S.dma("sp", lng[:], ln1_g[l:l + 1, :].partition_broadcast(128), writes=[b_ln])
                S.dma("sp", lnb[:], ln1_b[l:l + 1, :].partition_broadcast(128), writes=[b_ln])
                for k in range(8):
                    load_cvt(wout[:, k, :], b_wout, w_out[l, k * 128:(k + 1) * 128, :], [D])

                def transposes(pblk):
                    pbi, (pt0, pnt, pseg) = pblk
                    pxt, pbx = xt[pbi % 2], bxts[pbi % 2]
                    pn = pnt // 128
                    for c in range(8):
                        pf, pb_ = ph(c % 2)
                        S.group("pe", [TR(pf(i * 128, (i + 1) * 128), pxt[:, i, c * 128:(c + 1) * 128], ident_f[:]) for i in range(pn)],
                                reads=pbx[0:pn] + [b_identf], writes=[pb_])
                        S.op("act", lambda: nc.scalar.activation(out=h2T[:, c, pt0:pt0 + pnt], in_=pf(0, pnt), func=AF.Identity,
                                                                 scale=modT[:, 3, c, pseg:pseg + 1], bias=modT[:, 2, c, pseg:pseg + 1]),
                             reads=[pb_, b_modT], writes=[b_h2T])

                prev = None
                cur_seg = None
                for bi, (t0, nt, seg) in enumerate(blocks):
                    if last and seg == 0:
                        continue
                    ntile = nt // 128
                    if seg != cur_seg:
                        S.dma("sp", gate[:], gate_d[seg:seg + 1, :].partition_broadcast(128), writes=[b_gate])
                        cur_seg = seg
                    catt, b_cat = cat[0]
                    xtt, bxt = xt[bi % 2], bxts[bi % 2]
                    S.dma("sp", catt[:, :, 0:nt], catT_d[:, :, t0:t0 + nt].rearrange("c p t -> p c t"), writes=[b_cat])
                    S.dma("sp", xtt[:, 0:ntile, :], xsrc[t0:t0 + nt, :].rearrange("(i p) d -> p i d", p=128), writes=bxt[0:ntile])
                    po, pbo = [], []
                    for i in range(ntile):
                        pi, pbi_ = [], []
                        for hf in range(2):
                            pf, pb_ = ph(i * 2 + hf)
                            S.group("pe", [MM(pf(0, 512), catt[:, k, i * 128:(i + 1) * 128], wout[:, k, hf * 512:(hf + 1) * 512], k == 0, k == 7)
                                           for k in range(8)], reads=[b_cat, b_wout], writes=[pb_])
                            pi.append(pf)
                            pbi_.append(pb_)
                        po.append(pi)
                        pbo.append(pbi_)
                    pv_ = prev
                    epilogue_block(po, pbo, xtt, bxt, ntile, gate, b_gate, lng, lnb, b_ln, tmps, scbs,
                                   after_gm=(lambda: transposes(pv_)) if pv_ is not None else None)
                    S.dma("pool", xs_d[t0:t0 + nt, :].rearrange("(i p) d -> p i d", p=128), xtt[:, 0:ntile, :], reads=bxt[0:ntile], writes=[])
                    prev = (bi, (t0, nt, seg))
                transposes(prev)
                S.barrier()
            if stop_after == "C":
                break
            with ExitStack() as sc:
                wj = [sb([128, 8, 256], BF16, sc, "wj%d" % i) for i in range(3)]
                fd = [sb([128, 9, 128], BF16, sc, "fd%d" % i) for i in range(3)]
                gb = [[sb([128, 642], BF16, sc, "g%d_%d" % (i, q)) for q in range(3)] for i in range(2)]
                ge = [sb([128, 512], F32, sc, "ge%d" % i) for i in range(2)]
                ast = [sb([128, 512], BF16, sc, "ast%d" % i) for i in range(3)]
                for i in range(2):
                    for q in range(3):
                        S.op("dve", lambda: nc.vector.memset(gb[i][q][0][:], 0.0), writes=[gb[i][q][1]])

                def prep(j):
                    wjt, b_wj = wj[j % 3]
                    fdt, b_fd = fd[j % 3]
                    st_, bst = wst[wsti[0] % 2]
                    wsti[0] += 1
                    sv = st_[:, 0:2048].rearrange("p (k n) -> p k n", k=8)
                    S.dma("sp", sv[:, :, 0:128], w_up[l, :, j * 128:(j + 1) * 128].rearrange("(k p) n -> p k n", p=128), writes=[bst])
                    S.dma("sp", sv[:, :, 128:256], w_up[l, :, DFF + j * 128:DFF + (j + 1) * 128].rearrange("(k p) n -> p k n", p=128), writes=[bst])
                    S.op("pool", lambda: nc.gpsimd.tensor_copy(out=wjt[:], in_=sv), reads=[bst], writes=[b_wj])
                    for q in range(9):
                        S.op("pool", lambda: nc.gpsimd.tensor_scalar(out=fdt[:, q, :], in0=ident_f[:], scalar1=colT[:, j, R_FW + q:R_FW + q + 1], scalar2=None,
                                                                      op0=ALU.mult), reads=[b_identf, b_colT], writes=[b_fd])

                items = []
                for j in range(NJ):
                    first_of_j = True
                    for bi, (t0, nt, seg) in enumerate(blocks):
                        if last and seg == 0:
                            continue
                        items.append((j, t0, nt, seg, first_of_j))
                        first_of_j = False

                def stage1(n):
                    j, t0, nt, seg, first_of_j = items[n]
                    if first_of_j and j + 1 < NJ:
                        prep(j + 1)
                    wjt, b_wj = wj[j % 3]
                    (g0, b_g0), (gL, b_gL), (gR, b_gR) = gb[n % 2]
                    s0, s1 = seg_rng[seg]
                    pgt, pbg = PS[n % 2]
                    if seg == 0:
                        S.group("pe", [MM(pgt[:, 0:nt], wjt[:, k, 128:256], h2T[:, k, t0:t0 + nt], k == 0, k == 7) for k in range(8)],
                                reads=[b_wj, b_h2T], writes=[pbg])
                        S.op("act", lambda: nc.scalar.copy(out=g0[:, 1:1 + nt], in_=pgt[:, 0:nt]), reads=[pbg], writes=[b_g0])
                        S.op("dve", lambda: nc.vector.memset(g0[:, 1 + nt:2 + nt], 0.0), writes=[b_g0])
                    else:
                        base = t0 - 64
                        S.group("pe", [MM(pgt[:, 0:512], wjt[:, k, 128:256], h2T[:, k, base:base + 512], k == 0, k == 7) for k in range(8)] +
                                [MM(pgt[:, 512:640], wjt[:, k, 128:256], h2T[:, k, base + 512:base + 640], k == 0, k == 7) for k in range(8)],
                                reads=[b_wj, b_h2T], writes=[pbg])
                        S.op("act", lambda: nc.scalar.copy(out=g0[:, 1:513], in_=pgt[:, 0:512]), reads=[pbg], writes=[b_g0])
                        S.op("act", lambda: nc.scalar.copy(out=g0[:, 513:641], in_=pgt[:, 512:640]), reads=[pbg], writes=[b_g0])
                        if t0 == s0:
                            S.op("dve", lambda: nc.vector.memset(g0[:, 1:65], 0.0), writes=[b_g0])
                        if t0 + nt == s1:
                            S.op("dve", lambda: nc.vector.memset(g0[:, 577:641], 0.0), writes=[b_g0])
                        S.op("dve", lambda: nc.vector.tensor_copy(out=gL[:], in_=g0[:]), reads=[b_g0], writes=[b_gL])
                        S.op("dve", lambda: nc.vector.memset(gL[:, 64:641:64], 0.0), writes=[b_gL])
                        S.op("dve", lambda: nc.vector.tensor_copy(out=gR[:], in_=g0[:]), reads=[b_g0], writes=[b_gR])
                        S.op("dve", lambda: nc.vector.memset(gR[:, 1:641:64], 0.0), writes=[b_gR])

                def stage2(n):
                    j, t0, nt, seg, first_of_j = items[n]
                    wjt, b_wj = wj[j % 3]
                    fdt, b_fd = fd[j % 3]
                    (g0, b_g0), (gL, b_gL), (gR, b_gR) = gb[n % 2]
                    get, b_ge = ge[n % 2]
                    astt, b_ast = ast[n % 3]
                    pv, pbv = ph(6 + n % 2)
                    S.group("pe", [MM(pv(0, nt), wjt[:, k, 0:128], h2T[:, k, t0:t0 + nt], k == 0, k == 7) for k in range(8)],
                            reads=[b_wj, b_h2T], writes=[pbv])
                    pd, pbd_ = ph(4 + n % 2)
                    if seg == 0:
                        S.group("pe", [MM(pd(0, nt), fdt[:, 3 + kx, :], g0[:, kx:kx + nt], kx == 0, kx == 2) for kx in range(3)],
                                reads=[b_fd, b_g0], writes=[pbd_])
                    else:
                        fns = []
                        for ky in range(3):
                            for kx in range(3):
                                src = (gL, g0, gR)[kx]
                                off = 65 + 64 * (ky - 1) + (kx - 1)
                                fns.append(MM(pd(0, nt), fdt[:, ky * 3 + kx, :], src[:, off:off + nt], ky == 0 and kx == 0, ky == 2 and kx == 2))
                        S.group("pe", fns, reads=[b_fd, b_g0, b_gL, b_gR], writes=[pbd_])
                    S.op("act", lambda: nc.scalar.activation(out=get[:, 0:nt], in_=pd(0, nt), func=AF.Gelu_apprx_tanh,
                                                             bias=colT[:, j, R_FB:R_FB + 1], scale=1.0), reads=[pbd_, b_colT], writes=[b_ge])
                    S.op("dve", lambda: nc.vector.tensor_tensor(out=astt[:, 0:nt], in0=pv(0, nt), in1=get[:, 0:nt], op=ALU.mult),
                         reads=[pbv, b_ge], writes=[b_ast])
                    S.dma("pool", actT_d[j, :, t0:t0 + nt], astt[:, 0:nt], reads=[b_ast], writes=[])

                prep(0)
                stage1(0)
                for n in range(len(items)):
                    if n + 1 < len(items):
                        stage1(n + 1)
                    stage2(n)
                S.barrier()
        if stop_after == "D1":
            break

        with ExitStack() as sc:
            wdn, b_wdn = sb([128, NJ, D], BF16, sc, "wdn")
            actb = [sb([128, NJ, 512], BF16, sc, "actb%d" % i) for i in range(2)]
            xt = [sb([128, 4, D], F32, sc, "xtD%d" % i)[0] for i in range(2)]
            bxts = [[Buf("xtD") for _ in range(4)] for _ in range(2)]
            tmps = [sb([128, D], F32, sc, "tmpD%d" % i) for i in range(4)]
            gate, b_gate = sb([128, D], F32, sc, "gateD")
            lng, b_ln = sb([128, D], F32, sc, "lngD")
            lnb, _ = sb([128, D], F32, sc, "lnbD")
            scbs = mk_scbs(sc, 4)
            S.dma("sp", lng[:], ln2_g[l:l + 1, :].partition_broadcast(128), writes=[b_ln])
            S.dma("sp", lnb[:], ln2_b[l:l + 1, :].partition_broadcast(128), writes=[b_ln])
            for jp in range(0, NJ, 2):
                st_, bst_ = wst[(jp // 2) % 2]
                sv_ = st_[:, 0:2048].rearrange("p (a b) -> p a b", a=2)
                S.dma("sp" if (jp // 2) % 2 == 0 else "pool", sv_, w_down[l, jp * 128:(jp + 2) * 128, :].rearrange("(a p) n -> p a n", p=128), writes=[bst_])
                if (jp // 2) % 2 == 0:
                    S.op("pool", lambda: nc.gpsimd.tensor_copy(out=wdn[:, jp:jp + 2, :], in_=sv_), reads=[bst_], writes=[b_wdn])
                else:
                    S.op("dve", lambda: nc.vector.tensor_copy(out=wdn[:, jp:jp + 2, :], in_=sv_), reads=[bst_], writes=[b_wdn])
            cur_seg = None
            for bi, (t0, nt, seg) in enumerate(blocks):
                if last and seg == 0:
                    continue
                ntile = nt // 128
                if seg != cur_seg:
                    S.dma("sp", gate[:], gate_d[2 + seg:3 + seg, :].partition_broadcast(128), writes=[b_gate])
                    cur_seg = seg
                at, b_at = actb[bi % 2]
                xtt, bxt = xt[bi % 2], bxts[bi % 2]
                S.dma("sp", at[:, :, 0:nt], actT_d[:, :, t0:t0 + nt].rearrange("c p t -> p c t"), writes=[b_at])
                S.dma("sp", xtt[:, 0:ntile, :], xs_d[t0:t0 + nt, :].rearrange("(i p) d -> p i d", p=128), writes=bxt[0:ntile])
                po, pbo = [], []
                for i in range(ntile):
                    pi, pbi_ = [], []
                    for hf in range(2):
                        pf, pb_ = ph(i * 2 + hf)
                        S.group("pe", [MM(pf(0, 512), at[:, j, i * 128:(i + 1) * 128], wdn[:, j, hf * 512:(hf + 1) * 512], j == 0, j == NJ - 1)
                                       for j in range(NJ)], reads=[b_at, b_wdn], writes=[pb_])
                        pi.append(pf)
                        pbi_.append(pb_)
                    po.append(pi)
                    pbo.append(pbi_)
                epilogue_block(po, pbo, xtt, bxt, ntile, gate, b_gate, lng, lnb, b_ln, tmps, scbs)
                if last:
                    S.dma("pool", out_d[t0 - TC:t0 - TC + nt, :].rearrange("(i p) d -> p i d", p=128), xtt[:, 0:ntile, :], reads=bxt[0:ntile], writes=[])
                else:
                    S.dma("pool", xs_d[t0:t0 + nt, :].rearrange("(i p) d -> p i d", p=128), xtt[:, 0:ntile, :], reads=bxt[0:ntile], writes=[])
            S.barrier()

    S.barrier()
    es.close()
    return nc


_CACHE = {}


def kernel(**inputs):
    x = np.asarray(inputs["x"], np.float32)
    c = np.asarray(inputs["c"], np.float32)
    ctx = np.asarray(inputs["ctx"], np.float32)
    c_ctx = np.asarray(inputs["c_ctx"], np.float32)
    B = x.shape[0]
    if "nc" not in _CACHE:
        _CACHE["nc"] = build_program()
    nc = _CACHE["nc"]
    consts = make_consts()
    shared = {"consts": consts}
    for k in ("w_mod", "b_mod", "w_in", "conv_dw_w", "conv_dw_b", "conv_ln_g", "conv_ln_b", "conv_pw_w", "conv_pw_b", "ssm_conv_w",
              "ssm_conv_b", "ssm_norm_g", "pool_w", "pool_scale", "w_out", "ln1_g", "ln1_b", "w_up", "ffn_dw_b", "w_down", "ln2_g", "ln2_b"):
        shared[k] = np.ascontiguousarray(np.asarray(inputs[k], np.float32))
    shared["ffn_dw_w"] = np.ascontiguousarray(np.asarray(inputs["ffn_dw_w"], np.float32).reshape(DEPTH, 9, DFF))
    for k in ("ssm_dt_bias", "ssm_A_log", "ssm_D"):
        shared[k] = np.ascontiguousarray(np.asarray(inputs[k], np.float32).reshape(DEPTH, 16))
    in_maps = []
    for b in range(B):
        m = dict(shared)
        m["xs"] = np.ascontiguousarray(np.concatenate([ctx[b], x[b]], axis=0))
        m["cvec"] = np.ascontiguousarray(np.stack([c_ctx, c[b]], axis=0))
        in_maps.append(m)
    res = run_bass_kernel_spmd(nc, in_maps, core_ids=list(range(B)))
    return np.stack([np.asarray(r["out"], np.float32) for r in res.results], axis=0)
```
